# Optimizing a Trainium2 kernel written in Bass

```python
import math
import jax
import jax.numpy as jnp
from jax import lax
import numpy as np

D_MODEL = 1024
BATCH = 16
SEQ = 4096
DEPTH = 4

N_MIXERS = 2
N_HYENA = (DEPTH + 1) // 2
N_RWKV = DEPTH // 2
RMS_EPS = 1e-6

HY_ORDER = 2
HY_N_PROJ = HY_ORDER + 1
HY_SHORT_CONV = 3
HY_EMB = 33
HY_BANDS = (HY_EMB - 1) // 2
HY_FILTER_WIDTH = 64
HY_DECAY_TARGET = 1e-2
HY_FAST_DECAY_PCT = 0.3
HY_SLOW_DECAY_PCT = 1.5

RW_HEAD = 64
RW_HEADS = D_MODEL // RW_HEAD
RW_DECAY_LORA = 64
RW_ICLR_LORA = 64
RW_VRES_LORA = 32
RW_GATE_LORA = 128
RW_GN_EPS = RW_HEAD * 1e-5
RW_N_LERP = 6

D_FF = ((8 * D_MODEL // 3 + 255) // 256) * 256
FFN_CONV = 3

kernel_name = 'hybrid_hyena_rwkv7_convffn_encoder'


def _rmsnorm(x, g):
    x32 = x.astype(jnp.float32)
    y = x32 * lax.rsqrt(jnp.mean(x32 * x32, axis=-1, keepdims=True) + RMS_EPS)
    return (y * g.astype(jnp.float32)).astype(x.dtype)


def _dwconv_centred(x, w, b):
    K, C = w.shape
    y = lax.conv_general_dilated(
        x, w[:, None, :].astype(x.dtype), window_strides=(1,),
        padding=((K // 2, K // 2),), dimension_numbers=('NWC', 'WIO', 'NWC'),
        feature_group_count=C)
    return y + b.astype(x.dtype)


def _hyena_filter(L, w1, b1, w2, b2, w3, b3, w4, freq):
    f32 = jnp.float32
    n = jnp.arange(L, dtype=f32)
    t = (n / max(L - 1, 1))[:, None]
    fr = jnp.linspace(1e-4, HY_BANDS - 1, HY_BANDS, dtype=f32)
    ang = (2.0 * math.pi / L) * n[:, None] * fr[None, :]
    z = jnp.concatenate([t, jnp.cos(ang), -jnp.sin(ang)], axis=-1)
    fq = freq.astype(f32)
    hdn = jnp.sin(fq * (z @ w1.astype(f32) + b1.astype(f32)))
    hdn = jnp.sin(fq * (hdn @ w2.astype(f32) + b2.astype(f32)))
    hdn = jnp.sin(fq * (hdn @ w3.astype(f32) + b3.astype(f32)))
    k = hdn @ w4.astype(f32)
    D = w4.shape[-1] // 2
    deltas = jnp.abs(jnp.linspace(math.log(HY_DECAY_TARGET) / HY_SLOW_DECAY_PCT,
                                  math.log(HY_DECAY_TARGET) / HY_FAST_DECAY_PCT, D, dtype=f32))
    k = k * jnp.exp(-t * jnp.tile(deltas, 2)[None, :])
    return k[:, :D], k[:, D:]


def _bidir_fftconv(u, k_fwd, k_bwd):
    L = u.shape[1]
    n = 2 * L
    D = k_fwd.shape[1]
    k_c = jnp.concatenate([k_fwd.at[0].add(k_bwd[0]),
                           jnp.zeros((1, D), jnp.float32),
                           k_bwd[:0:-1]], axis=0)
    kf = jnp.fft.rfft(k_c, n=n, axis=0)
    uf = jnp.fft.rfft(u, n=n, axis=1)
    return jnp.fft.irfft(uf * kf[None], n=n, axis=1)[:, :L]


def _hyena_mixer(h, w_in, b_in, conv_w, conv_b, f_w1, f_b1, f_w2, f_b2, f_w3, f_b3,
                 f_w4, f_freq, skip, w_out, b_out):
    B, L, D = h.shape
    z = _dwconv_centred(h @ w_in + b_in, conv_w, conv_b)
    x0, x1, v = jnp.split(z, HY_N_PROJ, axis=-1)
    k_fwd, k_bwd = _hyena_filter(L, f_w1, f_b1, f_w2, f_b2, f_w3, f_b3, f_w4, f_freq)
    u = (v * x1).astype(jnp.float32)
    y = _bidir_fftconv(u, k_fwd, k_bwd) + u * skip.astype(jnp.float32)
    y = y.astype(h.dtype) * x0
    return y @ w_out + b_out


def _token_shift_delta(h):
    prev = jnp.pad(h[:, :-1], ((0, 0), (1, 0), (0, 0)))
    nxt = jnp.pad(h[:, 1:], ((0, 0), (0, 1), (0, 0)))
    return 0.5 * (prev + nxt) - h


def _decay(xw, w0, w1, w2):
    wl = w0.astype(jnp.float32) + (jnp.tanh(xw @ w1) @ w2).astype(jnp.float32)
    wl = -jax.nn.softplus(-wl) - 0.5
    return jnp.exp(-jnp.exp(wl))


def _wkv7_scan(r, w, k, v, kk, a):
    B, L, H, N = r.shape

    def step(state, inp):
        r_t, w_t, k_t, v_t, kk_t, a_t = inp
        sa = jnp.einsum('bhvk,bhk->bhv', state, kk_t)
        state = (state * w_t[:, :, None, :]
                 - sa[..., None] * (kk_t * a_t)[:, :, None, :]
                 + v_t[..., None] * k_t[:, :, None, :])
        return state, jnp.einsum('bhvk,bhk->bhv', state, r_t)

    xs = tuple(jnp.swapaxes(t, 0, 1) for t in (r, w, k, v, kk, a))
    s0 = jnp.zeros((B, H, N, N), jnp.float32)
    _, ys = lax.scan(step, s0, xs)
    return jnp.swapaxes(ys, 0, 1)


def _rwkv7_mixer(h, v_first, mu, w_r, w_k, w_v, w_o, w0, w1, w2, a0, a1, a2,
                 g1, g2, k_k, k_a, r_k, ln_w, ln_b, vres):
    B, L, D = h.shape
    H, N = RW_HEADS, RW_HEAD
    f32 = jnp.float32
    xx = _token_shift_delta(h)
    xr, xw, xk, xv, xa, xg = [h + xx * mu[j] for j in range(RW_N_LERP)]
    r = xr @ w_r
    k = xk @ w_k
    v = xv @ w_v
    a = jax.nn.sigmoid(a0 + (xa @ a1) @ a2)
    g = jax.nn.sigmoid(xg @ g1) @ g2
    if vres is None:
        v_first = v
    else:
        v0, v1, v2 = vres
        v = v + (v_first - v) * jax.nn.sigmoid(v0 + (xv @ v1) @ v2)

    def heads(t):
        return t.astype(f32).reshape(B, L, H, N)

    kk = heads(k * k_k)
    kk = kk / jnp.maximum(jnp.linalg.norm(kk, axis=-1, keepdims=True), 1e-12)
    k = k * (1.0 + (a - 1.0) * k_a)
    rh, kh, vh, ah = heads(r), heads(k), heads(v), heads(a)
    w_f = heads(_decay(xw, w0[0], w1[0], w2[0]))
    w_b = heads(_decay(xw, w0[1], w1[1], w2[1]))

    def flip(t):
        return jnp.flip(t, axis=1)

    y = _wkv7_scan(rh, w_f, kh, vh, kk, ah)
    y = y + flip(_wkv7_scan(flip(rh), flip(w_b), flip(kh), flip(vh), flip(kk), flip(ah)))
    mean = jnp.mean(y, axis=-1, keepdims=True)
    var = jnp.mean(jnp.square(y - mean), axis=-1, keepdims=True)
    y = ((y - mean) * lax.rsqrt(var + RW_GN_EPS)).reshape(B, L, D)
    y = y * ln_w.astype(f32) + ln_b.astype(f32)
    bonus = jnp.sum(rh * kh * r_k.astype(f32), axis=-1, keepdims=True) * vh
    y = y + bonus.reshape(B, L, D)
    return (y.astype(h.dtype) * g) @ w_o, v_first


def _conv_ffn(h, w_up, conv_w, conv_b, w_down):
    u = _dwconv_centred(h @ w_up, conv_w, conv_b)
    gate, val = jnp.split(u, 2, axis=-1)
    return (jax.nn.silu(gate) * val) @ w_down


def setup_inputs(seed: int = 0) -> dict:
    key = jax.random.key(seed)
    ks = iter(jax.random.split(key, 64))
    f32 = jnp.float32

    def nrm(shape, scale):
        return scale * jax.random.normal(next(ks), shape, f32)

    D, F, W = D_MODEL, D_FF, HY_FILTER_WIDTH
    NH, NR = N_HYENA, N_RWKV
    C3 = HY_N_PROJ * D
    ramp = (jnp.arange(D, dtype=f32) / (D - 1)) ** 0.85
    return {
        'x': nrm((BATCH, SEQ, D), 1.0),
        'norm_mix_g': 1.0 + nrm((DEPTH, D), 0.02),
        'norm_ffn_g': 1.0 + nrm((DEPTH, D), 0.02),
        'final_norm_g': 1.0 + nrm((D,), 0.02),
        'hy_w_in': nrm((NH, D, C3), D ** -0.5),
        'hy_b_in': nrm((NH, C3), 0.02),
        'hy_conv_w': nrm((NH, HY_SHORT_CONV, C3), HY_SHORT_CONV ** -0.5),
        'hy_conv_b': nrm((NH, C3), 0.02),
        'hy_f_w1': nrm((NH, HY_EMB, W), HY_EMB ** -0.5),
        'hy_f_b1': nrm((NH, W), 0.1),
        'hy_f_w2': nrm((NH, W, W), W ** -0.5),
        'hy_f_b2': nrm((NH, W), 0.1),
        'hy_f_w3': nrm((NH, W, W), W ** -0.5),
        'hy_f_b3': nrm((NH, W), 0.1),
        'hy_f_w4': nrm((NH, W, 2 * D), 0.01),
        'hy_f_freq': 1.0 + nrm((NH, W), 0.1),
        'hy_skip': nrm((NH, D), 1.0),
        'hy_w_out': nrm((NH, D, D), D ** -0.5),
        'hy_b_out': nrm((NH, D), 0.02),
        'rw_mu': jax.random.uniform(next(ks), (NR, RW_N_LERP, D), f32),
        'rw_w_r': nrm((NR, D, D), D ** -0.5),
        'rw_w_k': nrm((NR, D, D), D ** -0.5),
        'rw_w_v': nrm((NR, D, D), D ** -0.5),
        'rw_w_o': nrm((NR, D, D), D ** -0.5),
        'rw_w0': -6.5 + 5.0 * ramp + nrm((NR, 2, D), 0.1),
        'rw_w1': nrm((NR, 2, D, RW_DECAY_LORA), D ** -0.5),
        'rw_w2': nrm((NR, 2, RW_DECAY_LORA, D), 0.5 * RW_DECAY_LORA ** -0.5),
        'rw_a0': nrm((NR, D), 0.1),
        'rw_a1': nrm((NR, D, RW_ICLR_LORA), D ** -0.5),
        'rw_a2': nrm((NR, RW_ICLR_LORA, D), 0.5 * RW_ICLR_LORA ** -0.5),
        'rw_v0': 1.0 + nrm((NR - 1, D), 0.1),
        'rw_v1': nrm((NR - 1, D, RW_VRES_LORA), D ** -0.5),
        'rw_v2': nrm((NR - 1, RW_VRES_LORA, D), 0.5 * RW_VRES_LORA ** -0.5),
        'rw_g1': nrm((NR, D, RW_GATE_LORA), D ** -0.5),
        'rw_g2': nrm((NR, RW_GATE_LORA, D), RW_GATE_LORA ** -0.5),
        'rw_k_k': 0.85 + nrm((NR, D), 0.02),
        'rw_k_a': 1.0 + nrm((NR, D), 0.02),
        'rw_r_k': -0.04 + nrm((NR, RW_HEADS, RW_HEAD), 0.1),
        'rw_ln_w': 1.0 + nrm((NR, D), 0.02),
        'rw_ln_b': nrm((NR, D), 0.02),
        'ff_w_up': nrm((DEPTH, D, 2 * F), D ** -0.5),
        'ff_conv_w': nrm((DEPTH, FFN_CONV, 2 * F), FFN_CONV ** -0.5),
        'ff_conv_b': nrm((DEPTH, 2 * F), 0.02),
        'ff_w_down': nrm((DEPTH, F, D), F ** -0.5),
    }


def reference(x, norm_mix_g, norm_ffn_g, final_norm_g,
              hy_w_in, hy_b_in, hy_conv_w, hy_conv_b,
              hy_f_w1, hy_f_b1, hy_f_w2, hy_f_b2, hy_f_w3, hy_f_b3, hy_f_w4, hy_f_freq,
              hy_skip, hy_w_out, hy_b_out,
              rw_mu, rw_w_r, rw_w_k, rw_w_v, rw_w_o, rw_w0, rw_w1, rw_w2,
              rw_a0, rw_a1, rw_a2, rw_v0, rw_v1, rw_v2, rw_g1, rw_g2,
              rw_k_k, rw_k_a, rw_r_k, rw_ln_w, rw_ln_b,
              ff_w_up, ff_conv_w, ff_conv_b, ff_w_down):
    v_first = None
    for i in range(DEPTH):
        h = _rmsnorm(x, norm_mix_g[i])
        j = i // N_MIXERS
        if i % N_MIXERS == 0:
            y = _hyena_mixer(h, hy_w_in[j], hy_b_in[j], hy_conv_w[j], hy_conv_b[j],
                             hy_f_w1[j], hy_f_b1[j], hy_f_w2[j], hy_f_b2[j],
                             hy_f_w3[j], hy_f_b3[j], hy_f_w4[j], hy_f_freq[j],
                             hy_skip[j], hy_w_out[j], hy_b_out[j])
        else:
            vres = (rw_v0[j - 1], rw_v1[j - 1], rw_v2[j - 1]) if j > 0 else None
            y, v_first = _rwkv7_mixer(h, v_first, rw_mu[j], rw_w_r[j], rw_w_k[j], rw_w_v[j], rw_w_o[j],
                                      rw_w0[j], rw_w1[j], rw_w2[j], rw_a0[j], rw_a1[j], rw_a2[j],
                                      rw_g1[j], rw_g2[j], rw_k_k[j], rw_k_a[j], rw_r_k[j],
                                      rw_ln_w[j], rw_ln_b[j], vres)
        x = x + y
        h = _rmsnorm(x, norm_ffn_g[i])
        x = x + _conv_ffn(h, ff_w_up[i], ff_conv_w[i], ff_conv_b[i], ff_w_down[i])
    return _rmsnorm(x, final_norm_g)
```

```python
import contextlib
import os
import numpy as np
import ml_dtypes
import concourse.bass as bass
import concourse.mybir as mybir
from concourse.bass_utils import run_bass_kernel_spmd

F32 = mybir.dt.float32
BF16 = mybir.dt.bfloat16
I32 = mybir.dt.int32
ALU = mybir.AluOpType
AF = mybir.ActivationFunctionType
NPBF = ml_dtypes.bfloat16

D = 1024
KC = 8
FF = 2816
FC = 22
DEPTH = 4
PI = float(np.pi)

VEC_ORDER = ['norm_mix_g', 'norm_ffn_g', 'final_norm_g', 'hy_skip', 'hy_b_out', 'rw_mu', 'rw_w0', 'rw_a0',
             'rw_v0', 'rw_k_k', 'rw_k_a', 'rw_r_k', 'rw_ln_w', 'rw_ln_b', 'hy_b_in', 'hy_conv_w', 'hy_conv_b',
             'ff_conv_w', 'ff_conv_b']
VEC_LAST = {'norm_mix_g': 1024, 'norm_ffn_g': 1024, 'final_norm_g': 1024, 'hy_skip': 1024, 'hy_b_out': 1024,
            'rw_mu': 1024, 'rw_w0': 1024, 'rw_a0': 1024, 'rw_v0': 1024, 'rw_k_k': 1024, 'rw_k_a': 1024,
            'rw_r_k': 1024, 'rw_ln_w': 1024, 'rw_ln_b': 1024, 'hy_b_in': 3072, 'hy_conv_w': 3072,
            'hy_conv_b': 3072, 'ff_conv_w': 5632, 'ff_conv_b': 5632}
BIG_W = ['hy_w_in', 'hy_f_w1', 'hy_f_b1', 'hy_f_w2', 'hy_f_b2', 'hy_f_w3', 'hy_f_b3', 'hy_f_w4', 'hy_f_freq',
         'hy_skip', 'hy_w_out', 'rw_w_r', 'rw_w_k', 'rw_w_v', 'rw_w_o', 'rw_w1', 'rw_w2', 'rw_a1', 'rw_a2',
         'rw_v1', 'rw_v2', 'rw_g1', 'rw_g2', 'ff_w_up', 'ff_w_down']


def pack_vecs(inputs):
    rows, off, per = [], {}, {}
    r = 0
    for n in VEC_ORDER:
        a = np.asarray(inputs[n], np.float32).reshape(-1, 128)
        off[n] = r
        per[n] = VEC_LAST[n] // 128
        rows.append(a)
        r += a.shape[0]
    pv = np.concatenate(rows, 0)
    pad = (-pv.shape[0]) % 128
    if pad:
        pv = np.concatenate([pv, np.zeros((pad, 128), np.float32)], 0)
    return np.ascontiguousarray(pv), off, per


def vec_layout(shapes):
    off, per, r = {}, {}, 0
    for n in VEC_ORDER:
        cnt = int(np.prod(shapes[n])) // 128
        off[n] = r
        per[n] = VEC_LAST[n] // 128
        r += cnt
    r += (-r) % 128
    return off, per, r


def host_consts(L):
    N = 2 * L
    NT = L // 128
    c = {}
    c['ident'] = np.eye(128, dtype=np.float32)
    t = np.arange(L, dtype=np.int64)
    ang = 2.0 * np.pi * ((t[:, None] * t[None, :]) % N).astype(np.float64) / N
    C = np.cos(ang)
    S = np.sin(ang)

    def tile(M):
        return np.ascontiguousarray(M.reshape(NT, 128, NT, 128).transpose(2, 1, 0, 3).reshape(NT, 128, NT * 128).astype(NPBF))
    c['tabC'] = tile(C)
    c['tabS'] = tile(S)
    n = np.arange(L, dtype=np.float32)
    tt = (n / max(L - 1, 1)).astype(np.float32)
    fr = np.linspace(1e-4, 15, 16, dtype=np.float32)
    a2 = (np.float32(2.0 * np.pi / L) * n[:, None] * fr[None, :]).astype(np.float32)
    z = np.concatenate([tt[:, None], np.cos(a2), -np.sin(a2)], -1).astype(np.float32)
    c['zfeat'] = np.ascontiguousarray(z.T)
    dl = np.abs(np.linspace(np.log(1e-2) / 1.5, np.log(1e-2) / 0.3, D, dtype=np.float32))
    c['deltas2'] = np.tile(dl, 2)[None, :].astype(np.float32)
    c['negt'] = np.ascontiguousarray((-tt).reshape(NT, 128).T)
    p = np.arange(128)
    altv = np.where(p % 2 == 0, 1.0, -1.0)
    c['altcol'] = altv[:, None].astype(NPBF)
    c['altrow'] = altv[None, :].astype(NPBF)
    fs = np.full((128, NT), 2.0 / N, np.float32)
    fs[0, 0] = 1.0 / N
    c['fscale'] = fs
    i = (p % 64)[:, None]
    j = np.arange(64)[None, :]
    lo_s = (j < i).astype(np.float32)
    lo_i = (j <= i).astype(np.float32)
    up_s = (j > i).astype(np.float32)
    up_i = (j >= i).astype(np.float32)
    mA = np.stack([np.tile(-lo_s[:, None, :], (1, 4, 1)), np.tile(-up_s[:, None, :], (1, 4, 1))], 0)
    mT_f = np.concatenate([up_s, up_i], 1)
    mT_b = np.concatenate([lo_s, lo_i], 1)
    mB = np.stack([np.tile((mT_f * np.concatenate([-np.ones((1, 64)), np.ones((1, 64))], 1))[:, None, :], (1, 4, 1)),
                   np.tile((mT_b * np.concatenate([-np.ones((1, 64)), np.ones((1, 64))], 1))[:, None, :], (1, 4, 1))], 0)
    mC = np.stack([np.tile(mT_f[:, None, :], (1, 4, 1)), np.tile(mT_b[:, None, :], (1, 4, 1))], 0)
    c['mA'] = np.ascontiguousarray(mA.transpose(1, 0, 2, 3).reshape(128, 2 * 4 * 64)).astype(np.float32)
    c['mB'] = np.ascontiguousarray(mB.transpose(1, 0, 2, 3).reshape(128, 2 * 4 * 128)).astype(np.float32)
    c['mC'] = np.ascontiguousarray(mC.transpose(1, 0, 2, 3).reshape(128, 2 * 4 * 128)).astype(np.float32)
    bo = (p[:, None] // 64 == p[None, :] // 64).astype(np.float32)
    c['bones'] = bo.astype(NPBF)
    c['bo64'] = (bo / 64.0).astype(np.float32)
    tcol = np.arange(512)
    sm = np.stack([(tcol % 64 != 0), (tcol % 64 != 63)], 0).astype(np.float32)
    c['segm'] = np.ascontiguousarray(np.tile(sm[None], (128, 1, 1)).reshape(128, 1024))
    idr = (np.arange(64)[None, None, :] == (p % 64)[:, None, None]).astype(np.float32)
    c['identrep'] = np.ascontiguousarray(np.tile(idr, (1, 8, 1)).reshape(128, 512))
    return c


CONST_DT = {'ident': F32, 'tabC': BF16, 'tabS': BF16, 'zfeat': F32, 'deltas2': F32, 'negt': F32, 'altcol': BF16,
            'altrow': BF16, 'fscale': F32, 'mA': F32, 'mB': F32, 'mC': F32, 'bones': BF16, 'bo64': F32,
            'segm': F32, 'identrep': F32}


class _Eng:
    def __init__(self, name, eng, sem):
        self.name, self.eng, self.sem = name, eng, sem
        self.count = 0
        self.waited = {}


class _DSem:
    def __init__(self, sem):
        self.sem = sem
        self.count = 0


class Sched:
    def __init__(self, nc, es):
        self.nc, self.es = nc, es
        self.engs = {}
        for name, e in [('pe', nc.tensor), ('act', nc.scalar), ('dve', nc.vector), ('pool', nc.gpsimd), ('sp', nc.sync)]:
            self.engs[name] = _Eng(name, e, es.enter_context(nc.semaphore('sem_' + name)))
        self.res = {}
        self.dsems = []
        self.nins = 0
        self.fence = []
        self.fence_id = 0
        self.free = []
        self.cur = []

    def dsem(self, name, persist=False):
        if not persist and self.free:
            d = self.free.pop()
        else:
            d = _DSem(self.es.enter_context(self.nc.semaphore('ds_%d' % len(self.dsems))))
            self.dsems.append(d)
        if not persist:
            self.cur.append(d)
        return d

    def _collect(self, reads, writes):
        evs = []
        for k in reads:
            st = self.res.get(k)
            if st is not None and st[0] is not None:
                evs.append(st[0])
        for k in writes:
            st = self.res.get(k)
            if st is not None:
                if st[0] is not None:
                    evs.append(st[0])
                evs.extend(st[1].values())
        return evs

    def barrier(self):
        f = []
        for X in self.engs.values():
            if X.count > 0:
                f.append((X.sem, X.count, None))
        for d in self.dsems:
            if d.count > 0:
                f.append((d.sem, d.count, None))
        self.fence = f
        self.fence_id += 1
        self.free.extend(self.cur)
        self.cur = []

    def _wait(self, E, evs, skip_self=False):
        if getattr(E, 'fence_id', 0) != self.fence_id:
            E.fence_id = self.fence_id
            evs = list(evs) + [ev for ev in self.fence if ev[0] is not E.sem]
        need = {}
        for (sem, val, ds) in evs:
            if ds is not None:
                val = ds.count
            if skip_self and sem is E.sem:
                continue
            k = id(sem)
            if k not in need or need[k][1] < val:
                need[k] = (sem, val)
        for k, (sem, val) in need.items():
            if E.waited.get(k, 0) >= val:
                continue
            E.eng.wait_ge(sem, val)
            E.waited[k] = val

    def _commit(self, ev, reads, writes):
        for k in reads:
            st = self.res.get(k)
            if st is None:
                st = [None, {}]
                self.res[k] = st
            st[1][id(ev[0])] = ev
        for k in writes:
            self.res[k] = [ev, {}]

    def op(self, engname, fn, reads=(), writes=()):
        E = self.engs[engname]
        self._wait(E, self._collect(reads, writes), skip_self=(engname == 'pe'))
        ins = fn(E.eng)
        E.count += 1
        ins.then_inc(E.sem, 1)
        self._commit((E.sem, E.count, None), reads, writes)
        self.nins += 1

    def dma(self, qname, out, in_, ds, reads=(), writes=(), **kw):
        E = self.engs[qname]
        self._wait(E, self._collect(reads, writes))
        ins = E.eng.dma_start(out=out, in_=in_, **kw)
        ds.count += 16
        ins.then_inc(ds.sem, 16)
        self._commit((ds.sem, ds.count, ds), reads, writes)
        self.nins += 1

    def finish(self):
        E = self.engs['sp']
        for d in self.dsems:
            if d.count > 0 and E.waited.get(id(d.sem), 0) < d.count:
                E.eng.wait_ge(d.sem, d.count)
                E.waited[id(d.sem)] = d.count
        for n in ('pe', 'act', 'dve', 'pool'):
            X = self.engs[n]
            if X.count > 0:
                E.eng.wait_ge(X.sem, X.count)


class Cfg:
    def __init__(self, L=4096, NB=2, layers=('H', 'R', 'H', 'R'), ffn=True, dbg=False, stop_after=None):
        self.L, self.NB, self.layers, self.ffn, self.dbg, self.stop_after = L, NB, tuple(layers), ffn, dbg, stop_after


def build(cfg, shapes):
    L, NB = cfg.L, cfg.NB
    T = NB * L
    NT = L // 128
    TBS = L // 512
    NTB = T // 512
    NFFT = 2 * L
    nc = bass.Bass("TRN2", target_bir_lowering=False)
    voff, vper, vrows = vec_layout(shapes)
    NVC = vrows // 128

    def din(name, shape, dt=F32):
        return nc.dram_tensor(name, list(shape), dt, kind="ExternalInput").ap()

    def dscr(name, shape, dt):
        return nc.dram_tensor(name, list(shape), dt, kind=("ExternalOutput" if cfg.dbg else "Internal")).ap()

    x_in = din('x', [NB, L, D])
    out_d = nc.dram_tensor('out', [NB, L, D], F32, kind="ExternalOutput").ap()
    pvec = din('pvec', [vrows, 128])
    W = {n: din(n, shapes[n]) for n in BIG_W}
    cshape = {'ident': [128, 128], 'tabC': [NT, 128, NT * 128], 'tabS': [NT, 128, NT * 128], 'zfeat': [33, L],
              'deltas2': [1, 2048], 'negt': [128, NT], 'altcol': [128, 1], 'altrow': [1, 128], 'fscale': [128, NT],
              'mA': [128, 512], 'mB': [128, 1024], 'mC': [128, 1024], 'bones': [128, 128], 'bo64': [128, 128],
              'segm': [128, 1024], 'identrep': [128, 512]}
    CD = {n: din('c_' + n, cshape[n], CONST_DT[n]) for n in cshape}

    XT = dscr('XT', [D, T], F32)
    U = dscr('U', [2 * FF, T], BF16)
    Z = dscr('Z', [3 * D, T], BF16)
    GX0 = dscr('GX0', [D, T], BF16)
    YG = dscr('YG', [D, T], BF16)
    KS = dscr('KS', [L, D], BF16)
    KD = dscr('KD', [L, D], BF16)
    KRE = dscr('KRE', [L, D], F32)
    KIM = dscr('KIM', [L, D], F32)
    QT = dscr('QT', [2, D, T], F32)
    GS = dscr('GS', [2, D, T], F32)
    HS = dscr('HS', [2, D, T], F32)
    YL = dscr('YL', [D, T], F32)
    YS = dscr('YS', [2, D, T], F32)
    G2 = dscr('G2', [D, T], BF16)
    BON = dscr('BON', [D, T], BF16)
    VF = dscr('VF', [D, T], F32)
    PRJ = dscr('PRJ', [7, D, T], F32)

    def fm(ap):
        return ap.rearrange("(c p) t -> p c t", p=128)

    XTv, Uv, Zv, GX0v, YGv = fm(XT), fm(U), fm(Z), fm(GX0), fm(YG)
    YLv, G2v, BONv, VFv = fm(YL), fm(G2), fm(BON), fm(VF)
    QTv = [fm(QT[d]) for d in range(2)]
    GSv = [fm(GS[d]) for d in range(2)]
    HSv = [fm(HS[d]) for d in range(2)]
    YSv = [fm(YS[d]) for d in range(2)]
    PRJv = [fm(PRJ[i]) for i in range(7)]

    es = contextlib.ExitStack()
    with es:
        s = Sched(nc, es)

        class Pool_:
            def __init__(self):
                self.st = contextlib.ExitStack()
                self.n = 0

            def t(self, name, shape, dt):
                need = int(np.prod(shape[1:])) * (2 if dt == BF16 else 4)
                if need > nc.sbuf_bytes_remaining:
                    print("SBUF overflow allocating %s %s: need %d, remaining %d" % (name, shape, need, nc.sbuf_bytes_remaining), flush=True)
                    raise RuntimeError("SBUF overflow allocating %s %s: need %d, remaining %d" % (name, shape, need, nc.sbuf_bytes_remaining))
                return self.st.enter_context(nc.sbuf_tensor(name, list(shape), dt))

            def close(self):
                self.st.close()
                s.barrier()

        uid = [0]

        def uname(p):
            uid[0] += 1
            return "%s_%d" % (p, uid[0])

        gp = Pool_()
        es.callback(gp.close)
        ps = [es.enter_context(nc.psum_tensor("ps%d" % i, [128, 512], F32)) for i in range(6)]
        psbs = [es.enter_context(nc.psum_tensor("psb%d" % i, [128, 1024], BF16)) for i in range(2)]
        pscnt = [0]

        def nbank(lst=(0, 1, 2, 3, 4, 5)):
            i = lst[pscnt[0] % len(lst)]
            pscnt[0] += 1
            return i

        ecnt = [0]

        def evac_eng():
            ecnt[0] += 1
            return 'act' if ecnt[0] % 2 else 'dve'

        def copy_op(eng, out, in_):
            if eng == 'act':
                return lambda e: e.activation(out=out, in_=in_, func=AF.Copy)
            return lambda e: e.tensor_copy(out=out, in_=in_)

        ident = gp.t('ident', [128, 128], F32)
        identb = gp.t('identb', [128, 128], BF16)
        colsT = gp.t('colsT', [128, vrows], F32)
        ones_b = gp.t('ones_b', [128, 128], BF16)
        dsc = s.dsem('const', persist=True)
        s.dma('sp', ident[:], CD['ident'][:, :], dsc, writes=['ident'])
        s.op('dve', lambda e: e.tensor_copy(out=identb[:], in_=ident[:]), reads=['ident'], writes=['identb'])
        s.op('dve', lambda e: e.memset(ones_b[:], 1.0), writes=['ones_b'])
        pvst = gp.t('pvst', [128, 128], F32)
        ds_pv = s.dsem('pv', persist=True)
        for c in range(NVC):
            s.dma('sp', pvst[:], pvec[c * 128:(c + 1) * 128, :], ds_pv, writes=['pvst'])
            b = nbank()
            s.op('pe', lambda e, b=b: e.transpose(ps[b][:, 0:128], pvst[:], ident[:]), reads=['pvst', 'ident'], writes=[('ps', b)])
            s.op('dve', lambda e, b=b, c=c: e.tensor_copy(out=colsT[:, c * 128:(c + 1) * 128], in_=ps[b][:, 0:128]),
                 reads=[('ps', b)], writes=['colsT'])

        def col(name, idx, c):
            o = voff[name] + idx * vper[name] + c
            return colsT[:, o:o + 1]

        def colr(name, idx, c0, n):
            o = voff[name] + idx * vper[name] + c0
            return colsT[:, o:o + n]

        NSTG = 3
        stg = [gp.t('stg%d' % i, [128, 1024], F32) for i in range(NSTG)]
        ds_stg = [s.dsem('stg%d' % i, persist=True) for i in range(NSTG)]
        stgc = [0]

        def load_w(dst_fn, src, krows, ncols, key, qname='sp'):
            nk = (krows + 127) // 128
            for kc in range(nk):
                r = min(128, krows - kc * 128)
                for c0 in range(0, ncols, 1024):
                    n = min(1024, ncols - c0)
                    i = stgc[0] % NSTG
                    stgc[0] += 1
                    s.dma(qname, stg[i][0:r, 0:n], src[kc * 128:kc * 128 + r, c0:c0 + n], ds_stg[i], writes=[('stg', i)])
                    s.op('pool', lambda e, i=i, r=r, n=n, kc=kc, c0=c0: e.tensor_copy(out=dst_fn(kc, c0, n), in_=stg[i][0:r, 0:n]),
                         reads=[('stg', i)], writes=[(key, kc, c0)])

        def wkeys(key, krows, ncols):
            return [(key, kc, c0) for kc in range((krows + 127) // 128) for c0 in range(0, ncols, 1024)]

        def rmsnorm(lp, xin, xkey, Wd, gname, gidx, hout, hkey, tag):
            sq, rs = lp['sq'], lp['rs']
            s.op('act', lambda e: e.activation(out=sq[:, :, 0:Wd], in_=xin[:, :, 0:Wd], func=AF.Square), reads=[xkey], writes=['sq'])
            c0 = 0
            while c0 < Wd:
                n = min(512, Wd - c0)
                b = nbank()

                def g(e, b=b, c0=c0, n=n):
                    for k in range(8):
                        r = e.matmul(ps[b][:, 0:n], lhsT=ones_b[:], rhs=sq[:, k, c0:c0 + n], start=(k == 0), stop=(k == 7))
                    return r
                s.op('pe', g, reads=['sq', 'ones_b'], writes=[('ps', b)])
                s.op('dve', lambda e, b=b, c0=c0, n=n: e.tensor_scalar(out=rs[:, c0:c0 + n], in0=ps[b][:, 0:n], scalar1=1.0 / D, scalar2=1e-6,
                                                                        op0=ALU.mult, op1=ALU.add), reads=[('ps', b)], writes=[('rs', c0)])
                c0 += n
            rk = [('rs', c) for c in range(0, Wd, 512)]
            s.op('act', lambda e: e.activation(out=rs[:, 0:Wd], in_=rs[:, 0:Wd], func=AF.Sqrt), reads=rk, writes=rk)
            s.op('dve', lambda e: e.reciprocal(out=rs[:, 0:Wd], in_=rs[:, 0:Wd]), reads=rk, writes=rk)
            for k in range(8):
                s.op('dve', lambda e, k=k: e.scalar_tensor_tensor(out=hout[:, k, 0:Wd], in0=xin[:, k, 0:Wd], scalar=col(gname, gidx, k),
                                                                   in1=rs[:, 0:Wd], op0=ALU.mult, op1=ALU.mult),
                     reads=[xkey] + rk + ['colsT'], writes=[(hkey, k)])
            return [(hkey, k) for k in range(8)]

        def stage_in():
            lp = Pool_()
            xtok = [lp.t(uname('xtok'), [128, D], F32) for _ in range(4)]
            ds_xtok = [s.dsem(uname('xtok')) for _ in range(4)]
            xo = [lp.t(uname('xo'), [128, 8, 512], F32) for _ in range(2)]
            ds_xo = [s.dsem(uname('xo')) for _ in range(2)]
            for tb in range(NTB):
                b, t0 = tb // TBS, (tb % TBS) * 512
                o = tb % 2
                for j in range(4):
                    s.dma('sp', xtok[j][:], x_in[b, t0 + j * 128:t0 + (j + 1) * 128, :], ds_xtok[j], writes=[('xtok', j)])
                for k in range(8):
                    bk = nbank()

                    def g(e, bk=bk, k=k):
                        for j in range(4):
                            r = e.transpose(ps[bk][:, j * 128:(j + 1) * 128], xtok[j][:, k * 128:(k + 1) * 128], ident[:])
                        return r
                    s.op('pe', g, reads=[('xtok', j) for j in range(4)] + ['ident'], writes=[('ps', bk)])
                    en = evac_eng()
                    s.op(en, copy_op(en, xo[o][:, k, :], ps[bk][:, :]), reads=[('ps', bk)], writes=[('xo', o, k)])
                s.dma('pool', XTv[:, :, tb * 512:(tb + 1) * 512], xo[o][:], ds_xo[o], reads=[('xo', o, k) for k in range(8)], writes=[('XT', tb)])
            lp.close()

        def stage_out():
            lp = Pool_()
            lpd = {'sq': lp.t(uname('sq'), [128, 8, 512], BF16), 'rs': lp.t(uname('rs'), [128, 512], F32)}
            xin = [lp.t(uname('xin'), [128, 8, 512], F32) for _ in range(2)]
            ds_xin = [s.dsem(uname('xin')) for _ in range(2)]
            hh = lp.t(uname('hfin'), [128, 8, 512], F32)
            ot = [lp.t(uname('ot'), [128, D], F32) for _ in range(2)]
            ds_ot = [s.dsem(uname('ot')) for _ in range(2)]
            oc = 0
            for tb in range(NTB):
                b, t0 = tb // TBS, (tb % TBS) * 512
                xs = tb % 2
                s.dma('sp', xin[xs][:], XTv[:, :, tb * 512:(tb + 1) * 512], ds_xin[xs], reads=[('XT', tb)], writes=[('xin', xs)])
                hk = rmsnorm(lpd, xin[xs], ('xin', xs), 512, 'final_norm_g', 0, hh, 'hfin', 'f')
                for j in range(4):
                    o = oc % 2
                    oc += 1
                    for half in range(2):
                        bk = nbank()

                        def g(e, bk=bk, j=j, half=half):
                            for kk in range(4):
                                k = half * 4 + kk
                                r = e.transpose(ps[bk][:, kk * 128:(kk + 1) * 128], hh[:, k, j * 128:(j + 1) * 128], ident[:])
                            return r
                        s.op('pe', g, reads=hk + ['ident'], writes=[('ps', bk)])
                        en = evac_eng()
                        s.op(en, copy_op(en, ot[o][:, half * 512:(half + 1) * 512], ps[bk][:, :]), reads=[('ps', bk)], writes=[('ot', o, half)])
                    s.dma('pool', out_d[b, t0 + j * 128:t0 + (j + 1) * 128, :], ot[o][:], ds_ot[o], reads=[('ot', o, 0), ('ot', o, 1)], writes=[('OUT', tb, j)])
            lp.close()

        def stage_ffn(l):
            lp = Pool_()
            lpd = {'sq': lp.t(uname('sq'), [128, 8, 512], BF16), 'rs': lp.t(uname('rs'), [128, 512], F32)}
            wup = lp.t(uname('wup'), [128, 8, 2 * FF], BF16)
            load_w(lambda kc, c0, n: wup[:, kc, c0:c0 + n], W['ff_w_up'][l], D, 2 * FF, 'wup')
            wk = wkeys('wup', D, 2 * FF)
            xin = [lp.t(uname('xin'), [128, 8, 512], F32) for _ in range(2)]
            ds_xin = [s.dsem(uname('xin')) for _ in range(2)]
            hb = [lp.t(uname('hb'), [128, 8, 512], BF16) for _ in range(2)]
            ust = [lp.t(uname('ust'), [128, 4, 512], BF16) for _ in range(3)]
            ds_ust = [s.dsem(uname('ust')) for _ in range(3)]
            uc = 0
            for tb in range(NTB):
                xs = tb % 2
                s.dma('sp', xin[xs][:], XTv[:, :, tb * 512:(tb + 1) * 512], ds_xin[xs], reads=[('XT', tb)], writes=[('xin', xs)])
                hk = rmsnorm(lpd, xin[xs], ('xin', xs), 512, 'norm_ffn_g', l, hb[xs], ('hb', xs), 'u')
                for n4 in range(11):
                    o = uc % 3
                    uc += 1
                    for q in range(4):
                        n = n4 * 4 + q
                        bk = nbank()

                        def g(e, bk=bk, n=n, xs=xs):
                            for k in range(8):
                                r = e.matmul(ps[bk][:, :], lhsT=wup[:, k, n * 128:(n + 1) * 128], rhs=hb[xs][:, k, :], start=(k == 0), stop=(k == 7))
                            return r
                        s.op('pe', g, reads=hk + wk, writes=[('ps', bk)])
                        en = evac_eng()
                        s.op(en, copy_op(en, ust[o][:, q, :], ps[bk][:, :]), reads=[('ps', bk)], writes=[('ust', o, q)])
                    s.dma('pool', Uv[:, n4 * 4:(n4 + 1) * 4, tb * 512:(tb + 1) * 512], ust[o][:], ds_ust[o],
                          reads=[('ust', o, q) for q in range(4)], writes=[('U', tb, n4)])
            lp.close()
            lp = Pool_()
            wdn = lp.t(uname('wdn'), [128, FC, D], BF16)
            load_w(lambda kc, c0, n: wdn[:, kc, c0:c0 + n], W['ff_w_down'][l], FF, D, 'wdn')
            wk = wkeys('wdn', FF, D)
            dg = lp.t(uname('dg'), [128, 3 * 2 * FC, 128], BF16)
            for tap in range(3):
                for c in range(2 * FC):
                    s.op('dve', lambda e, tap=tap, c=c: e.tensor_scalar(out=dg[:, tap * 2 * FC + c, :], in0=identb[:], scalar1=col('ff_conv_w', l * 3 + tap, c), scalar2=None, op0=ALU.mult),
                         reads=['identb', 'colsT'], writes=[('dg', tap, c)])
            dgk = [('dg', tap, c) for tap in range(3) for c in range(2 * FC)]
            xin = [lp.t(uname('xin'), [128, 8, 512], F32) for _ in range(2)]
            ds_xin = [s.dsem(uname('xin')) for _ in range(2)]
            GSZ = 6
            groups = [(g0, min(GSZ, FC - g0)) for g0 in range(0, FC, GSZ)]
            uin = [lp.t(uname('uin'), [128, 2, GSZ, 514], BF16) for _ in range(2)]
            ds_uin = [s.dsem(uname('uin')) for _ in range(2)]
            act = [lp.t(uname('act'), [128, FC, 512], BF16) for _ in range(2)]
            sg = [lp.t(uname('sg'), [128, 512], F32) for _ in range(2)]
            ds_xo = [s.dsem(uname('xo')) for _ in range(2)]
            uc = 0
            pc = 0
            for tb in range(NTB):
                b, t0 = tb // TBS, (tb % TBS) * 512
                xs = tb % 2
                s.dma('sp', xin[xs][:], XTv[:, :, tb * 512:(tb + 1) * 512], ds_xin[xs], reads=[('XT', tb)], writes=[('xin', xs, n) for n in range(8)])
                lo = 1 if t0 == 0 else 0
                hi = 513 if t0 + 512 == L else 514
                nbr = [tb] + ([tb - 1] if lo == 0 else []) + ([tb + 1] if hi == 514 else [])
                ukeys = [('U', t_, n4) for t_ in nbr for n4 in range(11)]
                for (g0, gn) in groups:
                    us = uc % 2
                    uc += 1
                    if lo == 1:
                        s.op('pool', lambda e, us=us: e.memset(uin[us][:, :, :, 0:1], 0.0), writes=[('uin', us, 0), ('uin', us, 1)])
                    if hi == 513:
                        s.op('pool', lambda e, us=us: e.memset(uin[us][:, :, :, 513:514], 0.0), writes=[('uin', us, 0), ('uin', us, 1)])
                    c0 = tb * 512 - 1
                    for gv in range(2):
                        s.dma('sp', uin[us][:, gv, 0:gn, lo:hi], Uv[:, gv * FC + g0:gv * FC + g0 + gn, c0 + lo:c0 + hi], ds_uin[us],
                              reads=ukeys, writes=[('uin', us, gv)])
                    for i in range(gn):
                        gi = g0 + i
                        vi = FC + gi
                        p_ = pc % 2
                        pc += 1
                        bg, bv = nbank(), nbank()

                        def g(e, us=us, i=i, gi=gi, vi=vi, bg=bg, bv=bv):
                            for tap in range(3):
                                e.matmul(ps[bg][:, :], lhsT=dg[:, tap * 2 * FC + gi, :], rhs=uin[us][:, 0, i, tap:tap + 512], start=(tap == 0), stop=(tap == 2))
                            for tap in range(3):
                                r = e.matmul(ps[bv][:, :], lhsT=dg[:, tap * 2 * FC + vi, :], rhs=uin[us][:, 1, i, tap:tap + 512], start=(tap == 0), stop=(tap == 2))
                            return r
                        s.op('pe', g, reads=[('uin', us, 0), ('uin', us, 1)] + dgk, writes=[('ps', bg), ('ps', bv)])
                        s.op('act', lambda e, p_=p_, bg=bg, gi=gi: e.activation(out=sg[p_][:], in_=ps[bg][:, :], func=AF.Silu, bias=col('ff_conv_b', l, gi)),
                             reads=[('ps', bg), 'colsT'], writes=[('sg', p_)])
                        s.op('dve', lambda e, p_=p_, bv=bv, gi=gi, vi=vi, xs=xs: e.scalar_tensor_tensor(out=act[xs][:, gi, :], in0=ps[bv][:, :], scalar=col('ff_conv_b', l, vi), in1=sg[p_][:],
                                                                                                 op0=ALU.add, op1=ALU.mult),
                             reads=[('ps', bv), ('sg', p_), 'colsT'], writes=[('act', xs, gi)])
                ak = [('act', xs, gi) for gi in range(FC)]
                for n in range(8):
                    bk = nbank()

                    def g(e, bk=bk, n=n, xs=xs):
                        for i in range(FC):
                            r = e.matmul(ps[bk][:, :], lhsT=wdn[:, i, n * 128:(n + 1) * 128], rhs=act[xs][:, i, :], start=(i == 0), stop=(i == FC - 1))
                        return r
                    s.op('pe', g, reads=ak + wk, writes=[('ps', bk)])
                    s.op('dve', lambda e, bk=bk, n=n, xs=xs: e.tensor_tensor(out=xin[xs][:, n, :], in0=ps[bk][:, :], in1=xin[xs][:, n, :], op=ALU.add),
                         reads=[('ps', bk), ('xin', xs, n)], writes=[('xin', xs, n)])
                s.dma('pool', XTv[:, :, tb * 512:(tb + 1) * 512], xin[xs][:], ds_xo[xs], reads=[('xin', xs, n) for n in range(8)], writes=[('XT', tb)])
            lp.close()

        KNY = gp.t('KNY', [1, D], F32)

        def hy_filter(j):
            lp = Pool_()
            zf = lp.t(uname('zf'), [33, L], F32)
            w1 = lp.t(uname('hw1'), [33, 64], F32)
            w2 = lp.t(uname('hw2'), [64, 64], F32)
            w3 = lp.t(uname('hw3'), [64, 64], F32)
            w4 = lp.t(uname('hw4'), [64, 2048], F32)
            par = lp.t(uname('hpar'), [64, 8], F32)
            dl = lp.t(uname('dl'), [128, 2048], F32)
            ngt = lp.t(uname('ngt'), [128, NT], F32)
            skp = lp.t(uname('skp'), [1, D], F32)
            arg = lp.t(uname('arg'), [64, L], F32)
            hcur = lp.t(uname('hcur'), [64, L], F32)
            kint = lp.t(uname('kint'), [64, L], I32)
            kfl = lp.t(uname('kfl'), [64, L], F32)
            ds = s.dsem(uname('hyf'))
            s.dma('sp', zf[:], CD['zfeat'][:, :], ds, writes=['zf'])
            s.dma('sp', w1[:], W['hy_f_w1'][j], ds, writes=['hw1'])
            s.dma('sp', w2[:], W['hy_f_w2'][j], ds, writes=['hw2'])
            s.dma('sp', w3[:], W['hy_f_w3'][j], ds, writes=['hw3'])
            s.dma('sp', w4[:], W['hy_f_w4'][j], ds, writes=['hw4'])
            for i, nm in enumerate(['hy_f_b1', 'hy_f_b2', 'hy_f_b3', 'hy_f_freq']):
                s.dma('sp', par[:, i:i + 1], W[nm][j].rearrange("(p o) -> p o", o=1), ds, writes=[('hpar', i)])
            s.dma('sp', dl[:], CD['deltas2'][0:1, :].partition_broadcast(128), ds, writes=['dl'])
            s.dma('sp', ngt[:], CD['negt'][:, :], ds, writes=['ngt'])
            s.dma('sp', skp[:], W['hy_skip'][j:j + 1, :], ds, writes=['skp'])
            pk = [('hpar', i) for i in range(4)]
            s.op('dve', lambda e: e.tensor_tensor(out=par[:, 4:7], in0=par[:, 0:3], in1=par[:, 3:4].to_broadcast([64, 3]), op=ALU.mult),
                 reads=pk, writes=['hparfb'])
            srcs = [(zf, 33, 'zf', w1, 'hw1'), (hcur, 64, 'hcur', w2, 'hw2'), (hcur, 64, 'hcur', w3, 'hw3')]
            for i, (src, kk, skey, wt, wkey) in enumerate(srcs):
                for blk in range(L // 512):
                    bk = nbank()
                    s.op('pe', lambda e, bk=bk, src=src, kk=kk, wt=wt, blk=blk: e.matmul(ps[bk][0:64, :], lhsT=wt[0:kk, 0:64], rhs=src[0:kk, blk * 512:(blk + 1) * 512],
                                                                                          start=True, stop=True), reads=[skey, wkey], writes=[('ps', bk)])
                    s.op('dve', lambda e, bk=bk, blk=blk, i=i: e.tensor_scalar(out=arg[:, blk * 512:(blk + 1) * 512], in0=ps[bk][0:64, :], scalar1=par[:, 3:4],
                                                                                  scalar2=par[:, 4 + i:5 + i], op0=ALU.mult, op1=ALU.add),
                         reads=[('ps', bk), 'hparfb'] + pk, writes=['arg'])
                s.op('dve', lambda e: e.tensor_scalar(out=kint[:], in0=arg[:], scalar1=float(1.0 / (2 * np.pi)), scalar2=None, op0=ALU.mult), reads=['arg'], writes=['kint'])
                s.op('dve', lambda e: e.tensor_copy(out=kfl[:], in_=kint[:]), reads=['kint'], writes=['kfl'])
                s.op('dve', lambda e: e.scalar_tensor_tensor(out=arg[:], in0=kfl[:], scalar=float(-2 * np.pi), in1=arg[:], op0=ALU.mult, op1=ALU.add), reads=['kfl', 'arg'], writes=['arg'])
                s.op('dve', lambda e: e.tensor_single_scalar(out=kfl[:], in_=arg[:], scalar=PI, op=ALU.is_gt), reads=['arg'], writes=['kfl'])
                s.op('dve', lambda e: e.scalar_tensor_tensor(out=arg[:], in0=kfl[:], scalar=float(-2 * np.pi), in1=arg[:], op0=ALU.mult, op1=ALU.add), reads=['kfl', 'arg'], writes=['arg'])
                s.op('dve', lambda e: e.tensor_single_scalar(out=kfl[:], in_=arg[:], scalar=-PI, op=ALU.is_lt), reads=['arg'], writes=['kfl'])
                s.op('dve', lambda e: e.scalar_tensor_tensor(out=arg[:], in0=kfl[:], scalar=float(2 * np.pi), in1=arg[:], op0=ALU.mult, op1=ALU.add), reads=['kfl', 'arg'], writes=['arg'])
                s.op('dve', lambda e: e.tensor_scalar(out=arg[:], in0=arg[:], scalar1=-3.14159, scalar2=3.14159, op0=ALU.max, op1=ALU.min), reads=['arg'], writes=['arg'])
                s.op('act', lambda e: e.activation(out=hcur[:], in_=arg[:], func=AF.Sin), reads=['arg'], writes=['hcur'])
            dec = [lp.t(uname('dec'), [128, 2048], F32) for _ in range(2)]
            kfb = [lp.t(uname('kfb'), [128, 2048], F32) for _ in range(2)]
            ksd = [lp.t(uname('ksd'), [128, 2, D], BF16) for _ in range(2)]
            ds_ksd = [s.dsem(uname('ksd')) for _ in range(2)]
            for jt in range(NT):
                o = jt % 2
                s.op('act', lambda e, o=o, jt=jt: e.activation(out=dec[o][:], in_=dl[:], func=AF.Exp, scale=ngt[:, jt:jt + 1]), reads=['dl', 'ngt'], writes=[('dec', o)])
                for q in range(4):
                    bk = nbank()
                    s.op('pe', lambda e, bk=bk, jt=jt, q=q: e.matmul(ps[bk][:, :], lhsT=hcur[0:64, jt * 128:(jt + 1) * 128], rhs=w4[0:64, q * 512:(q + 1) * 512], start=True, stop=True),
                         reads=['hcur', 'hw4'], writes=[('ps', bk)])
                    s.op('dve', lambda e, bk=bk, o=o, q=q: e.tensor_tensor(out=kfb[o][:, q * 512:(q + 1) * 512], in0=ps[bk][:, :], in1=dec[o][:, q * 512:(q + 1) * 512], op=ALU.mult),
                         reads=[('ps', bk), ('dec', o)], writes=[('kfb', o, q)])
                kq = [('kfb', o, q) for q in range(4)]
                if jt == 0:
                    s.op('dve', lambda e, o=o: e.tensor_tensor(out=kfb[o][0:1, 0:D], in0=kfb[o][0:1, 0:D], in1=skp[0:1, :], op=ALU.add), reads=kq + ['skp'], writes=kq)
                s.op('pool', lambda e, o=o: e.tensor_tensor(out=ksd[o][:, 0, :], in0=kfb[o][:, 0:D], in1=kfb[o][:, D:2 * D], op=ALU.add), reads=kq, writes=[('ksd', o, 0)])
                s.op('pool', lambda e, o=o: e.tensor_tensor(out=ksd[o][:, 1, :], in0=kfb[o][:, 0:D], in1=kfb[o][:, D:2 * D], op=ALU.subtract), reads=kq, writes=[('ksd', o, 1)])
                s.dma('act', KS[jt * 128:(jt + 1) * 128, :], ksd[o][:, 0, :], ds_ksd[o], reads=[('ksd', o, 0)], writes=[('KS', jt)])
                s.dma('act', KD[jt * 128:(jt + 1) * 128, :], ksd[o][:, 1, :], ds_ksd[o], reads=[('ksd', o, 1)], writes=[('KD', jt)])
            lp.close()

        def hy_spectrum(j):
            lp = Pool_()
            ksb = lp.t(uname('ksb'), [128, NT, 512], BF16)
            kdb = lp.t(uname('kdb'), [128, NT, 512], BF16)
            tabc = [lp.t(uname('tabc'), [128, NT * 128], BF16) for _ in range(2)]
            tabs = [lp.t(uname('tabs'), [128, NT * 128], BF16) for _ in range(2)]
            ds_tc = [s.dsem(uname('tc')) for _ in range(2)]
            ds_ts = [s.dsem(uname('ts')) for _ in range(2)]
            fsc = lp.t(uname('fsc'), [128, NT], F32)
            alt = lp.t(uname('alt'), [128, 1], BF16)
            kst = [lp.t(uname('kst'), [128, 2, 512], F32) for _ in range(2)]
            ds_kst = [s.dsem(uname('kst')) for _ in range(2)]
            ds = s.dsem(uname('hys'))
            ds2 = s.dsem(uname('hys2'))
            s.dma('sp', fsc[:], CD['fscale'][:, :], ds, writes=['fsc'])
            s.dma('sp', alt[:], CD['altcol'][:, :], ds, writes=['alt'])
            KSv = KS.rearrange("(j p) c -> p j c", p=128)
            KDv = KD.rearrange("(j p) c -> p j c", p=128)
            kall = [('KS', jt) for jt in range(NT)] + [('KD', jt) for jt in range(NT)]
            cnt = 0
            for h2 in range(2):
                s.dma('sp', ksb[:], KSv[:, :, h2 * 512:(h2 + 1) * 512], ds2, reads=kall, writes=['ksb'])
                s.dma('sp', kdb[:], KDv[:, :, h2 * 512:(h2 + 1) * 512], ds2, reads=kall, writes=['kdb'])
                bn = nbank()

                def gn(e, bn=bn):
                    for jj in range(NT):
                        r = e.matmul(ps[bn][0:1, :], lhsT=alt[:, 0:1], rhs=ksb[:, jj, :], start=(jj == 0), stop=(jj == NT - 1))
                    return r
                s.op('pe', gn, reads=['ksb', 'alt'], writes=[('ps', bn)])
                s.op('dve', lambda e, bn=bn, h2=h2: e.tensor_scalar(out=KNY[0:1, h2 * 512:(h2 + 1) * 512], in0=ps[bn][0:1, :], scalar1=float(1.0 / NFFT), scalar2=None, op0=ALU.mult),
                     reads=[('ps', bn)], writes=[('KNY', h2)])
                for a in range(NT):
                    o = cnt % 2
                    cnt += 1
                    s.dma('sp', tabc[o][:], CD['tabC'][a], ds_tc[o], writes=[('tabc', o)])
                    s.dma('sp', tabs[o][:], CD['tabS'][a], ds_ts[o], writes=[('tabs', o)])
                    for (tab, tkey, src, skey, q) in ((tabc, 'tabc', ksb, 'ksb', 0), (tabs, 'tabs', kdb, 'kdb', 1)):
                        bk = nbank()

                        def g(e, bk=bk, tab=tab, src=src, o=o):
                            for jj in range(NT):
                                r = e.matmul(ps[bk][:, :], lhsT=tab[o][:, jj * 128:(jj + 1) * 128], rhs=src[:, jj, :], start=(jj == 0), stop=(jj == NT - 1))
                            return r
                        s.op('pe', g, reads=[(tkey, o), skey], writes=[('ps', bk)])
                        s.op('dve', lambda e, bk=bk, o=o, q=q, a=a: e.tensor_scalar(out=kst[o][:, q, :], in0=ps[bk][:, :], scalar1=fsc[:, a:a + 1], scalar2=None, op0=ALU.mult),
                             reads=[('ps', bk), 'fsc'], writes=[('kst', o, q)])
                    s.dma('act', KRE[a * 128:(a + 1) * 128, h2 * 512:(h2 + 1) * 512], kst[o][:, 0, :], ds_kst[o], reads=[('kst', o, 0)], writes=[('KRE', a, h2)])
                    s.dma('act', KIM[a * 128:(a + 1) * 128, h2 * 512:(h2 + 1) * 512], kst[o][:, 1, :], ds_kst[o], reads=[('kst', o, 1)], writes=[('KIM', a, h2)])
            lp.close()

        def hy_inproj(l, j):
            lp = Pool_()
            lpd = {'sq': lp.t(uname('sq'), [128, 8, 512], BF16), 'rs': lp.t(uname('rs'), [128, 512], F32)}
            win = lp.t(uname('win'), [128, 8, 3 * D], BF16)
            load_w(lambda kc, c0, n: win[:, kc, c0:c0 + n], W['hy_w_in'][j], D, 3 * D, 'win')
            wk = wkeys('win', D, 3 * D)
            xin = [lp.t(uname('xin'), [128, 8, 512], F32) for _ in range(2)]
            ds_xin = [s.dsem(uname('xin')) for _ in range(2)]
            hb = [lp.t(uname('hb'), [128, 8, 512], BF16) for _ in range(2)]
            zst = [lp.t(uname('zst'), [128, 4, 512], BF16) for _ in range(3)]
            ds_zst = [s.dsem(uname('zst')) for _ in range(3)]
            uc = 0
            for tb in range(NTB):
                xs = tb % 2
                s.dma('sp', xin[xs][:], XTv[:, :, tb * 512:(tb + 1) * 512], ds_xin[xs], reads=[('XT', tb)], writes=[('xin', xs)])
                hk = rmsnorm(lpd, xin[xs], ('xin', xs), 512, 'norm_mix_g', l, hb[xs], ('hb', xs), 'hy')
                for n4 in range(6):
                    o = uc % 3
                    uc += 1
                    for q in range(4):
                        n = n4 * 4 + q
                        bk = nbank()

                        def g(e, bk=bk, n=n, xs=xs):
                            for k in range(8):
                                r = e.matmul(ps[bk][:, :], lhsT=win[:, k, n * 128:(n + 1) * 128], rhs=hb[xs][:, k, :], start=(k == 0), stop=(k == 7))
                            return r
                        s.op('pe', g, reads=hk + wk, writes=[('ps', bk)])
                        en = evac_eng()
                        if en == 'act':
                            s.op('act', lambda e, bk=bk, o=o, q=q, n=n: e.activation(out=zst[o][:, q, :], in_=ps[bk][:, :], func=AF.Identity, bias=col('hy_b_in', j, n)),
                                 reads=[('ps', bk), 'colsT'], writes=[('zst', o, q)])
                        else:
                            s.op('dve', lambda e, bk=bk, o=o, q=q, n=n: e.tensor_scalar(out=zst[o][:, q, :], in0=ps[bk][:, :], scalar1=col('hy_b_in', j, n), scalar2=None, op0=ALU.add),
                                 reads=[('ps', bk), 'colsT'], writes=[('zst', o, q)])
                    s.dma('pool', Zv[:, n4 * 4:(n4 + 1) * 4, tb * 512:(tb + 1) * 512], zst[o][:], ds_zst[o],
                          reads=[('zst', o, q) for q in range(4)], writes=[('Z', tb, n4)])
            lp.close()

        def hy_conv(l, j):
            lp = Pool_()
            unb = lp.t(uname('unb'), [128, NT, 512], BF16)
            ynb = lp.t(uname('ynb'), [128, 2 * NT, 512], BF16)
            ynq = lp.t(uname('ynq'), [1, 512], BF16)
            alt = lp.t(uname('alt'), [128, 1], BF16)
            altr = lp.t(uname('altr'), [1, 128], BF16)
            zin = [lp.t(uname('zin'), [128, 3, 514], BF16) for _ in range(2)]
            ds_zin = [s.dsem(uname('zin')) for _ in range(2)]
            tcv = [[lp.t(uname('tcv'), [128, 512], F32) for _ in range(3)] for _ in range(2)]
            gx = [lp.t(uname('gx'), [128, 512], BF16) for _ in range(2)]
            ds_gx = [s.dsem(uname('gx')) for _ in range(2)]
            ub = [lp.t(uname('ub'), [128, 512], BF16) for _ in range(2)]
            tabc = [lp.t(uname('tabc'), [128, NT * 128], BF16) for _ in range(2)]
            tabs = [lp.t(uname('tabs'), [128, NT * 128], BF16) for _ in range(2)]
            ds_tc = [s.dsem(uname('tc')) for _ in range(2)]
            ds_ts = [s.dsem(uname('ts')) for _ in range(2)]
            kri = [lp.t(uname('kri'), [128, 2, 512], F32) for _ in range(2)]
            ds_kri = [s.dsem(uname('kri')) for _ in range(2)]
            tm = [lp.t(uname('tm'), [128, 512], F32) for _ in range(4)]
            ytok = [lp.t(uname('ytok'), [128, 512], F32) for _ in range(2)]
            gq = [lp.t(uname('gq'), [128, 4, 512], BF16) for _ in range(1)]
            ds_gq = [s.dsem(uname('gq')) for _ in range(1)]
            ygst = [lp.t(uname('ygst'), [128, 4, 512], BF16) for _ in range(1)]
            ds_yg = [s.dsem(uname('yg')) for _ in range(1)]
            ds = s.dsem(uname('hyc'))
            s.dma('sp', alt[:], CD['altcol'][:, :], ds, writes=['alt'])
            s.dma('sp', altr[:], CD['altrow'][:, :], ds, writes=['altr'])
            zc = 0
            tcnt = 0
            NP5 = L // 512
            for nb in range(NB * 2):
                b, h2 = nb // 2, nb % 2
                def piece(nb, b, h2, cc, cg, pc, zs):
                    nonlocal tcnt
                    if True:
                        t0 = pc * 512
                        lo = 1 if t0 == 0 else 0
                        hi = 513 if t0 + 512 == L else 514
                        tbg = b * TBS + pc
                        nbr = [tbg] + ([tbg - 1] if lo == 0 else []) + ([tbg + 1] if hi == 514 else [])
                        zkeys = [('Z', t_, n4) for t_ in nbr for n4 in range(6)]
                        if lo == 1:
                            s.op('pool', lambda e, zs=zs: e.memset(zin[zs][:, :, 0:1], 0.0), writes=[('zin', zs, q) for q in range(3)])
                            yield
                        if hi == 513:
                            s.op('pool', lambda e, zs=zs: e.memset(zin[zs][:, :, 513:514], 0.0), writes=[('zin', zs, q) for q in range(3)])
                            yield
                        c0 = b * L + t0 - 1
                        for q in range(3):
                            s.dma('sp', zin[zs][:, q, lo:hi], Zv[:, q * 8 + cg, c0 + lo:c0 + hi], ds_zin[zs], reads=zkeys, writes=[('zin', zs, q)])
                            yield
                        for q in range(3):
                            ci = q * 8 + cg
                            src = zin[zs][:, q, :]
                            dst = tcv[zs][q]
                            zk = [('zin', zs, q), 'colsT']
                            s.op('act', lambda e, src=src, dst=dst, ci=ci: e.activation(out=dst[:], in_=src[:, 1:513], func=AF.Identity, bias=col('hy_conv_b', j, ci),
                                                                                         scale=col('hy_conv_w', j * 3 + 1, ci)), reads=zk, writes=[('tcv', zs, q)])
                            yield
                            s.op('dve', lambda e, src=src, dst=dst, ci=ci: e.scalar_tensor_tensor(out=dst[:], in0=src[:, 0:512], scalar=col('hy_conv_w', j * 3 + 0, ci), in1=dst[:],
                                                                                                   op0=ALU.mult, op1=ALU.add), reads=zk + [('tcv', zs, q)], writes=[('tcv', zs, q)])
                            yield
                            s.op('dve', lambda e, src=src, dst=dst, ci=ci: e.scalar_tensor_tensor(out=dst[:], in0=src[:, 2:514], scalar=col('hy_conv_w', j * 3 + 2, ci), in1=dst[:],
                                                                                                   op0=ALU.mult, op1=ALU.add), reads=zk + [('tcv', zs, q)], writes=[('tcv', zs, q)])
                            yield
                        s.op('act', lambda e, zs=zs: e.activation(out=gx[zs][:], in_=tcv[zs][0][:], func=AF.Copy), reads=[('tcv', zs, 0)], writes=[('gx', zs)])
                        yield
                        s.dma('act', GX0v[:, cg, b * L + t0:b * L + t0 + 512], gx[zs][:], ds_gx[zs], reads=[('gx', zs)], writes=[('GX0', nb, cc, pc)])
                        yield
                        s.op('pool', lambda e, zs=zs: e.tensor_tensor(out=ub[zs][:], in0=tcv[zs][1][:], in1=tcv[zs][2][:], op=ALU.mult),
                             reads=[('tcv', zs, 1), ('tcv', zs, 2)], writes=[('ub', zs)])
                        yield
                        hb_ = tcnt % 2
                        tcnt += 1

                        def g(e, zs=zs, hb_=hb_):
                            for q in range(4):
                                r = e.transpose(psbs[hb_][:, q * 128:(q + 1) * 128], ub[zs][:, q * 128:(q + 1) * 128], identb[:])
                            return r
                        s.op('pe', g, reads=[('ub', zs), 'identb'], writes=[('psb', hb_)])
                        yield
                        en = evac_eng()
                        s.op(en, copy_op(en, unb[:, pc * 4:(pc + 1) * 4, cc * 128:(cc + 1) * 128], psbs[hb_][:, 0:512].rearrange("p (q c) -> p q c", c=128)),
                             reads=[('psb', hb_)], writes=[('unb', pc, cc)])
                        yield
                pieces = [(cc_, pc_) for cc_ in range(4) for pc_ in range(NP5)]
                for g0_ in range(0, len(pieces), 2):
                    alive = [piece(nb, b, h2, pieces[g0_ + q_][0], h2 * 4 + pieces[g0_ + q_][0], pieces[g0_ + q_][1], q_) for q_ in range(min(2, len(pieces) - g0_))]
                    while alive:
                        na_ = []
                        for g_ in alive:
                            try:
                                next(g_)
                                na_.append(g_)
                            except StopIteration:
                                pass
                        alive = na_
                ukeys = [('unb', pc_, cc_) for pc_ in range(NP5) for cc_ in range(4)]
                bn = nbank()

                def gn(e, bn=bn):
                    for jj in range(NT):
                        r = e.matmul(ps[bn][0:1, :], lhsT=alt[:, 0:1], rhs=unb[:, jj, :], start=(jj == 0), stop=(jj == NT - 1))
                    return r
                s.op('pe', gn, reads=ukeys + ['alt'], writes=[('ps', bn)])
                s.op('dve', lambda e, bn=bn, h2=h2: e.tensor_tensor(out=ynq[0:1, :], in0=ps[bn][0:1, :], in1=KNY[0:1, h2 * 512:(h2 + 1) * 512], op=ALU.mult),
                     reads=[('ps', bn), ('KNY', h2)], writes=['ynq'])
                for a in range(NT):
                    o = a % 2
                    s.dma('sp', tabc[o][:], CD['tabC'][a], ds_tc[o], writes=[('tabc', o)])
                    s.dma('sp', tabs[o][:], CD['tabS'][a], ds_ts[o], writes=[('tabs', o)])
                    s.dma('sp', kri[o][:, 0, :], KRE[a * 128:(a + 1) * 128, h2 * 512:(h2 + 1) * 512], ds_kri[o], reads=[('KRE', a, h2)], writes=[('kri', o, 0)])
                    s.dma('sp', kri[o][:, 1, :], KIM[a * 128:(a + 1) * 128, h2 * 512:(h2 + 1) * 512], ds_kri[o], reads=[('KIM', a, h2)], writes=[('kri', o, 1)])
                    bks = []
                    for (tab, tkey) in ((tabc, 'tabc'), (tabs, 'tabs')):
                        bk = nbank()
                        bks.append(bk)

                        def g(e, bk=bk, tab=tab, o=o):
                            for jj in range(NT):
                                r = e.matmul(ps[bk][:, :], lhsT=tab[o][:, jj * 128:(jj + 1) * 128], rhs=unb[:, jj, :], start=(jj == 0), stop=(jj == NT - 1))
                            return r
                        s.op('pe', g, reads=ukeys + [(tkey, o)], writes=[('ps', bk)])
                    br, bi = bks
                    kk_ = [('kri', o, 0), ('kri', o, 1)]
                    s.op('dve', lambda e, br=br, o=o: e.tensor_tensor(out=tm[0][:], in0=ps[br][:, :], in1=kri[o][:, 0, :], op=ALU.mult), reads=[('ps', br)] + kk_, writes=[('tm', 0)])
                    s.op('dve', lambda e, bi=bi, o=o: e.tensor_tensor(out=tm[1][:], in0=ps[bi][:, :], in1=kri[o][:, 1, :], op=ALU.mult), reads=[('ps', bi)] + kk_, writes=[('tm', 1)])
                    s.op('pool', lambda e, a=a: e.tensor_tensor(out=ynb[:, a, :], in0=tm[0][:], in1=tm[1][:], op=ALU.subtract), reads=[('tm', 0), ('tm', 1)], writes=[('ynb', a)])
                    s.op('dve', lambda e, br=br, o=o: e.tensor_tensor(out=tm[2][:], in0=ps[br][:, :], in1=kri[o][:, 1, :], op=ALU.mult), reads=[('ps', br)] + kk_, writes=[('tm', 2)])
                    s.op('dve', lambda e, bi=bi, o=o: e.tensor_tensor(out=tm[3][:], in0=ps[bi][:, :], in1=kri[o][:, 0, :], op=ALU.mult), reads=[('ps', bi)] + kk_, writes=[('tm', 3)])
                    s.op('pool', lambda e, a=a: e.tensor_tensor(out=ynb[:, NT + a, :], in0=tm[2][:], in1=tm[3][:], op=ALU.add), reads=[('tm', 2), ('tm', 3)], writes=[('ynb', NT + a)])
                ykeys = [('ynb', a) for a in range(2 * NT)] + ['ynq']
                for bp in range(NT):
                    o = bp % 2
                    q4 = bp % 4
                    g4 = 0
                    s.dma('sp', tabc[o][:], CD['tabC'][bp], ds_tc[o], writes=[('tabc', o)])
                    s.dma('sp', tabs[o][:], CD['tabS'][bp], ds_ts[o], writes=[('tabs', o)])
                    if q4 == 0:
                        cols_ = slice(b * L + (bp // 4) * 512, b * L + (bp // 4) * 512 + 512)
                        s.dma('sp', gq[g4][:], GX0v[:, h2 * 4:(h2 + 1) * 4, cols_], ds_gq[g4],
                              reads=[('GX0', nb, cc, bp // 4) for cc in range(4)], writes=[('gq', g4)])
                    bk = nbank()

                    def g(e, bk=bk, o=o):
                        for jj in range(NT):
                            e.matmul(ps[bk][:, :], lhsT=tabc[o][:, jj * 128:(jj + 1) * 128], rhs=ynb[:, jj, :], start=(jj == 0), stop=False)
                        for jj in range(NT):
                            e.matmul(ps[bk][:, :], lhsT=tabs[o][:, jj * 128:(jj + 1) * 128], rhs=ynb[:, NT + jj, :], start=False, stop=False)
                        return e.matmul(ps[bk][:, :], lhsT=altr[0:1, :], rhs=ynq[0:1, :], start=False, stop=True)
                    s.op('pe', g, reads=ykeys + [('tabc', o), ('tabs', o), 'altr'], writes=[('ps', bk)])
                    s.op('act', lambda e, bk=bk, o=o: e.activation(out=ytok[o][:], in_=ps[bk][:, :], func=AF.Copy), reads=[('ps', bk)], writes=[('ytok', o)])
                    bt = nbank()

                    def g2(e, bt=bt, o=o):
                        for cc in range(4):
                            r = e.transpose(ps[bt][:, cc * 128:(cc + 1) * 128], ytok[o][:, cc * 128:(cc + 1) * 128], ident[:])
                        return r
                    s.op('pe', g2, reads=[('ytok', o), 'ident'], writes=[('ps', bt)])
                    s.op('dve', lambda e, bt=bt, g4=g4, q4=q4: e.tensor_tensor(out=ygst[g4][:, :, q4 * 128:(q4 + 1) * 128], in0=ps[bt][:, :].rearrange("p (c t) -> p c t", t=128),
                                                                                in1=gq[g4][:, :, q4 * 128:(q4 + 1) * 128], op=ALU.mult),
                         reads=[('ps', bt), ('gq', g4)], writes=[('ygst', g4, q4)])
                    if q4 == 3:
                        cols_ = slice(b * L + (bp // 4) * 512, b * L + (bp // 4) * 512 + 512)
                        s.dma('pool', YGv[:, h2 * 4:(h2 + 1) * 4, cols_], ygst[g4][:], ds_yg[g4], reads=[('ygst', g4, q) for q in range(4)],
                              writes=[('YG', b * TBS + bp // 4, h2)])
            lp.close()

        def mix_outproj(l, wname, widx, bias_name, bias_idx, ygkeys_fn, norm_after=None):
            lp = Pool_()
            wo = lp.t(uname('wo'), [128, 8, D], BF16)
            load_w(lambda kc, c0, n: wo[:, kc, c0:c0 + n], W[wname][widx], D, D, 'wo')
            wk = wkeys('wo', D, D)
            xin = [lp.t(uname('xin'), [128, 8, 512], F32) for _ in range(2)]
            ds_xin = [s.dsem(uname('xin')) for _ in range(2)]
            ygi = [lp.t(uname('ygi'), [128, 8, 512], BF16) for _ in range(2)]
            ds_ygi = [s.dsem(uname('ygi')) for _ in range(2)]
            ds_xo = [s.dsem(uname('xo')) for _ in range(2)]
            for tb in range(NTB):
                xs = tb % 2
                s.dma('sp', xin[xs][:], XTv[:, :, tb * 512:(tb + 1) * 512], ds_xin[xs], reads=[('XT', tb)], writes=[('xin', xs, n) for n in range(8)])
                s.dma('sp', ygi[xs][:], YGv[:, :, tb * 512:(tb + 1) * 512], ds_ygi[xs], reads=ygkeys_fn(tb), writes=[('ygi', xs)])
                for n in range(8):
                    bk = nbank()

                    def g(e, bk=bk, n=n, xs=xs):
                        for k in range(8):
                            r = e.matmul(ps[bk][:, :], lhsT=wo[:, k, n * 128:(n + 1) * 128], rhs=ygi[xs][:, k, :], start=(k == 0), stop=(k == 7))
                        return r
                    s.op('pe', g, reads=[('ygi', xs)] + wk, writes=[('ps', bk)])
                    if bias_name is not None:
                        s.op('dve', lambda e, bk=bk, n=n, xs=xs: e.scalar_tensor_tensor(out=xin[xs][:, n, :], in0=ps[bk][:, :], scalar=col(bias_name, bias_idx, n), in1=xin[xs][:, n, :],
                                                                                       op0=ALU.add, op1=ALU.add), reads=[('ps', bk), ('xin', xs, n), 'colsT'], writes=[('xin', xs, n)])
                    else:
                        s.op('dve', lambda e, bk=bk, n=n, xs=xs: e.tensor_tensor(out=xin[xs][:, n, :], in0=ps[bk][:, :], in1=xin[xs][:, n, :], op=ALU.add),
                             reads=[('ps', bk), ('xin', xs, n)], writes=[('xin', xs, n)])
                s.dma('pool', XTv[:, :, tb * 512:(tb + 1) * 512], xin[xs][:], ds_xo[xs], reads=[('xin', xs, n) for n in range(8)], writes=[('XT', tb)])
            lp.close()

        def stage_hyena(l, j):
            hy_filter(j)
            hy_spectrum(j)
            hy_inproj(l, j)
            hy_conv(l, j)
            mix_outproj(l, 'hy_w_out', j, 'hy_b_out', j, lambda tb: [('YG', tb, 0), ('YG', tb, 1)])

        P_R, P_K, P_V, P_A, P_SF, P_SB, P_SV = range(7)

        def rw_proj(l, j):
            vres = (j > 0)
            lp = Pool_()
            lpd = {'sq': lp.t(uname('sq'), [128, 8, 514], BF16), 'rs': lp.t(uname('rs'), [128, 514], F32)}
            wr = lp.t(uname('wr'), [128, 8, D], BF16)
            wk_ = lp.t(uname('wk'), [128, 8, D], BF16)
            wv = lp.t(uname('wv'), [128, 8, D], BF16)
            w1 = [lp.t(uname('w1'), [128, 8, 64], BF16) for _ in range(2)]
            a1 = lp.t(uname('a1'), [128, 8, 64], BF16)
            g1 = lp.t(uname('g1'), [128, 8, 128], BF16)
            w2 = [lp.t(uname('w2'), [64, D], BF16) for _ in range(2)]
            a2 = lp.t(uname('a2'), [64, D], BF16)
            g2 = lp.t(uname('g2'), [128, D], BF16)
            load_w(lambda kc, c0, n: wr[:, kc, c0:c0 + n], W['rw_w_r'][j], D, D, 'wr')
            load_w(lambda kc, c0, n: wk_[:, kc, c0:c0 + n], W['rw_w_k'][j], D, D, 'wk')
            load_w(lambda kc, c0, n: wv[:, kc, c0:c0 + n], W['rw_w_v'][j], D, D, 'wv')
            for d in range(2):
                load_w(lambda kc, c0, n, d=d: w1[d][:, kc, 0:n], W['rw_w1'][j, d], D, 64, ('w1', d))
                load_w(lambda kc, c0, n, d=d: w2[d][0:64, c0:c0 + n], W['rw_w2'][j, d], 64, D, ('w2', d))
            load_w(lambda kc, c0, n: a1[:, kc, 0:n], W['rw_a1'][j], D, 64, 'a1')
            load_w(lambda kc, c0, n: a2[0:64, c0:c0 + n], W['rw_a2'][j], 64, D, 'a2')
            load_w(lambda kc, c0, n: g1[:, kc, 0:n], W['rw_g1'][j], D, 128, 'g1')
            load_w(lambda kc, c0, n: g2[:, c0:c0 + n], W['rw_g2'][j], 128, D, 'g2')
            if vres:
                v1 = lp.t(uname('v1'), [128, 8, 32], BF16)
                v2 = lp.t(uname('v2'), [32, D], BF16)
                load_w(lambda kc, c0, n: v1[:, kc, 0:n], W['rw_v1'][j - 1], D, 32, 'v1')
                load_w(lambda kc, c0, n: v2[0:32, c0:c0 + n], W['rw_v2'][j - 1], 32, D, 'v2')
            xh = lp.t(uname('xh'), [128, 8, 514], F32)
            ds_xh = s.dsem(uname('xh'))
            xx = lp.t(uname('xx'), [128, 8, 512], F32)
            xj = [lp.t(uname('xj'), [128, 8, 512], BF16) for _ in range(2)]
            lo_t = {nm: lp.t(uname(nm), [128, 512], BF16) for nm in ('twf', 'twb', 'ta', 'tv', 'tg')}
            ost = [lp.t(uname('ost'), [128, 512], F32) for _ in range(4)]
            ds_ost = [s.dsem(uname('ost')) for _ in range(4)]
            ds_vf = [s.dsem(uname('vfs')) for _ in range(4)]
            gst = [lp.t(uname('gst'), [128, 512], BF16) for _ in range(2)]
            ds_gst = [s.dsem(uname('gst')) for _ in range(2)]
            oc = [0]
            gc = [0]
            xc = [0]

            def make_xj(m, hk):
                sl = xc[0] % 2
                xc[0] += 1
                for k in range(8):
                    s.op('dve', lambda e, k=k, sl=sl: e.scalar_tensor_tensor(out=xj[sl][:, k, :], in0=xx[:, k, :], scalar=col('rw_mu', j * 6 + m, k), in1=xh[:, k, 1:513],
                                                                            op0=ALU.mult, op1=ALU.add), reads=['xx', 'colsT'] + hk, writes=[('xj', sl, k)])
                return sl, [('xj', sl, k) for k in range(8)]

            def proj(wt_fn, wkeys_, sl, xk, M, bank_rows=None):
                bk = nbank()

                def g(e, bk=bk):
                    for k in range(8):
                        r = e.matmul(ps[bk][0:M, :], lhsT=wt_fn(k), rhs=xj[sl][:, k, :], start=(k == 0), stop=(k == 7))
                    return r
                s.op('pe', g, reads=xk + wkeys_, writes=[('ps', bk)])
                return bk

            def store_f32(bk, pidx, n, tb, func=None, bias=None):
                o = oc[0] % 4
                oc[0] += 1
                if func is None:
                    en = evac_eng()
                    s.op(en, copy_op(en, ost[o][:], ps[bk][:, :]), reads=[('ps', bk)], writes=[('ost', o)])
                else:
                    if bias is None:
                        s.op('act', lambda e, bk=bk, o=o: e.activation(out=ost[o][:], in_=ps[bk][:, :], func=func), reads=[('ps', bk)], writes=[('ost', o)])
                    else:
                        s.op('act', lambda e, bk=bk, o=o: e.activation(out=ost[o][:], in_=ps[bk][:, :], func=func, bias=bias), reads=[('ps', bk), 'colsT'], writes=[('ost', o)])
                s.dma('pool', PRJv[pidx][:, n, tb * 512:(tb + 1) * 512], ost[o][:], ds_ost[o], reads=[('ost', o)], writes=[('PRJ', pidx, tb, n)])
                return o

            for tb in range(NTB):
                b, t0 = tb // TBS, (tb % TBS) * 512
                lo = 1 if t0 == 0 else 0
                hi = 513 if t0 + 512 == L else 514
                nbr = [tb] + ([tb - 1] if lo == 0 else []) + ([tb + 1] if hi == 514 else [])
                if lo == 1:
                    s.op('pool', lambda e: e.memset(xh[:, :, 0:1], 0.0), writes=['xhl'])
                if hi == 513:
                    s.op('pool', lambda e: e.memset(xh[:, :, 513:514], 0.0), writes=['xhl'])
                c0 = tb * 512 - 1
                s.dma('sp', xh[:, :, lo:hi], XTv[:, :, c0 + lo:c0 + hi], ds_xh, reads=[('XT', t_) for t_ in nbr], writes=['xhl'] + [('xh', k) for k in range(8)])
                CUT = int(os.environ.get('CUT', '99'))
                if CUT <= 0:
                    continue
                hk = rmsnorm(lpd, xh, 'xhl', 514, 'norm_mix_g', l, xh, 'xh', 'rw')
                if CUT <= 1:
                    continue
                s.op('dve', lambda e: e.tensor_tensor(out=xx[:], in0=xh[:, :, 0:512], in1=xh[:, :, 2:514], op=ALU.add), reads=hk, writes=['xx'])
                s.op('dve', lambda e: e.scalar_tensor_tensor(out=xx[:], in0=xx[:], scalar=0.5, in1=xh[:, :, 1:513], op0=ALU.mult, op1=ALU.subtract), reads=hk + ['xx'], writes=['xx'])
                if CUT <= 2:
                    continue
                sl, xk = make_xj(1, hk)
                for d, nm in ((0, 'twf'), (1, 'twb')):
                    bk = proj(lambda k, d=d: w1[d][:, k, :], wkeys(('w1', d), D, 64), sl, xk, 64)
                    s.op('act', lambda e, bk=bk, nm=nm: e.activation(out=lo_t[nm][0:64, :], in_=ps[bk][0:64, :], func=AF.Tanh), reads=[('ps', bk)], writes=[nm])
                for d, nm, pidx in ((0, 'twf', P_SF), (1, 'twb', P_SB)):
                    for n in range(8):
                        bk = nbank()
                        s.op('pe', lambda e, bk=bk, d=d, n=n, nm=nm: e.matmul(ps[bk][:, :], lhsT=w2[d][0:64, n * 128:(n + 1) * 128], rhs=lo_t[nm][0:64, :], start=True, stop=True),
                             reads=[nm] + wkeys(('w2', d), 64, D), writes=[('ps', bk)])
                        store_f32(bk, pidx, n, tb, AF.Sigmoid, col('rw_w0', j * 2 + d, n))
                if CUT <= 3:
                    continue
                sl, xk = make_xj(4, hk)
                bk = proj(lambda k: a1[:, k, :], wkeys('a1', D, 64), sl, xk, 64)
                s.op('act', lambda e, bk=bk: e.activation(out=lo_t['ta'][0:64, :], in_=ps[bk][0:64, :], func=AF.Copy), reads=[('ps', bk)], writes=['ta'])
                for n in range(8):
                    bk = nbank()
                    s.op('pe', lambda e, bk=bk, n=n: e.matmul(ps[bk][:, :], lhsT=a2[0:64, n * 128:(n + 1) * 128], rhs=lo_t['ta'][0:64, :], start=True, stop=True),
                         reads=['ta'] + wkeys('a2', 64, D), writes=[('ps', bk)])
                    store_f32(bk, P_A, n, tb, AF.Sigmoid, col('rw_a0', j, n))
                if CUT <= 4:
                    continue
                sl, xk = make_xj(5, hk)
                bk = proj(lambda k: g1[:, k, :], wkeys('g1', D, 128), sl, xk, 128)
                s.op('act', lambda e, bk=bk: e.activation(out=lo_t['tg'][:], in_=ps[bk][:, :], func=AF.Sigmoid), reads=[('ps', bk)], writes=['tg'])
                for n in range(8):
                    bk = nbank()
                    s.op('pe', lambda e, bk=bk, n=n: e.matmul(ps[bk][:, :], lhsT=g2[:, n * 128:(n + 1) * 128], rhs=lo_t['tg'][:], start=True, stop=True),
                         reads=['tg'] + wkeys('g2', 128, D), writes=[('ps', bk)])
                    o = gc[0] % 2
                    gc[0] += 1
                    en = evac_eng()
                    s.op(en, copy_op(en, gst[o][:], ps[bk][:, :]), reads=[('ps', bk)], writes=[('gst', o)])
                    s.dma('pool', G2v[:, n, tb * 512:(tb + 1) * 512], gst[o][:], ds_gst[o], reads=[('gst', o)], writes=[('G2', tb, n)])
                if CUT <= 5:
                    continue
                for (m, wt, wkey, pidx) in ((0, wr, 'wr', P_R), (2, wk_, 'wk', P_K), (3, wv, 'wv', P_V))[:int(os.environ.get('CUTM', '3'))]:
                    sl, xk = make_xj(m, hk)
                    if m == 3 and vres:
                        bk = proj(lambda k: v1[:, k, :], wkeys('v1', D, 32), sl, xk, 32)
                        s.op('act', lambda e, bk=bk: e.activation(out=lo_t['tv'][0:32, :], in_=ps[bk][0:32, :], func=AF.Copy), reads=[('ps', bk)], writes=['tv'])
                        for n in range(8):
                            bk = nbank()
                            s.op('pe', lambda e, bk=bk, n=n: e.matmul(ps[bk][:, :], lhsT=v2[0:32, n * 128:(n + 1) * 128], rhs=lo_t['tv'][0:32, :], start=True, stop=True),
                                 reads=['tv'] + wkeys('v2', 32, D), writes=[('ps', bk)])
                            store_f32(bk, P_SV, n, tb, AF.Sigmoid, col('rw_v0', j - 1, n))
                    for n in range(8):
                        bk = proj(lambda k, wt=wt, n=n: wt[:, k, n * 128:(n + 1) * 128], wkeys(wkey, D, D), sl, xk, 128)
                        o = store_f32(bk, pidx, n, tb)
                        if m == 3 and not vres:
                            s.dma('act', VFv[:, n, tb * 512:(tb + 1) * 512], ost[o][:], ds_vf[o], reads=[('ost', o)], writes=[('VF', tb, n)])
            lp.close()

        def rw_chunk(l, j):
            vres = (j > 0)
            lp = Pool_()
            mA = lp.t(uname('mA'), [128, 2, 4, 64], F32)
            mB = lp.t(uname('mB'), [128, 2, 4, 128], F32)
            mC = lp.t(uname('mC'), [128, 2, 4, 128], F32)
            bones = lp.t(uname('bones'), [128, 128], BF16)
            segm = lp.t(uname('segm'), [128, 2, 512], F32)
            idr = lp.t(uname('idr'), [128, 8, 64], F32)
            ds = s.dsem(uname('rwc'))
            s.dma('sp', mA[:], CD['mA'].rearrange("p (d c k) -> p d c k", d=2, c=4), ds, writes=['mA'])
            s.dma('sp', mB[:], CD['mB'].rearrange("p (d c k) -> p d c k", d=2, c=4), ds, writes=['mB'])
            s.dma('sp', mC[:], CD['mC'].rearrange("p (d c k) -> p d c k", d=2, c=4), ds, writes=['mC'])
            s.dma('sp', bones[:], CD['bones'][:, :], ds, writes=['bones'])
            s.dma('sp', segm[:], CD['segm'].rearrange("p (d t) -> p d t", d=2), ds, writes=['segm'])
            s.dma('sp', idr[:], CD['identrep'].rearrange("p (c k) -> p c k", c=8), ds, writes=['idr'])
            NIN = 8 if vres else 6
            inp_ = [lp.t(uname('rin'), [128, NIN, 512], F32) for _ in range(2)]
            ds_in = [s.dsem(uname('rin')) for _ in range(2)]
            I_R, I_K, I_V, I_A, I_SF, I_SB, I_SV, I_VF = range(8)
            tf = {nm: [lp.t(uname(nm), [128, 512], F32) for _ in range(2)] for nm in ('kk', 'rn', 'kkn', 'fac', 'kmod', 'bb')}
            td = {nm: lp.t(uname(nm), [128, 512], F32) for nm in ('lw', 'cin', 'cex', 'e1', 'e2', 'e3', 'e4')}
            kk2 = lp.t(uname('kk2'), [128, 512], BF16)
            rkb = lp.t(uname('rkb'), [128, 512], BF16)
            wc = [lp.t(uname('wc'), [128, 2, 8], F32) for _ in range(2)]
            LR = [lp.t(uname('LR'), [128, 2, 8, 128], BF16) for _ in range(2)]
            BK = [lp.t(uname('BK'), [128, 2, 8, 128], BF16) for _ in range(2)]
            Bh = [lp.t(uname('Bh'), [128, 2, 512], BF16) for _ in range(2)]
            Kh = [lp.t(uname('Kh'), [128, 2, 512], BF16) for _ in range(2)]
            vb = [lp.t(uname('vb'), [128, 512], BF16) for _ in range(2)]
            Vt = lp.t(uname('Vt'), [128, 4, 128], BF16)
            Bht = lp.t(uname('Bht'), [128, 2, 4, 128], BF16)
            Kht = lp.t(uname('Kht'), [128, 2, 4, 128], BF16)
            Zt = lp.t(uname('Zt'), [128, 2, 2, 4, 128], BF16)
            XX = lp.t(uname('XX'), [128, 4, 2, 4, 64], BF16)
            MBt = lp.t(uname('MBt'), [128, 4, 4, 128], BF16)
            MCt = lp.t(uname('MCt'), [128, 4, 4, 128], BF16)
            qst = [lp.t(uname('qst'), [128, 2, 8, 64], F32) for _ in range(2)]
            gst_ = [lp.t(uname('gst'), [128, 2, 8, 64], F32) for _ in range(2)]
            hst = [lp.t(uname('hst'), [128, 2, 8, 64], F32) for _ in range(2)]
            yst = [lp.t(uname('yst'), [128, 8, 64], F32) for _ in range(2)]
            dwc = lp.t(uname('dwc'), [128, 8, 64], F32)
            bon = [lp.t(uname('bon'), [128, 512], BF16) for _ in range(2)]
            ds_out = [s.dsem(uname('rwo')) for _ in range(2)]
            it = 0
            pbc = [0]

            def v3(t_):
                return t_[:].rearrange("p (c k) -> p c k", k=64)

            def P1(tb, n, sl):
                if True:
                    X = inp_[sl]
                    cols_ = slice(tb * 512, (tb + 1) * 512)
                    ik = []
                    for q, pidx in enumerate((P_R, P_K, P_V, P_A, P_SF, P_SB) + ((P_SV,) if vres else ())):
                        s.dma('sp', X[:, q, :], PRJv[pidx][:, n, cols_], ds_in[sl], reads=[('PRJ', pidx, tb, n)], writes=[('rin', sl, q)])
                        yield
                        ik.append(('rin', sl, q))
                    if vres:
                        s.dma('sp', X[:, I_VF, :], VFv[:, n, cols_], ds_in[sl], reads=[('VF', tb, n)], writes=[('rin', sl, I_VF)])
                        yield
                        ik.append(('rin', sl, I_VF))
                    r_, k_, v_, a_ = X[:, I_R, :], X[:, I_K, :], X[:, I_V, :], X[:, I_A, :]
                    F = {nm: tf[nm][sl] for nm in tf}
                    fk = lambda nm: (nm, sl)
                    if vres:
                        s.op('pool', lambda e, X=X: e.tensor_tensor(out=X[:, I_VF, :], in0=X[:, I_VF, :], in1=X[:, I_V, :], op=ALU.subtract), reads=ik, writes=[('rin', sl, I_VF)])
                        yield
                        s.op('pool', lambda e, X=X: e.tensor_tensor(out=X[:, I_VF, :], in0=X[:, I_VF, :], in1=X[:, I_SV, :], op=ALU.mult), reads=ik, writes=[('rin', sl, I_VF)])
                        yield
                        s.op('pool', lambda e, X=X: e.tensor_tensor(out=X[:, I_V, :], in0=X[:, I_V, :], in1=X[:, I_VF, :], op=ALU.add), reads=ik, writes=[('rin', sl, I_V)])
                        yield
                    s.op('dve', lambda e, F=F, k_=k_: e.tensor_scalar(out=F['kk'][:], in0=k_, scalar1=col('rw_k_k', j, n), scalar2=None, op0=ALU.mult), reads=ik + ['colsT'], writes=[fk('kk')])
                    yield
                    s.op('pool', lambda e, F=F: e.tensor_tensor(out=kk2[:], in0=F['kk'][:], in1=F['kk'][:], op=ALU.mult), reads=[fk('kk')], writes=['kk2'])
                    yield
                    bk = nbank()
                    s.op('pe', lambda e, bk=bk: e.matmul(ps[bk][:, :], lhsT=bones[:], rhs=kk2[:], start=True, stop=True), reads=['kk2', 'bones'], writes=[('ps', bk)])
                    yield
                    s.op('dve', lambda e, bk=bk, F=F: e.tensor_scalar(out=F['rn'][:], in0=ps[bk][:, :], scalar1=1e-24, scalar2=None, op0=ALU.max), reads=[('ps', bk)], writes=[fk('rn')])
                    yield
                    s.op('act', lambda e, F=F: e.activation(out=F['rn'][:], in_=F['rn'][:], func=AF.Sqrt), reads=[fk('rn')], writes=[fk('rn')])
                    yield
                    s.op('dve', lambda e, F=F: e.reciprocal(out=F['rn'][:], in_=F['rn'][:]), reads=[fk('rn')], writes=[fk('rn')])
                    yield
                    s.op('pool', lambda e, F=F: e.tensor_tensor(out=F['kkn'][:], in0=F['kk'][:], in1=F['rn'][:], op=ALU.mult), reads=[fk('kk'), fk('rn')], writes=[fk('kkn')])
                    yield
                    s.op('dve', lambda e, F=F, a_=a_: e.tensor_scalar(out=F['fac'][:], in0=a_, scalar1=-1.0, scalar2=col('rw_k_a', j, n), op0=ALU.add, op1=ALU.mult),
                         reads=ik + ['colsT'], writes=[fk('fac')])
                    yield
                    s.op('dve', lambda e, F=F, k_=k_: e.scalar_tensor_tensor(out=F['kmod'][:], in0=F['fac'][:], scalar=1.0, in1=k_, op0=ALU.add, op1=ALU.mult),
                         reads=ik + [fk('fac')], writes=[fk('kmod')])
                    yield
                    s.op('pool', lambda e, F=F, a_=a_: e.tensor_tensor(out=F['bb'][:], in0=F['kkn'][:], in1=a_, op=ALU.mult), reads=ik + [fk('kkn')], writes=[fk('bb')])
                    yield
                    s.op('dve', lambda e, F=F, r_=r_: e.scalar_tensor_tensor(out=rkb[:], in0=r_, scalar=col('rw_r_k', j, n), in1=F['kmod'][:], op0=ALU.mult, op1=ALU.mult),
                         reads=ik + [fk('kmod'), 'colsT'], writes=['rkb'])
                    yield
                    bk = nbank()
                    s.op('pe', lambda e, bk=bk: e.matmul(ps[bk][:, :], lhsT=bones[:], rhs=rkb[:], start=True, stop=True), reads=['rkb', 'bones'], writes=[('ps', bk)])
                    yield
                    s.op('dve', lambda e, bk=bk, v_=v_, sl=sl: e.tensor_tensor(out=bon[sl][:], in0=ps[bk][:, :], in1=v_, op=ALU.mult), reads=[('ps', bk)] + ik, writes=[('bon', sl)])
                    yield
                    s.dma('act', BONv[:, n, cols_], bon[sl][:], ds_out[sl], reads=[('bon', sl)], writes=[('BON', tb, n)])
                    yield
                    s.op('act', lambda e, v_=v_, sl=sl: e.activation(out=vb[sl][:], in_=v_, func=AF.Copy), reads=ik, writes=[('vb', sl)])
                    yield
                    for d in range(2):
                        sg_ = X[:, I_SF + d, :]
                        s.op('dve', lambda e, sg_=sg_: e.tensor_scalar(out=td['lw'][:], in0=sg_, scalar1=-0.6065306597126334, scalar2=None, op0=ALU.mult), reads=ik, writes=['lw'])
                        yield
                        if d == 0:
                            s.op('dve', lambda e: e.tensor_tensor_scan(out=td['cin'][:], data0=segm[:, 0, :], data1=td['lw'][:], initial=0.0, op0=ALU.mult, op1=ALU.add),
                                 reads=['lw', 'segm'], writes=['cin'])
                            yield
                            totap = v3(td['cin'])[:, :, 63:64]
                        else:
                            s.op('dve', lambda e: e.tensor_tensor_scan(out=td['cin'][:, ::-1], data0=segm[:, 1, ::-1], data1=td['lw'][:, ::-1], initial=0.0, op0=ALU.mult, op1=ALU.add),
                                 reads=['lw', 'segm'], writes=['cin'])
                            yield
                            totap = v3(td['cin'])[:, :, 0:1]
                        s.op('pool', lambda e: e.tensor_tensor(out=td['cex'][:], in0=td['cin'][:], in1=td['lw'][:], op=ALU.subtract), reads=['cin', 'lw'], writes=['cex'])
                        yield
                        s.op('act', lambda e: e.activation(out=td['e1'][:], in_=td['cex'][:], func=AF.Exp), reads=['cex'], writes=['e1'])
                        yield
                        s.op('act', lambda e: e.activation(out=td['e2'][:], in_=td['cin'][:], func=AF.Exp, scale=-1.0), reads=['cin'], writes=['e2'])
                        yield
                        s.op('act', lambda e: e.activation(out=td['e3'][:], in_=td['cin'][:], func=AF.Exp), reads=['cin'], writes=['e3'])
                        yield
                        s.op('dve', lambda e, totap=totap: e.tensor_tensor(out=v3(td['e4']), in0=totap.to_broadcast([128, 8, 64]), in1=v3(td['cin']), op=ALU.subtract), reads=['cin'], writes=['e4'])
                        yield
                        s.op('act', lambda e: e.activation(out=td['e4'][:], in_=td['e4'][:], func=AF.Exp), reads=['e4'], writes=['e4'])
                        yield
                        s.op('act', lambda e, totap=totap, d=d, sl=sl: e.activation(out=wc[sl][:, d, :].rearrange("p (c o) -> p c o", o=1), in_=totap, func=AF.Exp), reads=['cin'], writes=[('wc', sl, d)])
                        yield
                        s.op('dve', lambda e, F=F, d=d, sl=sl: e.tensor_tensor(out=LR[sl][:, d, :, 0:64], in0=v3(F['kkn']), in1=v3(td['e1']), op=ALU.mult), reads=[fk('kkn'), 'e1'], writes=[('LR', sl, d, 0)])
                        yield
                        s.op('pool', lambda e, d=d, sl=sl, r_=r_: e.tensor_tensor(out=LR[sl][:, d, :, 64:128], in0=r_.rearrange("p (c k) -> p c k", k=64), in1=v3(td['e3']), op=ALU.mult),
                             reads=ik + ['e3'], writes=[('LR', sl, d, 1)])
                        yield
                        s.op('dve', lambda e, F=F, d=d, sl=sl: e.tensor_tensor(out=BK[sl][:, d, :, 0:64], in0=v3(F['bb']), in1=v3(td['e2']), op=ALU.mult), reads=[fk('bb'), 'e2'], writes=[('BK', sl, d, 0)])
                        yield
                        s.op('pool', lambda e, F=F, d=d, sl=sl: e.tensor_tensor(out=BK[sl][:, d, :, 64:128], in0=v3(F['kmod']), in1=v3(td['e2']), op=ALU.mult), reads=[fk('kmod'), 'e2'], writes=[('BK', sl, d, 1)])
                        yield
                        s.op('dve', lambda e, F=F, d=d, sl=sl: e.tensor_tensor(out=Bh[sl][:, d, :], in0=F['bb'][:], in1=td['e4'][:], op=ALU.mult), reads=[fk('bb'), 'e4'], writes=[('Bh', sl, d)])
                        yield
                        s.op('pool', lambda e, F=F, d=d, sl=sl: e.tensor_tensor(out=Kh[sl][:, d, :], in0=F['kmod'][:], in1=td['e4'][:], op=ALU.mult), reads=[fk('kmod'), 'e4'], writes=[('Kh', sl, d)])
                        yield
            def P2M(tb, n, sl, pump):
                if True:
                    X = inp_[sl]
                    cols_ = slice(tb * 512, (tb + 1) * 512)
                    ik = [('rin', sl, q) for q in range(NIN)]
                    r_, k_, v_, a_ = X[:, I_R, :], X[:, I_K, :], X[:, I_V, :], X[:, I_A, :]
                    F = {nm: tf[nm][sl] for nm in tf}
                    fk = lambda nm: (nm, sl)
                    lrk = [('LR', sl, d, q) for d in range(2) for q in range(2)]
                    bkk = [('BK', sl, d, q) for d in range(2) for q in range(2)]

                    def tr128(src_fn, skeys, dst, dkey):
                        hb_ = pbc[0] % 2
                        pbc[0] += 1

                        def g(e, hb_=hb_):
                            for cp in range(4):
                                r = e.transpose(psbs[hb_][:, cp * 128:(cp + 1) * 128], src_fn(cp), identb[:])
                            return r
                        s.op('pe', g, reads=skeys + ['identb'], writes=[('psb', hb_)])
                        en = evac_eng()
                        s.op(en, copy_op(en, dst, psbs[hb_][:, 0:512].rearrange("p (c k) -> p c k", k=128)), reads=[('psb', hb_)], writes=[dkey])
                    tr128(lambda cp: vb[sl][:, cp * 128:(cp + 1) * 128], [('vb', sl)], Vt[:], 'Vt')
                    for d in range(2):
                        tr128(lambda cp, d=d: Bh[sl][:, d, cp * 128:(cp + 1) * 128], [('Bh', sl, d)], Bht[:, d, :, :], ('Bht', d))
                        tr128(lambda cp, d=d: Kh[sl][:, d, cp * 128:(cp + 1) * 128], [('Kh', sl, d)], Kht[:, d, :, :], ('Kht', d))
                        hb_ = pbc[0] % 2
                        pbc[0] += 1

                        def g(e, hb_=hb_, d=d):
                            for c in range(8):
                                par, cp = c % 2, c // 2
                                r = e.transpose(psbs[hb_][64 * par:64 * par + 64, cp * 128:(cp + 1) * 128], LR[sl][:, d, c, 0:64], identb[:])
                            return r
                        s.op('pe', g, reads=lrk + ['identb'], writes=[('psb', hb_)])
                        s.op('dve', lambda e, hb_=hb_, d=d: e.tensor_copy(out=Zt[:, d, :, :, 0:64], in_=psbs[hb_][:, 0:512].rearrange("p (c h k) -> p h c k", c=4, h=2)),
                             reads=[('psb', hb_)], writes=[('Zt', d, 0, 0), ('Zt', d, 1, 0)])
                    for d in range(2):
                        for hh in range(2):
                            cb = d * 2 + hh
                            hp = 64 * hh
                            bA, bB, bC = nbank(), nbank(), nbank()

                            def g(e, d=d, hp=hp, bA=bA, bB=bB, bC=bC):
                                for c in range(8):
                                    par, cp = c % 2, c // 2
                                    po = 64 * par
                                    e.matmul(ps[bA][po:po + 64, cp * 64:(cp + 1) * 64], lhsT=LR[sl][hp:hp + 64, d, c, 0:64], rhs=BK[sl][hp:hp + 64, d, c, 0:64], start=True, stop=True)
                                    e.matmul(ps[bB][po:po + 64, cp * 128:(cp + 1) * 128], lhsT=BK[sl][hp:hp + 64, d, c, 0:64], rhs=LR[sl][hp:hp + 64, d, c, :], start=True, stop=True)
                                    r = e.matmul(ps[bC][po:po + 64, cp * 128:(cp + 1) * 128], lhsT=BK[sl][hp:hp + 64, d, c, 64:128], rhs=LR[sl][hp:hp + 64, d, c, :], start=True, stop=True)
                                return r
                            s.op('pe', g, reads=lrk + bkk, writes=[('ps', bA), ('ps', bB), ('ps', bC)])
                            s.op('dve', lambda e, bA=bA, cb=cb, d=d: e.tensor_tensor(out=XX[:, cb, 0, :, :], in0=ps[bA][:, 0:256].rearrange("p (c k) -> p c k", k=64), in1=mA[:, d, :, :], op=ALU.mult),
                                 reads=[('ps', bA), 'mA'], writes=[('XX', cb)])
                            s.op('dve', lambda e, bB=bB, cb=cb, d=d: e.tensor_tensor(out=MBt[:, cb, :, :], in0=ps[bB][:, :].rearrange("p (c k) -> p c k", k=128), in1=mB[:, d, :, :], op=ALU.mult),
                                 reads=[('ps', bB), 'mB'], writes=[('MBt', cb)])
                            s.op('dve', lambda e, bC=bC, cb=cb, d=d: e.tensor_tensor(out=MCt[:, cb, :, :], in0=ps[bC][:, :].rearrange("p (c k) -> p c k", k=128), in1=mC[:, d, :, :], op=ALU.mult),
                                 reads=[('ps', bC), 'mC'], writes=[('MCt', cb)])
                            bD = nbank()

                            def g(e, cb=cb, hh=hh, bD=bD):
                                for c in range(8):
                                    par, cp = c % 2, c // 2
                                    po = 64 * par
                                    r = e.matmul(ps[bD][po:po + 64, cp * 64:(cp + 1) * 64], lhsT=MCt[po:po + 64, cb, cp, 0:64], rhs=Vt[po:po + 64, cp, hh * 64:(hh + 1) * 64], start=True, stop=True)
                                return r
                            s.op('pe', g, reads=[('MCt', cb), 'Vt'], writes=[('ps', bD)])
                            s.op('act', lambda e, bD=bD, d=d, hh=hh: e.activation(out=Zt[:, d, hh, :, 64:128], in_=ps[bD][:, 0:256].rearrange("p (c k) -> p c k", k=64), func=AF.Copy, scale=-1.0),
                                 reads=[('ps', bD)], writes=[('Zt', d, hh, 1)])
                            pump()
                    for lv in range(6):
                        for d in range(2):
                            for hh in range(2):
                                cb = d * 2 + hh
                                zk = [('Zt', d, hh, 0), ('Zt', d, hh, 1)]
                                bX, bZ = nbank(), nbank()

                                def g(e, cb=cb, d=d, hh=hh, bX=bX, bZ=bZ, lv=lv):
                                    for c in range(8):
                                        par, cp = c % 2, c // 2
                                        po = 64 * par
                                        Xc = XX[po:po + 64, cb, 0, cp, :]
                                        XTc = MBt[po:po + 64, cb, cp, 0:64] if lv == 0 else XX[po:po + 64, cb, 1, cp, :]
                                        if lv < 5:
                                            e.matmul(ps[bX][po:po + 64, cp * 64:(cp + 1) * 64], lhsT=XTc, rhs=Xc, start=True, stop=True)
                                            e.matmul(ps[bX][po:po + 64, 256 + cp * 64:256 + (cp + 1) * 64], lhsT=Xc, rhs=XTc, start=True, stop=True)
                                        r = e.matmul(ps[bZ][po:po + 64, cp * 128:(cp + 1) * 128], lhsT=XTc, rhs=Zt[po:po + 64, d, hh, cp, :], start=True, stop=True)
                                    return r
                                s.op('pe', g, reads=[('XX', cb), ('MBt', cb)] + zk, writes=[('ps', bX), ('ps', bZ)])
                                if lv < 5:
                                    s.op('act', lambda e, bX=bX, cb=cb: e.activation(out=XX[:, cb, :, :, :], in_=ps[bX][:, :].rearrange("p (x c k) -> p x c k", x=2, c=4), func=AF.Copy),
                                         reads=[('ps', bX)], writes=[('XX', cb)])
                                s.op('dve', lambda e, bZ=bZ, d=d, hh=hh: e.tensor_tensor(out=Zt[:, d, hh, :, :], in0=ps[bZ][:, :].rearrange("p (c k) -> p c k", k=128), in1=Zt[:, d, hh, :, :], op=ALU.add),
                                     reads=[('ps', bZ)] + zk, writes=zk)
                                pump()
                    allz = [('Zt', d, hh, q) for d in range(2) for hh in range(2) for q in range(2)]
                    allm = [('MBt', cb) for cb in range(4)] + [('MCt', cb) for cb in range(4)]
                    def pv(bank):
                        return ps[bank][:, 0:256].rearrange("p (c k) -> p c k", k=64)
                    for d in range(2):
                        bQ, bG, bH = (nbank(), nbank()), (nbank(), nbank()), (nbank(), nbank())

                        def g(e, d=d, bQ=bQ, bG=bG, bH=bH):
                            for par in range(2):
                                po = 64 * par
                                for hh in range(2):
                                    cb = d * 2 + hh
                                    hp = 64 * hh
                                    for cp in range(4):
                                        KKh = Zt[po:po + 64, d, hh, cp, 0:64]
                                        NU = Zt[po:po + 64, d, hh, cp, 64:128]
                                        e.matmul(ps[bQ[par]][hp:hp + 64, cp * 64:(cp + 1) * 64], lhsT=KKh, rhs=MBt[po:po + 64, cb, cp, 64:128], start=True, stop=True)
                                        e.matmul(ps[bG[par]][hp:hp + 64, cp * 64:(cp + 1) * 64], lhsT=KKh, rhs=Bht[po:po + 64, d, cp, hh * 64:(hh + 1) * 64], start=True, stop=True)
                                        e.matmul(ps[bH[par]][hp:hp + 64, cp * 64:(cp + 1) * 64], lhsT=Kht[po:po + 64, d, cp, hh * 64:(hh + 1) * 64], rhs=Vt[po:po + 64, cp, hh * 64:(hh + 1) * 64], start=True, stop=False)
                                        r = e.matmul(ps[bH[par]][hp:hp + 64, cp * 64:(cp + 1) * 64], lhsT=Bht[po:po + 64, d, cp, hh * 64:(hh + 1) * 64], rhs=NU, start=False, stop=True)
                            return r
                        s.op('pe', g, reads=allz + allm + ['Vt', ('Bht', d), ('Kht', d)], writes=[('ps', x) for x in bQ + bG + bH])
                        s.op('dve', lambda e, d=d: e.tensor_tensor(out=dwc[:], in0=idr[:], in1=wc[sl][:, d, :].rearrange("p (c o) -> p c o", o=1).to_broadcast([128, 8, 64]), op=ALU.mult),
                             reads=['idr', ('wc', sl, d)], writes=['dwc'])
                        for par in range(2):
                            s.op('dve', lambda e, bQ=bQ, d=d, par=par: e.scalar_tensor_tensor(out=qst[sl][:, d, par::2, :], in0=pv(bQ[par]), scalar=-1.0, in1=LR[sl][:, d, par::2, 64:128], op0=ALU.mult, op1=ALU.add),
                                 reads=[('ps', bQ[par])] + lrk, writes=[('qst', sl, d, par)])
                            s.op('dve', lambda e, bG=bG, d=d, par=par: e.scalar_tensor_tensor(out=gst_[sl][:, d, par::2, :], in0=pv(bG[par]), scalar=-1.0, in1=dwc[:, par::2, :], op0=ALU.mult, op1=ALU.add),
                                 reads=[('ps', bG[par]), 'dwc'], writes=[('gstq', sl, d, par)])
                            s.op('act', lambda e, bH=bH, d=d, par=par: e.activation(out=hst[sl][:, d, par::2, :], in_=pv(bH[par]), func=AF.Copy),
                                 reads=[('ps', bH[par])], writes=[('hst', sl, d, par)])
                        s.dma('act', QTv[d][:, n, cols_], qst[sl][:, d, :, :].rearrange("p c k -> p (c k)"), ds_out[sl], reads=[('qst', sl, d, 0), ('qst', sl, d, 1)], writes=[('QT', d, tb, n)])
                        s.dma('act', GSv[d][:, n, cols_], gst_[sl][:, d, :, :].rearrange("p c k -> p (c k)"), ds_out[sl], reads=[('gstq', sl, d, 0), ('gstq', sl, d, 1)], writes=[('GS', d, tb, n)])
                        s.dma('act', HSv[d][:, n, cols_], hst[sl][:, d, :, :].rearrange("p c k -> p (c k)"), ds_out[sl], reads=[('hst', sl, d, 0), ('hst', sl, d, 1)], writes=[('HS', d, tb, n)])
                    bY = (nbank(), nbank())

                    def g(e, bY=bY):
                        for par in range(2):
                            po = 64 * par
                            for hh in range(2):
                                hp = 64 * hh
                                for cp in range(4):
                                    for d in range(2):
                                        cb = d * 2 + hh
                                        e.matmul(ps[bY[par]][hp:hp + 64, cp * 64:(cp + 1) * 64], lhsT=Vt[po:po + 64, cp, hh * 64:(hh + 1) * 64], rhs=MCt[po:po + 64, cb, cp, 64:128], start=(d == 0), stop=False)
                                        r = e.matmul(ps[bY[par]][hp:hp + 64, cp * 64:(cp + 1) * 64], lhsT=Zt[po:po + 64, d, hh, cp, 64:128], rhs=MBt[po:po + 64, cb, cp, 64:128], start=False, stop=(d == 1))
                        return r
                    s.op('pe', g, reads=allz + allm + ['Vt'], writes=[('ps', bY[0]), ('ps', bY[1])])
                    for par in range(2):
                        s.op('act', lambda e, bY=bY, par=par: e.activation(out=yst[sl][:, par::2, :], in_=pv(bY[par]), func=AF.Copy), reads=[('ps', bY[par])], writes=[('yst', sl, par)])
                    s.dma('act', YLv[:, n, cols_], yst[sl][:].rearrange("p c k -> p (c k)"), ds_out[sl], reads=[('yst', sl, 0), ('yst', sl, 1)], writes=[('YL', tb, n)])
            items = [(tb_, n_) for tb_ in range(NTB) for n_ in range(8)]
            SENT = object()

            def drain(g_):
                for _ in g_:
                    pass
            drain(P1(items[0][0], items[0][1], 0))
            for i_, (tb_, n_) in enumerate(items):
                sl_ = i_ % 2
                nxt = P1(items[i_ + 1][0], items[i_ + 1][1], 1 - sl_) if i_ + 1 < len(items) else iter(())

                def pump(k_=3, nxt=nxt):
                    for _ in range(k_):
                        if next(nxt, SENT) is SENT:
                            break
                P2M(tb_, n_, sl_, pump)
                drain(nxt)
            lp.close()

        def rw_scan(l, j):
            lp = Pool_()
            NS = 2
            qt = [lp.t(uname('qt'), [128, 8, 512], F32) for _ in range(NS)]
            gs = [lp.t(uname('gs'), [128, 8, 512], F32) for _ in range(NS)]
            hs = [lp.t(uname('hs'), [128, 8, 512], F32) for _ in range(NS)]
            ds_q = [s.dsem(uname('qt')) for _ in range(NS)]
            ys = [lp.t(uname('ys'), [128, 8, 512], F32) for _ in range(2)]
            ds_ys = [s.dsem(uname('ys')) for _ in range(2)]
            ST = {(b, d): [lp.t(uname('ST'), [128, 8, 64], F32) for _ in range(2)] for b in range(NB) for d in range(2)}
            stc = {}
            for b in range(NB):
                for d in range(2):
                    s.op('pool', lambda e, b=b, d=d: e.memset(ST[(b, d)][0][:], 0.0), writes=[('ST', b, d, 0)])
                    stc[(b, d)] = 0
            lc = 0
            yc = 0
            for it in range(TBS):
                for b in range(NB):
                    ctx = []
                    for d in range(2):
                        blk = it if d == 0 else TBS - 1 - it
                        tb = b * TBS + blk
                        cols_ = slice(tb * 512, (tb + 1) * 512)
                        sl = d
                        s.dma('sp', qt[sl][:], QTv[d][:, :, cols_], ds_q[sl], reads=[('QT', d, tb, n) for n in range(8)], writes=[('qt', sl)])
                        s.dma('sp', gs[sl][:], GSv[d][:, :, cols_], ds_q[sl], reads=[('GS', d, tb, n) for n in range(8)], writes=[('gs', sl)])
                        s.dma('sp', hs[sl][:], HSv[d][:, :, cols_], ds_q[sl], reads=[('HS', d, tb, n) for n in range(8)], writes=[('hs', sl)])
                        ctx.append((tb, cols_))
                    for step in range(8):
                        for d in range(2):
                            sl = d
                            yo = d
                            c = step if d == 0 else 7 - step
                            cur = stc[(b, d)]
                            nxt = 1 - cur
                            Sc, Sn = ST[(b, d)][cur], ST[(b, d)][nxt]
                            bY, bS = nbank(), nbank()

                            def g(e, bY=bY, Sc=Sc, c=c, sl=sl):
                                for n in range(8):
                                    for hh in range(2):
                                        hp = 64 * hh
                                        r = e.matmul(ps[bY][hp:hp + 64, n * 64:(n + 1) * 64], lhsT=Sc[hp:hp + 64, n, :], rhs=qt[sl][hp:hp + 64, n, c * 64:(c + 1) * 64], start=True, stop=True)
                                return r
                            s.op('pe', g, reads=[('ST', b, d, cur), ('qt', sl)], writes=[('ps', bY)])
                            s.op('act', lambda e, bY=bY, yo=yo, c=c: e.activation(out=ys[yo][:, :, c * 64:(c + 1) * 64], in_=ps[bY][:, :].rearrange("p (n k) -> p n k", k=64), func=AF.Copy),
                                 reads=[('ps', bY)], writes=[('ys', yo, c)])

                            def g2(e, bS=bS, Sc=Sc, c=c, sl=sl):
                                for n in range(8):
                                    for hh in range(2):
                                        hp = 64 * hh
                                        r = e.matmul(ps[bS][hp:hp + 64, n * 64:(n + 1) * 64], lhsT=gs[sl][hp:hp + 64, n, c * 64:(c + 1) * 64], rhs=Sc[hp:hp + 64, n, :], start=True, stop=True)
                                return r
                            s.op('pe', g2, reads=[('ST', b, d, cur), ('gs', sl)], writes=[('ps', bS)])
                            s.op('dve', lambda e, bS=bS, Sn=Sn, c=c, sl=sl: e.tensor_tensor(out=Sn[:], in0=ps[bS][:, :].rearrange("p (n k) -> p n k", k=64), in1=hs[sl][:, :, c * 64:(c + 1) * 64], op=ALU.add),
                                 reads=[('ps', bS), ('hs', sl)], writes=[('ST', b, d, nxt)])
                            stc[(b, d)] = nxt
                    for d in range(2):
                        tb, cols_ = ctx[d]
                        s.dma('pool', YSv[d][:, :, cols_], ys[d][:], ds_ys[d], reads=[('ys', d, c) for c in range(8)], writes=[('YS', d, tb)])
            lp.close()

        def rw_post(l, j):
            lp = Pool_()
            bo64 = lp.t(uname('bo64'), [128, 128], F32)
            ds = s.dsem(uname('rwp'))
            s.dma('sp', bo64[:], CD['bo64'][:, :], ds, writes=['bo64'])
            yin = [lp.t(uname('yin'), [128, 3, 512], F32) for _ in range(4)]
            bg = [lp.t(uname('bg'), [128, 2, 512], BF16) for _ in range(4)]
            ds_in = [s.dsem(uname('yin')) for _ in range(4)]
            tt_ = {nm: [lp.t(uname(nm), [128, 512], F32) for _ in range(4)] for nm in ('y', 'sq', 'mean', 'var', 'dd')}
            ygo = [lp.t(uname('ygo'), [128, 512], BF16) for _ in range(4)]
            ds_o = [s.dsem(uname('ygo')) for _ in range(4)]
            def body(tb, n, sl):
                cols_ = slice(tb * 512, (tb + 1) * 512)
                if True:
                    Y = yin[sl]
                    s.dma('sp', Y[:, 0, :], YLv[:, n, cols_], ds_in[sl], reads=[('YL', tb, n)], writes=[('yin', sl, 0)])
                    yield
                    s.dma('sp', Y[:, 1, :], YSv[0][:, n, cols_], ds_in[sl], reads=[('YS', 0, tb)], writes=[('yin', sl, 1)])
                    yield
                    s.dma('sp', Y[:, 2, :], YSv[1][:, n, cols_], ds_in[sl], reads=[('YS', 1, tb)], writes=[('yin', sl, 2)])
                    yield
                    s.dma('sp', bg[sl][:, 0, :], BONv[:, n, cols_], ds_in[sl], reads=[('BON', tb, n)], writes=[('bg', sl, 0)])
                    yield
                    s.dma('sp', bg[sl][:, 1, :], G2v[:, n, cols_], ds_in[sl], reads=[('G2', tb, n)], writes=[('bg', sl, 1)])
                    yield
                    yk = [('yin', sl, q) for q in range(3)]
                    Tt = {nm: tt_[nm][sl] for nm in tt_}
                    tk = lambda nm: (nm, sl)
                    s.op('dve', lambda e, Y=Y, Tt=Tt: e.tensor_tensor(out=Tt['y'][:], in0=Y[:, 0, :], in1=Y[:, 1, :], op=ALU.add), reads=yk, writes=[tk('y')])
                    yield
                    s.op('dve', lambda e, Y=Y, Tt=Tt: e.tensor_tensor(out=Tt['y'][:], in0=Tt['y'][:], in1=Y[:, 2, :], op=ALU.add), reads=yk + [tk('y')], writes=[tk('y')])
                    yield
                    s.op('act', lambda e, Tt=Tt: e.activation(out=Tt['sq'][:], in_=Tt['y'][:], func=AF.Square), reads=[tk('y')], writes=[tk('sq')])
                    yield
                    bM, bE = 2 * sl, 2 * sl + 1
                    s.op('pe', lambda e, bM=bM, Tt=Tt: e.matmul(ps[bM][:, :], lhsT=bo64[:], rhs=Tt['y'][:], start=True, stop=True), reads=[tk('y'), 'bo64'], writes=[('ps', bM)])
                    yield
                    s.op('pe', lambda e, bE=bE, Tt=Tt: e.matmul(ps[bE][:, :], lhsT=bo64[:], rhs=Tt['sq'][:], start=True, stop=True), reads=[tk('sq'), 'bo64'], writes=[('ps', bE)])
                    yield
                    s.op('act', lambda e, bM=bM, Tt=Tt: e.activation(out=Tt['mean'][:], in_=ps[bM][:, :], func=AF.Copy), reads=[('ps', bM)], writes=[tk('mean')])
                    yield
                    s.op('dve', lambda e, Tt=Tt: e.tensor_tensor(out=Tt['var'][:], in0=Tt['mean'][:], in1=Tt['mean'][:], op=ALU.mult), reads=[tk('mean')], writes=[tk('var')])
                    yield
                    s.op('dve', lambda e, bE=bE, Tt=Tt: e.tensor_tensor(out=Tt['var'][:], in0=ps[bE][:, :], in1=Tt['var'][:], op=ALU.subtract), reads=[('ps', bE), tk('var')], writes=[tk('var')])
                    yield
                    s.op('dve', lambda e, Tt=Tt: e.tensor_scalar(out=Tt['var'][:], in0=Tt['var'][:], scalar1=0.0, scalar2=64e-5, op0=ALU.max, op1=ALU.add), reads=[tk('var')], writes=[tk('var')])
                    yield
                    s.op('act', lambda e, Tt=Tt: e.activation(out=Tt['var'][:], in_=Tt['var'][:], func=AF.Sqrt), reads=[tk('var')], writes=[tk('var')])
                    yield
                    s.op('dve', lambda e, Tt=Tt: e.reciprocal(out=Tt['var'][:], in_=Tt['var'][:]), reads=[tk('var')], writes=[tk('var')])
                    yield
                    s.op('dve', lambda e, Tt=Tt: e.tensor_tensor(out=Tt['dd'][:], in0=Tt['y'][:], in1=Tt['mean'][:], op=ALU.subtract), reads=[tk('y'), tk('mean')], writes=[tk('dd')])
                    yield
                    s.op('dve', lambda e, Tt=Tt: e.tensor_tensor(out=Tt['dd'][:], in0=Tt['dd'][:], in1=Tt['var'][:], op=ALU.mult), reads=[tk('dd'), tk('var')], writes=[tk('dd')])
                    yield
                    s.op('act', lambda e, Tt=Tt: e.activation(out=Tt['dd'][:], in_=Tt['dd'][:], func=AF.Identity, bias=col('rw_ln_b', j, n), scale=col('rw_ln_w', j, n)),
                         reads=[tk('dd'), 'colsT'], writes=[tk('dd')])
                    yield
                    s.op('pool', lambda e, Tt=Tt, sl=sl: e.tensor_tensor(out=Tt['dd'][:], in0=Tt['dd'][:], in1=bg[sl][:, 0, :], op=ALU.add), reads=[tk('dd'), ('bg', sl, 0)], writes=[tk('dd')])
                    yield
                    s.op('pool', lambda e, Tt=Tt, sl=sl: e.tensor_tensor(out=ygo[sl][:], in0=Tt['dd'][:], in1=bg[sl][:, 1, :], op=ALU.mult), reads=[tk('dd'), ('bg', sl, 1)], writes=[('ygo', sl)])
                    yield
                    s.dma('act', YGv[:, n, cols_], ygo[sl][:], ds_o[sl], reads=[('ygo', sl)], writes=[('YGr', tb, n)])
                    yield
            KI = 3
            items = [(tb_, n_) for tb_ in range(NTB) for n_ in range(8)]
            for g0_ in range(0, len(items), KI):
                gens = [body(items[g0_ + q_][0], items[g0_ + q_][1], q_) for q_ in range(min(KI, len(items) - g0_))]
                alive = list(gens)
                while alive:
                    nxt_alive = []
                    for g_ in alive:
                        try:
                            next(g_)
                            nxt_alive.append(g_)
                        except StopIteration:
                            pass
                    alive = nxt_alive
            lp.close()

        def stage_rwkv(l, j):
            rw_proj(l, j)
            if cfg.stop_after == 'proj':
                return
            rw_chunk(l, j)
            if cfg.stop_after == 'chunk':
                return
            rw_scan(l, j)
            if cfg.stop_after == 'scan':
                return
            rw_post(l, j)
            if cfg.stop_after == 'post':
                return
            mix_outproj(l, 'rw_w_o', j, None, None, lambda tb: [('YGr', tb, n) for n in range(8)])

        try:
            stage_in()
            for l, kind in enumerate(cfg.layers):
                if kind == 'H':
                    stage_hyena(l, l // 2)
                elif kind == 'R':
                    stage_rwkv(l, l // 2)
                if cfg.ffn:
                    stage_ffn(l)
            stage_out()
            s.finish()
        except Exception:
            import traceback
            print(traceback.format_exc()[-1500:], flush=True)
            raise
    return nc


_CACHE = {}


def make_in_maps(cfg, inputs, ncores):
    pv, _, _ = pack_vecs(inputs)
    consts = host_consts(cfg.L)
    x = np.asarray(inputs['x'], np.float32)
    maps = []
    for c in range(ncores):
        m = {'x': np.ascontiguousarray(x[c * cfg.NB:(c + 1) * cfg.NB]), 'pvec': pv}
        for n in BIG_W:
            m[n] = np.ascontiguousarray(np.asarray(inputs[n], np.float32))
        for n, v in consts.items():
            m['c_' + n] = v
        maps.append(m)
    return maps


def kernel(**inputs):
    cfg = Cfg()
    shapes = {k: tuple(np.asarray(v).shape) for k, v in inputs.items()}
    nc = build(cfg, shapes)
    maps = make_in_maps(cfg, inputs, 8)
    res = run_bass_kernel_spmd(nc, maps, core_ids=list(range(8)))
    return np.concatenate([np.asarray(r['out'], np.float32) for r in res.results], axis=0)
```

```python
import contextlib
import os
import numpy as np
import ml_dtypes
import concourse.bass as bass
import concourse.mybir as mybir
from concourse.bass_utils import run_bass_kernel_spmd

F32 = mybir.dt.float32
BF16 = mybir.dt.bfloat16
I32 = mybir.dt.int32
ALU = mybir.AluOpType
AF = mybir.ActivationFunctionType
NPBF = ml_dtypes.bfloat16

D = 1024
KC = 8
FF = 2816
FC = 22
DEPTH = 4
PI = float(np.pi)

VEC_ORDER = ['norm_mix_g', 'norm_ffn_g', 'final_norm_g', 'hy_skip', 'hy_b_out', 'rw_mu', 'rw_w0', 'rw_a0',
             'rw_v0', 'rw_k_k', 'rw_k_a', 'rw_r_k', 'rw_ln_w', 'rw_ln_b', 'hy_b_in', 'hy_conv_w', 'hy_conv_b',
             'ff_conv_w', 'ff_conv_b']
VEC_LAST = {'norm_mix_g': 1024, 'norm_ffn_g': 1024, 'final_norm_g': 1024, 'hy_skip': 1024, 'hy_b_out': 1024,
            'rw_mu': 1024, 'rw_w0': 1024, 'rw_a0': 1024, 'rw_v0': 1024, 'rw_k_k': 1024, 'rw_k_a': 1024,
            'rw_r_k': 1024, 'rw_ln_w': 1024, 'rw_ln_b': 1024, 'hy_b_in': 3072, 'hy_conv_w': 3072,
            'hy_conv_b': 3072, 'ff_conv_w': 5632, 'ff_conv_b': 5632}
BIG_W = ['hy_w_in', 'hy_f_w1', 'hy_f_b1', 'hy_f_w2', 'hy_f_b2', 'hy_f_w3', 'hy_f_b3', 'hy_f_w4', 'hy_f_freq',
         'hy_skip', 'hy_w_out', 'rw_w_r', 'rw_w_k', 'rw_w_v', 'rw_w_o', 'rw_w1', 'rw_w2', 'rw_a1', 'rw_a2',
         'rw_v1', 'rw_v2', 'rw_g1', 'rw_g2', 'ff_w_up', 'ff_w_down']


def pack_vecs(inputs):
    rows, off, per = [], {}, {}
    r = 0
    for n in VEC_ORDER:
        a = np.asarray(inputs[n], np.float32).reshape(-1, 128)
        off[n] = r
        per[n] = VEC_LAST[n] // 128
        rows.append(a)
        r += a.shape[0]
    pv = np.concatenate(rows, 0)
    pad = (-pv.shape[0]) % 128
    if pad:
        pv = np.concatenate([pv, np.zeros((pad, 128), np.float32)], 0)
    return np.ascontiguousarray(pv), off, per


def vec_layout(shapes):
    off, per, r = {}, {}, 0
    for n in VEC_ORDER:
        cnt = int(np.prod(shapes[n])) // 128
        off[n] = r
        per[n] = VEC_LAST[n] // 128
        r += cnt
    r += (-r) % 128
    return off, per, r


def host_consts(L):
    N = 2 * L
    NT = L // 128
    c = {}
    c['ident'] = np.eye(128, dtype=np.float32)
    t = np.arange(L, dtype=np.int64)
    ang = 2.0 * np.pi * ((t[:, None] * t[None, :]) % N).astype(np.float64) / N
    C = np.cos(ang)
    S = np.sin(ang)

    def tile(M):
        return np.ascontiguousarray(M.reshape(NT, 128, NT, 128).transpose(2, 1, 0, 3).reshape(NT, 128, NT * 128).astype(NPBF))
    c['tabC'] = tile(C)
    c['tabS'] = tile(S)
    n = np.arange(L, dtype=np.float32)
    tt = (n / max(L - 1, 1)).astype(np.float32)
    fr = np.linspace(1e-4, 15, 16, dtype=np.float32)
    a2 = (np.float32(2.0 * np.pi / L) * n[:, None] * fr[None, :]).astype(np.float32)
    z = np.concatenate([tt[:, None], np.cos(a2), -np.sin(a2)], -1).astype(np.float32)
    c['zfeat'] = np.ascontiguousarray(z.T)
    dl = np.abs(np.linspace(np.log(1e-2) / 1.5, np.log(1e-2) / 0.3, D, dtype=np.float32))
    c['deltas2'] = np.tile(dl, 2)[None, :].astype(np.float32)
    c['negt'] = np.ascontiguousarray((-tt).reshape(NT, 128).T)
    p = np.arange(128)
    altv = np.where(p % 2 == 0, 1.0, -1.0)
    c['altcol'] = altv[:, None].astype(NPBF)
    c['altrow'] = altv[None, :].astype(NPBF)
    fs = np.full((128, NT), 2.0 / N, np.float32)
    fs[0, 0] = 1.0 / N
    c['fscale'] = fs
    i = (p % 64)[:, None]
    j = np.arange(64)[None, :]
    lo_s = (j < i).astype(np.float32)
    lo_i = (j <= i).astype(np.float32)
    up_s = (j > i).astype(np.float32)
    up_i = (j >= i).astype(np.float32)
    mA = np.stack([np.tile(-lo_s[:, None, :], (1, 4, 1)), np.tile(-up_s[:, None, :], (1, 4, 1))], 0)
    mT_f = np.concatenate([up_s, up_i], 1)
    mT_b = np.concatenate([lo_s, lo_i], 1)
    mB = np.stack([np.tile((mT_f * np.concatenate([-np.ones((1, 64)), np.ones((1, 64))], 1))[:, None, :], (1, 4, 1)),
                   np.tile((mT_b * np.concatenate([-np.ones((1, 64)), np.ones((1, 64))], 1))[:, None, :], (1, 4, 1))], 0)
    mC = np.stack([np.tile(mT_f[:, None, :], (1, 4, 1)), np.tile(mT_b[:, None, :], (1, 4, 1))], 0)
    c['mA'] = np.ascontiguousarray(mA.transpose(1, 0, 2, 3).reshape(128, 2 * 4 * 64)).astype(np.float32)
    c['mB'] = np.ascontiguousarray(mB.transpose(1, 0, 2, 3).reshape(128, 2 * 4 * 128)).astype(np.float32)
    c['mC'] = np.ascontiguousarray(mC.transpose(1, 0, 2, 3).reshape(128, 2 * 4 * 128)).astype(np.float32)
    bo = (p[:, None] // 64 == p[None, :] // 64).astype(np.float32)
    c['bones'] = bo.astype(NPBF)
    c['bo64'] = (bo / 64.0).astype(np.float32)
    tcol = np.arange(512)
    sm = np.stack([(tcol % 64 != 0), (tcol % 64 != 63)], 0).astype(np.float32)
    c['segm'] = np.ascontiguousarray(np.tile(sm[None], (128, 1, 1)).reshape(128, 1024))
    idr = (np.arange(64)[None, None, :] == (p % 64)[:, None, None]).astype(np.float32)
    c['identrep'] = np.ascontiguousarray(np.tile(idr, (1, 8, 1)).reshape(128, 512))
    return c


CONST_DT = {'ident': F32, 'tabC': BF16, 'tabS': BF16, 'zfeat': F32, 'deltas2': F32, 'negt': F32, 'altcol': BF16,
            'altrow': BF16, 'fscale': F32, 'mA': F32, 'mB': F32, 'mC': F32, 'bones': BF16, 'bo64': F32,
            'segm': F32, 'identrep': F32}


class _Eng:
    def __init__(self, name, eng, sem):
        self.name, self.eng, self.sem = name, eng, sem
        self.count = 0
        self.waited = {}


class _DSem:
    def __init__(self, sem):
        self.sem = sem
        self.count = 0


class Sched:
    def __init__(self, nc, es):
        self.nc, self.es = nc, es
        self.engs = {}
        for name, e in [('pe', nc.tensor), ('act', nc.scalar), ('dve', nc.vector), ('pool', nc.gpsimd), ('sp', nc.sync)]:
            self.engs[name] = _Eng(name, e, es.enter_context(nc.semaphore('sem_' + name)))
        self.res = {}
        self.dsems = []
        self.nins = 0
        self.fence = []
        self.fence_id = 0
        self.free = []
        self.cur = []

    def dsem(self, name, persist=False):
        if not persist and self.free:
            d = self.free.pop()
        else:
            d = _DSem(self.es.enter_context(self.nc.semaphore('ds_%d' % len(self.dsems))))
            self.dsems.append(d)
        if not persist:
            self.cur.append(d)
        return d

    def _collect(self, reads, writes):
        evs = []
        for k in reads:
            st = self.res.get(k)
            if st is not None and st[0] is not None:
                evs.append(st[0])
        for k in writes:
            st = self.res.get(k)
            if st is not None:
                if st[0] is not None:
                    evs.append(st[0])
                evs.extend(st[1].values())
        return evs

    def barrier(self):
        f = []
        for X in self.engs.values():
            if X.count > 0:
                f.append((X.sem, X.count, None))
        for d in self.dsems:
            if d.count > 0:
                f.append((d.sem, d.count, None))
        self.fence = f
        self.fence_id += 1
        self.free.extend(self.cur)
        self.cur = []

    def _wait(self, E, evs, skip_self=False):
        if getattr(E, 'fence_id', 0) != self.fence_id:
            E.fence_id = self.fence_id
            evs = list(evs) + [ev for ev in self.fence if ev[0] is not E.sem]
        need = {}
        for (sem, val, ds) in evs:
            if ds is not None:
                val = ds.count
            if skip_self and sem is E.sem:
                continue
            k = id(sem)
            if k not in need or need[k][1] < val:
                need[k] = (sem, val)
        for k, (sem, val) in need.items():
            if E.waited.get(k, 0) >= val:
                continue
            E.eng.wait_ge(sem, val)
            E.waited[k] = val

    def _commit(self, ev, reads, writes):
        for k in reads:
            st = self.res.get(k)
            if st is None:
                st = [None, {}]
                self.res[k] = st
            st[1][id(ev[0])] = ev
        for k in writes:
            self.res[k] = [ev, {}]

    def op(self, engname, fn, reads=(), writes=()):
        E = self.engs[engname]
        self._wait(E, self._collect(reads, writes), skip_self=(engname == 'pe'))
        ins = fn(E.eng)
        E.count += 1
        ins.then_inc(E.sem, 1)
        self._commit((E.sem, E.count, None), reads, writes)
        self.nins += 1

    def dma(self, qname, out, in_, ds, reads=(), writes=(), **kw):
        E = self.engs[qname]
        self._wait(E, self._collect(reads, writes))
        ins = E.eng.dma_start(out=out, in_=in_, **kw)
        ds.count += 16
        ins.then_inc(ds.sem, 16)
        self._commit((ds.sem, ds.count, ds), reads, writes)
        self.nins += 1

    def finish(self):
        E = self.engs['sp']
        for d in self.dsems:
            if d.count > 0 and E.waited.get(id(d.sem), 0) < d.count:
                E.eng.wait_ge(d.sem, d.count)
                E.waited[id(d.sem)] = d.count
        for n in ('pe', 'act', 'dve', 'pool'):
            X = self.engs[n]
            if X.count > 0:
                E.eng.wait_ge(X.sem, X.count)


class Cfg:
    def __init__(self, L=4096, NB=2, layers=('H', 'R', 'H', 'R'), ffn=True, dbg=False, stop_after=None):
        self.L, self.NB, self.layers, self.ffn, self.dbg, self.stop_after = L, NB, tuple(layers), ffn, dbg, stop_after


def build(cfg, shapes):
    L, NB = cfg.L, cfg.NB
    T = NB * L
    NT = L // 128
    TBS = L // 512
    NTB = T // 512
    NFFT = 2 * L
    nc = bass.Bass("TRN2", target_bir_lowering=False)
    voff, vper, vrows = vec_layout(shapes)
    NVC = vrows // 128

    def din(name, shape, dt=F32):
        return nc.dram_tensor(name, list(shape), dt, kind="ExternalInput").ap()

    def dscr(name, shape, dt):
        return nc.dram_tensor(name, list(shape), dt, kind=("ExternalOutput" if cfg.dbg else "Internal")).ap()

    x_in = din('x', [NB, L, D])
    out_d = nc.dram_tensor('out', [NB, L, D], F32, kind="ExternalOutput").ap()
    pvec = din('pvec', [vrows, 128])
    W = {n: din(n, shapes[n]) for n in BIG_W}
    cshape = {'ident': [128, 128], 'tabC': [NT, 128, NT * 128], 'tabS': [NT, 128, NT * 128], 'zfeat': [33, L],
              'deltas2': [1, 2048], 'negt': [128, NT], 'altcol': [128, 1], 'altrow': [1, 128], 'fscale': [128, NT],
              'mA': [128, 512], 'mB': [128, 1024], 'mC': [128, 1024], 'bones': [128, 128], 'bo64': [128, 128],
              'segm': [128, 1024], 'identrep': [128, 512]}
    CD = {n: din('c_' + n, cshape[n], CONST_DT[n]) for n in cshape}

    XT = dscr('XT', [D, T], F32)
    U = dscr('U', [2 * FF, T], BF16)
    Z = dscr('Z', [3 * D, T], BF16)
    GX0 = dscr('GX0', [D, T], BF16)
    YG = dscr('YG', [D, T], BF16)
    KS = dscr('KS', [L, D], BF16)
    KD = dscr('KD', [L, D], BF16)
    KRE = dscr('KRE', [L, D], F32)
    KIM = dscr('KIM', [L, D], F32)
    QT = dscr('QT', [2, D, T], F32)
    GS = dscr('GS', [2, D, T], F32)
    HS = dscr('HS', [2, D, T], F32)
    YL = dscr('YL', [D, T], F32)
    YS = dscr('YS', [2, D, T], F32)
    G2 = dscr('G2', [D, T], BF16)
    BON = dscr('BON', [D, T], BF16)
    VF = dscr('VF', [D, T], F32)
    PRJ = dscr('PRJ', [7, D, T], F32)

    def fm(ap):
        return ap.rearrange("(c p) t -> p c t", p=128)

    XTv, Uv, Zv, GX0v, YGv = fm(XT), fm(U), fm(Z), fm(GX0), fm(YG)
    YLv, G2v, BONv, VFv = fm(YL), fm(G2), fm(BON), fm(VF)
    QTv = [fm(QT[d]) for d in range(2)]
    GSv = [fm(GS[d]) for d in range(2)]
    HSv = [fm(HS[d]) for d in range(2)]
    YSv = [fm(YS[d]) for d in range(2)]
    PRJv = [fm(PRJ[i]) for i in range(7)]

    es = contextlib.ExitStack()
    with es:
        s = Sched(nc, es)

        class Pool_:
            def __init__(self):
                self.st = contextlib.ExitStack()
                self.n = 0

            def t(self, name, shape, dt):
                need = int(np.prod(shape[1:])) * (2 if dt == BF16 else 4)
                if need > nc.sbuf_bytes_remaining:
                    print("SBUF overflow allocating %s %s: need %d, remaining %d" % (name, shape, need, nc.sbuf_bytes_remaining), flush=True)
                    raise RuntimeError("SBUF overflow allocating %s %s: need %d, remaining %d" % (name, shape, need, nc.sbuf_bytes_remaining))
                return self.st.enter_context(nc.sbuf_tensor(name, list(shape), dt))

            def close(self):
                self.st.close()
                s.barrier()

        uid = [0]

        def uname(p):
            uid[0] += 1
            return "%s_%d" % (p, uid[0])

        gp = Pool_()
        es.callback(gp.close)
        ps = [es.enter_context(nc.psum_tensor("ps%d" % i, [128, 512], F32)) for i in range(6)]
        psbs = [es.enter_context(nc.psum_tensor("psb%d" % i, [128, 1024], BF16)) for i in range(2)]
        pscnt = [0]

        def nbank(lst=(0, 1, 2, 3, 4, 5)):
            i = lst[pscnt[0] % len(lst)]
            pscnt[0] += 1
            return i

        ecnt = [0]

        def evac_eng():
            ecnt[0] += 1
            return 'act' if ecnt[0] % 2 else 'dve'

        def copy_op(eng, out, in_):
            if eng == 'act':
                return lambda e: e.activation(out=out, in_=in_, func=AF.Copy)
            return lambda e: e.tensor_copy(out=out, in_=in_)

        ident = gp.t('ident', [128, 128], F32)
        identb = gp.t('identb', [128, 128], BF16)
        colsT = gp.t('colsT', [128, vrows], F32)
        ones_b = gp.t('ones_b', [128, 128], BF16)
        dsc = s.dsem('const', persist=True)
        s.dma('sp', ident[:], CD['ident'][:, :], dsc, writes=['ident'])
        s.op('dve', lambda e: e.tensor_copy(out=identb[:], in_=ident[:]), reads=['ident'], writes=['identb'])
        s.op('dve', lambda e: e.memset(ones_b[:], 1.0), writes=['ones_b'])
        pvst = gp.t('pvst', [128, 128], F32)
        ds_pv = s.dsem('pv', persist=True)
        for c in range(NVC):
            s.dma('sp', pvst[:], pvec[c * 128:(c + 1) * 128, :], ds_pv, writes=['pvst'])
            b = nbank()
            s.op('pe', lambda e, b=b: e.transpose(ps[b][:, 0:128], pvst[:], ident[:]), reads=['pvst', 'ident'], writes=[('ps', b)])
            s.op('dve', lambda e, b=b, c=c: e.tensor_copy(out=colsT[:, c * 128:(c + 1) * 128], in_=ps[b][:, 0:128]),
                 reads=[('ps', b)], writes=['colsT'])

        def col(name, idx, c):
            o = voff[name] + idx * vper[name] + c
            return colsT[:, o:o + 1]

        def colr(name, idx, c0, n):
            o = voff[name] + idx * vper[name] + c0
            return colsT[:, o:o + n]

        NSTG = 3
        stg = [gp.t('stg%d' % i, [128, 1024], F32) for i in range(NSTG)]
        ds_stg = [s.dsem('stg%d' % i, persist=True) for i in range(NSTG)]
        stgc = [0]

        def load_w(dst_fn, src, krows, ncols, key, qname='sp'):
            nk = (krows + 127) // 128
            for kc in range(nk):
                r = min(128, krows - kc * 128)
                for c0 in range(0, ncols, 1024):
                    n = min(1024, ncols - c0)
                    i = stgc[0] % NSTG
                    stgc[0] += 1
                    s.dma(qname, stg[i][0:r, 0:n], src[kc * 128:kc * 128 + r, c0:c0 + n], ds_stg[i], writes=[('stg', i)])
                    s.op('pool', lambda e, i=i, r=r, n=n, kc=kc, c0=c0: e.tensor_copy(out=dst_fn(kc, c0, n), in_=stg[i][0:r, 0:n]),
                         reads=[('stg', i)], writes=[(key, kc, c0)])

        def wkeys(key, krows, ncols):
            return [(key, kc, c0) for kc in range((krows + 127) // 128) for c0 in range(0, ncols, 1024)]

        def rmsnorm(lp, xin, xkey, Wd, gname, gidx, hout, hkey, tag):
            sq, rs = lp['sq'], lp['rs']
            s.op('act', lambda e: e.activation(out=sq[:, :, 0:Wd], in_=xin[:, :, 0:Wd], func=AF.Square), reads=[xkey], writes=['sq'])
            c0 = 0
            while c0 < Wd:
                n = min(512, Wd - c0)
                b = nbank()

                def g(e, b=b, c0=c0, n=n):
                    for k in range(8):
                        r = e.matmul(ps[b][:, 0:n], lhsT=ones_b[:], rhs=sq[:, k, c0:c0 + n], start=(k == 0), stop=(k == 7))
                    return r
                s.op('pe', g, reads=['sq', 'ones_b'], writes=[('ps', b)])
                s.op('dve', lambda e, b=b, c0=c0, n=n: e.tensor_scalar(out=rs[:, c0:c0 + n], in0=ps[b][:, 0:n], scalar1=1.0 / D, scalar2=1e-6,
                                                                        op0=ALU.mult, op1=ALU.add), reads=[('ps', b)], writes=[('rs', c0)])
                c0 += n
            rk = [('rs', c) for c in range(0, Wd, 512)]
            s.op('act', lambda e: e.activation(out=rs[:, 0:Wd], in_=rs[:, 0:Wd], func=AF.Sqrt), reads=rk, writes=rk)
            s.op('dve', lambda e: e.reciprocal(out=rs[:, 0:Wd], in_=rs[:, 0:Wd]), reads=rk, writes=rk)
            for k in range(8):
                s.op('dve', lambda e, k=k: e.scalar_tensor_tensor(out=hout[:, k, 0:Wd], in0=xin[:, k, 0:Wd], scalar=col(gname, gidx, k),
                                                                   in1=rs[:, 0:Wd], op0=ALU.mult, op1=ALU.mult),
                     reads=[xkey] + rk + ['colsT'], writes=[(hkey, k)])
            return [(hkey, k) for k in range(8)]

        def stage_in():
            lp = Pool_()
            xtok = [lp.t(uname('xtok'), [128, D], F32) for _ in range(4)]
            ds_xtok = [s.dsem(uname('xtok')) for _ in range(4)]
            xo = [lp.t(uname('xo'), [128, 8, 512], F32) for _ in range(2)]
            ds_xo = [s.dsem(uname('xo')) for _ in range(2)]
            for tb in range(NTB):
                b, t0 = tb // TBS, (tb % TBS) * 512
                o = tb % 2
                for j in range(4):
                    s.dma('sp', xtok[j][:], x_in[b, t0 + j * 128:t0 + (j + 1) * 128, :], ds_xtok[j], writes=[('xtok', j)])
                for k in range(8):
                    bk = nbank()

                    def g(e, bk=bk, k=k):
                        for j in range(4):
                            r = e.transpose(ps[bk][:, j * 128:(j + 1) * 128], xtok[j][:, k * 128:(k + 1) * 128], ident[:])
                        return r
                    s.op('pe', g, reads=[('xtok', j) for j in range(4)] + ['ident'], writes=[('ps', bk)])
                    en = evac_eng()
                    s.op(en, copy_op(en, xo[o][:, k, :], ps[bk][:, :]), reads=[('ps', bk)], writes=[('xo', o, k)])
                s.dma('pool', XTv[:, :, tb * 512:(tb + 1) * 512], xo[o][:], ds_xo[o], reads=[('xo', o, k) for k in range(8)], writes=[('XT', tb)])
            lp.close()

        def stage_out():
            lp = Pool_()
            lpd = {'sq': lp.t(uname('sq'), [128, 8, 512], BF16), 'rs': lp.t(uname('rs'), [128, 512], F32)}
            xin = [lp.t(uname('xin'), [128, 8, 512], F32) for _ in range(2)]
            ds_xin = [s.dsem(uname('xin')) for _ in range(2)]
            hh = lp.t(uname('hfin'), [128, 8, 512], F32)
            ot = [lp.t(uname('ot'), [128, D], F32) for _ in range(2)]
            ds_ot = [s.dsem(uname('ot')) for _ in range(2)]
            oc = 0
            for tb in range(NTB):
                b, t0 = tb // TBS, (tb % TBS) * 512
                xs = tb % 2
                s.dma('sp', xin[xs][:], XTv[:, :, tb * 512:(tb + 1) * 512], ds_xin[xs], reads=[('XT', tb)], writes=[('xin', xs)])
                hk = rmsnorm(lpd, xin[xs], ('xin', xs), 512, 'final_norm_g', 0, hh, 'hfin', 'f')
                for j in range(4):
                    o = oc % 2
                    oc += 1
                    for half in range(2):
                        bk = nbank()

                        def g(e, bk=bk, j=j, half=half):
                            for kk in range(4):
                                k = half * 4 + kk
                                r = e.transpose(ps[bk][:, kk * 128:(kk + 1) * 128], hh[:, k, j * 128:(j + 1) * 128], ident[:])
                            return r
                        s.op('pe', g, reads=hk + ['ident'], writes=[('ps', bk)])
                        en = evac_eng()
                        s.op(en, copy_op(en, ot[o][:, half * 512:(half + 1) * 512], ps[bk][:, :]), reads=[('ps', bk)], writes=[('ot', o, half)])
                    s.dma('pool', out_d[b, t0 + j * 128:t0 + (j + 1) * 128, :], ot[o][:], ds_ot[o], reads=[('ot', o, 0), ('ot', o, 1)], writes=[('OUT', tb, j)])
            lp.close()

        def stage_ffn(l):
            lp = Pool_()
            lpd = {'sq': lp.t(uname('sq'), [128, 8, 512], BF16), 'rs': lp.t(uname('rs'), [128, 512], F32)}
            wup = lp.t(uname('wup'), [128, 8, 2 * FF], BF16)
            load_w(lambda kc, c0, n: wup[:, kc, c0:c0 + n], W['ff_w_up'][l], D, 2 * FF, 'wup')
            wk = wkeys('wup', D, 2 * FF)
            xin = [lp.t(uname('xin'), [128, 8, 512], F32) for _ in range(2)]
            ds_xin = [s.dsem(uname('xin')) for _ in range(2)]
            hb = [lp.t(uname('hb'), [128, 8, 512], BF16) for _ in range(2)]
            ust = [lp.t(uname('ust'), [128, 4, 512], BF16) for _ in range(3)]
            ds_ust = [s.dsem(uname('ust')) for _ in range(3)]
            uc = 0
            def prep_u(tb_):
                xs_ = tb_ % 2
                s.dma('sp', xin[xs_][:], XTv[:, :, tb_ * 512:(tb_ + 1) * 512], ds_xin[xs_], reads=[('XT', tb_)], writes=[('xin', xs_)])
                return rmsnorm(lpd, xin[xs_], ('xin', xs_), 512, 'norm_ffn_g', l, hb[xs_], ('hb', xs_), 'u')
            hk_next = prep_u(0)
            for tb in range(NTB):
                xs = tb % 2
                hk = hk_next
                if tb + 1 < NTB:
                    hk_next = prep_u(tb + 1)
                for n4 in range(11):
                    o = uc % 3
                    uc += 1
                    for q in range(4):
                        n = n4 * 4 + q
                        bk = nbank()

                        def g(e, bk=bk, n=n, xs=xs):
                            for k in range(8):
                                r = e.matmul(ps[bk][:, :], lhsT=wup[:, k, n * 128:(n + 1) * 128], rhs=hb[xs][:, k, :], start=(k == 0), stop=(k == 7))
                            return r
                        s.op('pe', g, reads=hk + wk, writes=[('ps', bk)])
                        en = evac_eng()
                        s.op(en, copy_op(en, ust[o][:, q, :], ps[bk][:, :]), reads=[('ps', bk)], writes=[('ust', o, q)])
                    s.dma('pool', Uv[:, n4 * 4:(n4 + 1) * 4, tb * 512:(tb + 1) * 512], ust[o][:], ds_ust[o],
                          reads=[('ust', o, q) for q in range(4)], writes=[('U', tb, n4)])
            lp.close()
            lp = Pool_()
            wdn = lp.t(uname('wdn'), [128, FC, D], BF16)
            load_w(lambda kc, c0, n: wdn[:, kc, c0:c0 + n], W['ff_w_down'][l], FF, D, 'wdn')
            wk = wkeys('wdn', FF, D)
            dg = lp.t(uname('dg'), [128, 3 * 2 * FC, 128], BF16)
            for tap in range(3):
                for c in range(2 * FC):
                    s.op('dve', lambda e, tap=tap, c=c: e.tensor_scalar(out=dg[:, tap * 2 * FC + c, :], in0=identb[:], scalar1=col('ff_conv_w', l * 3 + tap, c), scalar2=None, op0=ALU.mult),
                         reads=['identb', 'colsT'], writes=[('dg', tap, c)])
            dgk = [('dg', tap, c) for tap in range(3) for c in range(2 * FC)]
            xin = [lp.t(uname('xin'), [128, 8, 512], F32) for _ in range(2)]
            ds_xin = [s.dsem(uname('xin')) for _ in range(2)]
            GSZ = 6
            groups = [(g0, min(GSZ, FC - g0)) for g0 in range(0, FC, GSZ)]
            uin = [lp.t(uname('uin'), [128, 2, GSZ, 514], BF16) for _ in range(2)]
            ds_uin = [s.dsem(uname('uin')) for _ in range(2)]
            act = [lp.t(uname('act'), [128, FC, 512], BF16) for _ in range(2)]
            sg = [lp.t(uname('sg'), [128, 512], F32) for _ in range(2)]
            ds_xo = [s.dsem(uname('xo')) for _ in range(2)]
            uc = 0
            pc = 0
            for tb in range(NTB):
                b, t0 = tb // TBS, (tb % TBS) * 512
                xs = tb % 2
                s.dma('sp', xin[xs][:], XTv[:, :, tb * 512:(tb + 1) * 512], ds_xin[xs], reads=[('XT', tb)], writes=[('xin', xs, n) for n in range(8)])
                lo = 1 if t0 == 0 else 0
                hi = 513 if t0 + 512 == L else 514
                nbr = [tb] + ([tb - 1] if lo == 0 else []) + ([tb + 1] if hi == 514 else [])
                ukeys = [('U', t_, n4) for t_ in nbr for n4 in range(11)]
                for (g0, gn) in groups:
                    us = uc % 2
                    uc += 1
                    if lo == 1:
                        s.op('pool', lambda e, us=us: e.memset(uin[us][:, :, :, 0:1], 0.0), writes=[('uin', us, 0), ('uin', us, 1)])
                    if hi == 513:
                        s.op('pool', lambda e, us=us: e.memset(uin[us][:, :, :, 513:514], 0.0), writes=[('uin', us, 0), ('uin', us, 1)])
                    c0 = tb * 512 - 1
                    for gv in range(2):
                        s.dma('sp', uin[us][:, gv, 0:gn, lo:hi], Uv[:, gv * FC + g0:gv * FC + g0 + gn, c0 + lo:c0 + hi], ds_uin[us],
                              reads=ukeys, writes=[('uin', us, gv)])
                    for i in range(gn):
                        gi = g0 + i
                        vi = FC + gi
                        p_ = pc % 2
                        pc += 1
                        bg, bv = nbank(), nbank()

                        def g(e, us=us, i=i, gi=gi, vi=vi, bg=bg, bv=bv):
                            for tap in range(3):
                                e.matmul(ps[bg][:, :], lhsT=dg[:, tap * 2 * FC + gi, :], rhs=uin[us][:, 0, i, tap:tap + 512], start=(tap == 0), stop=(tap == 2))
                            for tap in range(3):
                                r = e.matmul(ps[bv][:, :], lhsT=dg[:, tap * 2 * FC + vi, :], rhs=uin[us][:, 1, i, tap:tap + 512], start=(tap == 0), stop=(tap == 2))
                            return r
                        s.op('pe', g, reads=[('uin', us, 0), ('uin', us, 1)] + dgk, writes=[('ps', bg), ('ps', bv)])
                        s.op('act', lambda e, p_=p_, bg=bg, gi=gi: e.activation(out=sg[p_][:], in_=ps[bg][:, :], func=AF.Silu, bias=col('ff_conv_b', l, gi)),
                             reads=[('ps', bg), 'colsT'], writes=[('sg', p_)])
                        s.op('dve', lambda e, p_=p_, bv=bv, gi=gi, vi=vi, xs=xs: e.scalar_tensor_tensor(out=act[xs][:, gi, :], in0=ps[bv][:, :], scalar=col('ff_conv_b', l, vi), in1=sg[p_][:],
                                                                                                 op0=ALU.add, op1=ALU.mult),
                             reads=[('ps', bv), ('sg', p_), 'colsT'], writes=[('act', xs, gi)])
                ak = [('act', xs, gi) for gi in range(FC)]
                for n in range(8):
                    bk = nbank()

                    def g(e, bk=bk, n=n, xs=xs):
                        for i in range(FC):
                            r = e.matmul(ps[bk][:, :], lhsT=wdn[:, i, n * 128:(n + 1) * 128], rhs=act[xs][:, i, :], start=(i == 0), stop=(i == FC - 1))
                        return r
                    s.op('pe', g, reads=ak + wk, writes=[('ps', bk)])
                    s.op('dve', lambda e, bk=bk, n=n, xs=xs: e.tensor_tensor(out=xin[xs][:, n, :], in0=ps[bk][:, :], in1=xin[xs][:, n, :], op=ALU.add),
                         reads=[('ps', bk), ('xin', xs, n)], writes=[('xin', xs, n)])
                s.dma('pool', XTv[:, :, tb * 512:(tb + 1) * 512], xin[xs][:], ds_xo[xs], reads=[('xin', xs, n) for n in range(8)], writes=[('XT', tb)])
            lp.close()

        KNY = gp.t('KNY', [1, D], F32)

        def hy_filter(j):
            lp = Pool_()
            zf = lp.t(uname('zf'), [33, L], F32)
            w1 = lp.t(uname('hw1'), [33, 64], F32)
            w2 = lp.t(uname('hw2'), [64, 64], F32)
            w3 = lp.t(uname('hw3'), [64, 64], F32)
            w4 = lp.t(uname('hw4'), [64, 2048], F32)
            par = lp.t(uname('hpar'), [64, 8], F32)
            dl = lp.t(uname('dl'), [128, 2048], F32)
            ngt = lp.t(uname('ngt'), [128, NT], F32)
            skp = lp.t(uname('skp'), [1, D], F32)
            arg = lp.t(uname('arg'), [64, L], F32)
            hcur = lp.t(uname('hcur'), [64, L], F32)
            kint = lp.t(uname('kint'), [64, L], I32)
            kfl = lp.t(uname('kfl'), [64, L], F32)
            ds = s.dsem(uname('hyf'))
            s.dma('sp', zf[:], CD['zfeat'][:, :], ds, writes=['zf'])
            s.dma('sp', w1[:], W['hy_f_w1'][j], ds, writes=['hw1'])
            s.dma('sp', w2[:], W['hy_f_w2'][j], ds, writes=['hw2'])
            s.dma('sp', w3[:], W['hy_f_w3'][j], ds, writes=['hw3'])
            s.dma('sp', w4[:], W['hy_f_w4'][j], ds, writes=['hw4'])
            for i, nm in enumerate(['hy_f_b1', 'hy_f_b2', 'hy_f_b3', 'hy_f_freq']):
                s.dma('sp', par[:, i:i + 1], W[nm][j].rearrange("(p o) -> p o", o=1), ds, writes=[('hpar', i)])
            s.dma('sp', dl[:], CD['deltas2'][0:1, :].partition_broadcast(128), ds, writes=['dl'])
            s.dma('sp', ngt[:], CD['negt'][:, :], ds, writes=['ngt'])
            s.dma('sp', skp[:], W['hy_skip'][j:j + 1, :], ds, writes=['skp'])
            pk = [('hpar', i) for i in range(4)]
            s.op('dve', lambda e: e.tensor_tensor(out=par[:, 4:7], in0=par[:, 0:3], in1=par[:, 3:4].to_broadcast([64, 3]), op=ALU.mult),
                 reads=pk, writes=['hparfb'])
            srcs = [(zf, 33, 'zf', w1, 'hw1'), (hcur, 64, 'hcur', w2, 'hw2'), (hcur, 64, 'hcur', w3, 'hw3')]
            for i, (src, kk, skey, wt, wkey) in enumerate(srcs):
                for blk in range(L // 512):
                    bk = nbank()
                    s.op('pe', lambda e, bk=bk, src=src, kk=kk, wt=wt, blk=blk: e.matmul(ps[bk][0:64, :], lhsT=wt[0:kk, 0:64], rhs=src[0:kk, blk * 512:(blk + 1) * 512],
                                                                                          start=True, stop=True), reads=[skey, wkey], writes=[('ps', bk)])
                    s.op('dve', lambda e, bk=bk, blk=blk, i=i: e.tensor_scalar(out=arg[:, blk * 512:(blk + 1) * 512], in0=ps[bk][0:64, :], scalar1=par[:, 3:4],
                                                                                  scalar2=par[:, 4 + i:5 + i], op0=ALU.mult, op1=ALU.add),
                         reads=[('ps', bk), 'hparfb'] + pk, writes=['arg'])
                s.op('dve', lambda e: e.tensor_scalar(out=kint[:], in0=arg[:], scalar1=float(1.0 / (2 * np.pi)), scalar2=None, op0=ALU.mult), reads=['arg'], writes=['kint'])
                s.op('dve', lambda e: e.tensor_copy(out=kfl[:], in_=kint[:]), reads=['kint'], writes=['kfl'])
                s.op('dve', lambda e: e.scalar_tensor_tensor(out=arg[:], in0=kfl[:], scalar=float(-2 * np.pi), in1=arg[:], op0=ALU.mult, op1=ALU.add), reads=['kfl', 'arg'], writes=['arg'])
                s.op('dve', lambda e: e.tensor_single_scalar(out=kfl[:], in_=arg[:], scalar=PI, op=ALU.is_gt), reads=['arg'], writes=['kfl'])
                s.op('dve', lambda e: e.scalar_tensor_tensor(out=arg[:], in0=kfl[:], scalar=float(-2 * np.pi), in1=arg[:], op0=ALU.mult, op1=ALU.add), reads=['kfl', 'arg'], writes=['arg'])
                s.op('dve', lambda e: e.tensor_single_scalar(out=kfl[:], in_=arg[:], scalar=-PI, op=ALU.is_lt), reads=['arg'], writes=['kfl'])
                s.op('dve', lambda e: e.scalar_tensor_tensor(out=arg[:], in0=kfl[:], scalar=float(2 * np.pi), in1=arg[:], op0=ALU.mult, op1=ALU.add), reads=['kfl', 'arg'], writes=['arg'])
                s.op('dve', lambda e: e.tensor_scalar(out=arg[:], in0=arg[:], scalar1=-3.14159, scalar2=3.14159, op0=ALU.max, op1=ALU.min), reads=['arg'], writes=['arg'])
                s.op('act', lambda e: e.activation(out=hcur[:], in_=arg[:], func=AF.Sin), reads=['arg'], writes=['hcur'])
            dec = [lp.t(uname('dec'), [128, 2048], F32) for _ in range(2)]
            kfb = [lp.t(uname('kfb'), [128, 2048], F32) for _ in range(2)]
            ksd = [lp.t(uname('ksd'), [128, 2, D], BF16) for _ in range(2)]
            ds_ksd = [s.dsem(uname('ksd')) for _ in range(2)]
            for jt in range(NT):
                o = jt % 2
                s.op('act', lambda e, o=o, jt=jt: e.activation(out=dec[o][:], in_=dl[:], func=AF.Exp, scale=ngt[:, jt:jt + 1]), reads=['dl', 'ngt'], writes=[('dec', o)])
                for q in range(4):
                    bk = nbank()
                    s.op('pe', lambda e, bk=bk, jt=jt, q=q: e.matmul(ps[bk][:, :], lhsT=hcur[0:64, jt * 128:(jt + 1) * 128], rhs=w4[0:64, q * 512:(q + 1) * 512], start=True, stop=True),
                         reads=['hcur', 'hw4'], writes=[('ps', bk)])
                    s.op('dve', lambda e, bk=bk, o=o, q=q: e.tensor_tensor(out=kfb[o][:, q * 512:(q + 1) * 512], in0=ps[bk][:, :], in1=dec[o][:, q * 512:(q + 1) * 512], op=ALU.mult),
                         reads=[('ps', bk), ('dec', o)], writes=[('kfb', o, q)])
                kq = [('kfb', o, q) for q in range(4)]
                if jt == 0:
                    s.op('dve', lambda e, o=o: e.tensor_tensor(out=kfb[o][0:1, 0:D], in0=kfb[o][0:1, 0:D], in1=skp[0:1, :], op=ALU.add), reads=kq + ['skp'], writes=kq)
                s.op('pool', lambda e, o=o: e.tensor_tensor(out=ksd[o][:, 0, :], in0=kfb[o][:, 0:D], in1=kfb[o][:, D:2 * D], op=ALU.add), reads=kq, writes=[('ksd', o, 0)])
                s.op('pool', lambda e, o=o: e.tensor_tensor(out=ksd[o][:, 1, :], in0=kfb[o][:, 0:D], in1=kfb[o][:, D:2 * D], op=ALU.subtract), reads=kq, writes=[('ksd', o, 1)])
                s.dma('act', KS[jt * 128:(jt + 1) * 128, :], ksd[o][:, 0, :], ds_ksd[o], reads=[('ksd', o, 0)], writes=[('KS', jt)])
                s.dma('act', KD[jt * 128:(jt + 1) * 128, :], ksd[o][:, 1, :], ds_ksd[o], reads=[('ksd', o, 1)], writes=[('KD', jt)])
            lp.close()

        def hy_spectrum(j):
            lp = Pool_()
            ksb = lp.t(uname('ksb'), [128, NT, 512], BF16)
            kdb = lp.t(uname('kdb'), [128, NT, 512], BF16)
            tabc = [lp.t(uname('tabc'), [128, NT * 128], BF16) for _ in range(2)]
            tabs = [lp.t(uname('tabs'), [128, NT * 128], BF16) for _ in range(2)]
            ds_tc = [s.dsem(uname('tc')) for _ in range(2)]
            ds_ts = [s.dsem(uname('ts')) for _ in range(2)]
            fsc = lp.t(uname('fsc'), [128, NT], F32)
            alt = lp.t(uname('alt'), [128, 1], BF16)
            kst = [lp.t(uname('kst'), [128, 2, 512], F32) for _ in range(2)]
            ds_kst = [s.dsem(uname('kst')) for _ in range(2)]
            ds = s.dsem(uname('hys'))
            ds2 = s.dsem(uname('hys2'))
            s.dma('sp', fsc[:], CD['fscale'][:, :], ds, writes=['fsc'])
            s.dma('sp', alt[:], CD['altcol'][:, :], ds, writes=['alt'])
            KSv = KS.rearrange("(j p) c -> p j c", p=128)
            KDv = KD.rearrange("(j p) c -> p j c", p=128)
            kall = [('KS', jt) for jt in range(NT)] + [('KD', jt) for jt in range(NT)]
            cnt = 0
            for h2 in range(2):
                s.dma('sp', ksb[:], KSv[:, :, h2 * 512:(h2 + 1) * 512], ds2, reads=kall, writes=['ksb'])
                s.dma('sp', kdb[:], KDv[:, :, h2 * 512:(h2 + 1) * 512], ds2, reads=kall, writes=['kdb'])
                bn = nbank()

                def gn(e, bn=bn):
                    for jj in range(NT):
                        r = e.matmul(ps[bn][0:1, :], lhsT=alt[:, 0:1], rhs=ksb[:, jj, :], start=(jj == 0), stop=(jj == NT - 1))
                    return r
                s.op('pe', gn, reads=['ksb', 'alt'], writes=[('ps', bn)])
                s.op('dve', lambda e, bn=bn, h2=h2: e.tensor_scalar(out=KNY[0:1, h2 * 512:(h2 + 1) * 512], in0=ps[bn][0:1, :], scalar1=float(1.0 / NFFT), scalar2=None, op0=ALU.mult),
                     reads=[('ps', bn)], writes=[('KNY', h2)])
                for a in range(NT):
                    o = cnt % 2
                    cnt += 1
                    s.dma('sp', tabc[o][:], CD['tabC'][a], ds_tc[o], writes=[('tabc', o)])
                    s.dma('sp', tabs[o][:], CD['tabS'][a], ds_ts[o], writes=[('tabs', o)])
                    for (tab, tkey, src, skey, q) in ((tabc, 'tabc', ksb, 'ksb', 0), (tabs, 'tabs', kdb, 'kdb', 1)):
                        bk = nbank()

                        def g(e, bk=bk, tab=tab, src=src, o=o):
                            for jj in range(NT):
                                r = e.matmul(ps[bk][:, :], lhsT=tab[o][:, jj * 128:(jj + 1) * 128], rhs=src[:, jj, :], start=(jj == 0), stop=(jj == NT - 1))
                            return r
                        s.op('pe', g, reads=[(tkey, o), skey], writes=[('ps', bk)])
                        s.op('dve', lambda e, bk=bk, o=o, q=q, a=a: e.tensor_scalar(out=kst[o][:, q, :], in0=ps[bk][:, :], scalar1=fsc[:, a:a + 1], scalar2=None, op0=ALU.mult),
                             reads=[('ps', bk), 'fsc'], writes=[('kst', o, q)])
                    s.dma('act', KRE[a * 128:(a + 1) * 128, h2 * 512:(h2 + 1) * 512], kst[o][:, 0, :], ds_kst[o], reads=[('kst', o, 0)], writes=[('KRE', a, h2)])
                    s.dma('act', KIM[a * 128:(a + 1) * 128, h2 * 512:(h2 + 1) * 512], kst[o][:, 1, :], ds_kst[o], reads=[('kst', o, 1)], writes=[('KIM', a, h2)])
            lp.close()

        def hy_inproj(l, j):
            lp = Pool_()
            lpd = {'sq': lp.t(uname('sq'), [128, 8, 512], BF16), 'rs': lp.t(uname('rs'), [128, 512], F32)}
            win = lp.t(uname('win'), [128, 8, 3 * D], BF16)
            load_w(lambda kc, c0, n: win[:, kc, c0:c0 + n], W['hy_w_in'][j], D, 3 * D, 'win')
            wk = wkeys('win', D, 3 * D)
            xin = [lp.t(uname('xin'), [128, 8, 512], F32) for _ in range(2)]
            ds_xin = [s.dsem(uname('xin')) for _ in range(2)]
            hb = [lp.t(uname('hb'), [128, 8, 512], BF16) for _ in range(2)]
            zst = [lp.t(uname('zst'), [128, 4, 512], BF16) for _ in range(3)]
            ds_zst = [s.dsem(uname('zst')) for _ in range(3)]
            uc = 0
            def prep_h(tb_):
                xs_ = tb_ % 2
                s.dma('sp', xin[xs_][:], XTv[:, :, tb_ * 512:(tb_ + 1) * 512], ds_xin[xs_], reads=[('XT', tb_)], writes=[('xin', xs_)])
                return rmsnorm(lpd, xin[xs_], ('xin', xs_), 512, 'norm_mix_g', l, hb[xs_], ('hb', xs_), 'hy')
            hk_next = prep_h(0)
            for tb in range(NTB):
                xs = tb % 2
                hk = hk_next
                if tb + 1 < NTB:
                    hk_next = prep_h(tb + 1)
                for n4 in range(6):
                    o = uc % 3
                    uc += 1
                    for q in range(4):
                        n = n4 * 4 + q
                        bk = nbank()

                        def g(e, bk=bk, n=n, xs=xs):
                            for k in range(8):
                                r = e.matmul(ps[bk][:, :], lhsT=win[:, k, n * 128:(n + 1) * 128], rhs=hb[xs][:, k, :], start=(k == 0), stop=(k == 7))
                            return r
                        s.op('pe', g, reads=hk + wk, writes=[('ps', bk)])
                        en = evac_eng()
                        if en == 'act':
                            s.op('act', lambda e, bk=bk, o=o, q=q, n=n: e.activation(out=zst[o][:, q, :], in_=ps[bk][:, :], func=AF.Identity, bias=col('hy_b_in', j, n)),
                                 reads=[('ps', bk), 'colsT'], writes=[('zst', o, q)])
                        else:
                            s.op('dve', lambda e, bk=bk, o=o, q=q, n=n: e.tensor_scalar(out=zst[o][:, q, :], in0=ps[bk][:, :], scalar1=col('hy_b_in', j, n), scalar2=None, op0=ALU.add),
                                 reads=[('ps', bk), 'colsT'], writes=[('zst', o, q)])
                    s.dma('pool', Zv[:, n4 * 4:(n4 + 1) * 4, tb * 512:(tb + 1) * 512], zst[o][:], ds_zst[o],
                          reads=[('zst', o, q) for q in range(4)], writes=[('Z', tb, n4)])
            lp.close()

        def hy_conv(l, j):
            lp = Pool_()
            unb = lp.t(uname('unb'), [128, NT, 512], BF16)
            ynb = lp.t(uname('ynb'), [128, 2 * NT, 512], BF16)
            ynq = lp.t(uname('ynq'), [1, 512], BF16)
            alt = lp.t(uname('alt'), [128, 1], BF16)
            altr = lp.t(uname('altr'), [1, 128], BF16)
            zin = [lp.t(uname('zin'), [128, 3, 514], BF16) for _ in range(2)]
            ds_zin = [s.dsem(uname('zin')) for _ in range(2)]
            tcv = [[lp.t(uname('tcv'), [128, 512], F32) for _ in range(3)] for _ in range(2)]
            gx = [lp.t(uname('gx'), [128, 512], BF16) for _ in range(2)]
            ds_gx = [s.dsem(uname('gx')) for _ in range(2)]
            ub = [lp.t(uname('ub'), [128, 512], BF16) for _ in range(2)]
            tabc = [lp.t(uname('tabc'), [128, NT * 128], BF16) for _ in range(2)]
            tabs = [lp.t(uname('tabs'), [128, NT * 128], BF16) for _ in range(2)]
            ds_tc = [s.dsem(uname('tc')) for _ in range(2)]
            ds_ts = [s.dsem(uname('ts')) for _ in range(2)]
            kri = [lp.t(uname('kri'), [128, 2, 512], F32) for _ in range(2)]
            ds_kri = [s.dsem(uname('kri')) for _ in range(2)]
            tm = [lp.t(uname('tm'), [128, 512], F32) for _ in range(4)]
            ytok = [lp.t(uname('ytok'), [128, 512], F32) for _ in range(2)]
            gq = [lp.t(uname('gq'), [128, 4, 512], BF16) for _ in range(1)]
            ds_gq = [s.dsem(uname('gq')) for _ in range(1)]
            ygst = [lp.t(uname('ygst'), [128, 4, 512], BF16) for _ in range(1)]
            ds_yg = [s.dsem(uname('yg')) for _ in range(1)]
            ds = s.dsem(uname('hyc'))
            s.dma('sp', alt[:], CD['altcol'][:, :], ds, writes=['alt'])
            s.dma('sp', altr[:], CD['altrow'][:, :], ds, writes=['altr'])
            zc = 0
            tcnt = 0
            NP5 = L // 512
            for nb in range(NB * 2):
                b, h2 = nb // 2, nb % 2
                def piece(nb, b, h2, cc, cg, pc, zs):
                    nonlocal tcnt
                    if True:
                        t0 = pc * 512
                        lo = 1 if t0 == 0 else 0
                        hi = 513 if t0 + 512 == L else 514
                        tbg = b * TBS + pc
                        nbr = [tbg] + ([tbg - 1] if lo == 0 else []) + ([tbg + 1] if hi == 514 else [])
                        zkeys = [('Z', t_, n4) for t_ in nbr for n4 in range(6)]
                        if lo == 1:
                            s.op('pool', lambda e, zs=zs: e.memset(zin[zs][:, :, 0:1], 0.0), writes=[('zin', zs, q) for q in range(3)])
                            yield
                        if hi == 513:
                            s.op('pool', lambda e, zs=zs: e.memset(zin[zs][:, :, 513:514], 0.0), writes=[('zin', zs, q) for q in range(3)])
                            yield
                        c0 = b * L + t0 - 1
                        for q in range(3):
                            s.dma('sp', zin[zs][:, q, lo:hi], Zv[:, q * 8 + cg, c0 + lo:c0 + hi], ds_zin[zs], reads=zkeys, writes=[('zin', zs, q)])
                            yield
                        for q in range(3):
                            ci = q * 8 + cg
                            src = zin[zs][:, q, :]
                            dst = tcv[zs][q]
                            zk = [('zin', zs, q), 'colsT']
                            s.op('act', lambda e, src=src, dst=dst, ci=ci: e.activation(out=dst[:], in_=src[:, 1:513], func=AF.Identity, bias=col('hy_conv_b', j, ci),
                                                                                         scale=col('hy_conv_w', j * 3 + 1, ci)), reads=zk, writes=[('tcv', zs, q)])
                            yield
                            s.op('dve', lambda e, src=src, dst=dst, ci=ci: e.scalar_tensor_tensor(out=dst[:], in0=src[:, 0:512], scalar=col('hy_conv_w', j * 3 + 0, ci), in1=dst[:],
                                                                                                   op0=ALU.mult, op1=ALU.add), reads=zk + [('tcv', zs, q)], writes=[('tcv', zs, q)])
                            yield
                            s.op('dve', lambda e, src=src, dst=dst, ci=ci: e.scalar_tensor_tensor(out=dst[:], in0=src[:, 2:514], scalar=col('hy_conv_w', j * 3 + 2, ci), in1=dst[:],
                                                                                                   op0=ALU.mult, op1=ALU.add), reads=zk + [('tcv', zs, q)], writes=[('tcv', zs, q)])
                            yield
                        s.op('act', lambda e, zs=zs: e.activation(out=gx[zs][:], in_=tcv[zs][0][:], func=AF.Copy), reads=[('tcv', zs, 0)], writes=[('gx', zs)])
                        yield
                        s.dma('act', GX0v[:, cg, b * L + t0:b * L + t0 + 512], gx[zs][:], ds_gx[zs], reads=[('gx', zs)], writes=[('GX0', nb, cc, pc)])
                        yield
                        s.op('pool', lambda e, zs=zs: e.tensor_tensor(out=ub[zs][:], in0=tcv[zs][1][:], in1=tcv[zs][2][:], op=ALU.mult),
                             reads=[('tcv', zs, 1), ('tcv', zs, 2)], writes=[('ub', zs)])
                        yield
                        hb_ = tcnt % 2
                        tcnt += 1

                        def g(e, zs=zs, hb_=hb_):
                            for q in range(4):
                                r = e.transpose(psbs[hb_][:, q * 128:(q + 1) * 128], ub[zs][:, q * 128:(q + 1) * 128], identb[:])
                            return r
                        s.op('pe', g, reads=[('ub', zs), 'identb'], writes=[('psb', hb_)])
                        yield
                        en = evac_eng()
                        s.op(en, copy_op(en, unb[:, pc * 4:(pc + 1) * 4, cc * 128:(cc + 1) * 128], psbs[hb_][:, 0:512].rearrange("p (q c) -> p q c", c=128)),
                             reads=[('psb', hb_)], writes=[('unb', pc, cc)])
                        yield
                pieces = [(cc_, pc_) for cc_ in range(4) for pc_ in range(NP5)]
                for g0_ in range(0, len(pieces), 2):
                    alive = [piece(nb, b, h2, pieces[g0_ + q_][0], h2 * 4 + pieces[g0_ + q_][0], pieces[g0_ + q_][1], q_) for q_ in range(min(2, len(pieces) - g0_))]
                    while alive:
                        na_ = []
                        for g_ in alive:
                            try:
                                next(g_)
                                na_.append(g_)
                            except StopIteration:
                                pass
                        alive = na_
                ukeys = [('unb', pc_, cc_) for pc_ in range(NP5) for cc_ in range(4)]
                bn = nbank()

                def gn(e, bn=bn):
                    for jj in range(NT):
                        r = e.matmul(ps[bn][0:1, :], lhsT=alt[:, 0:1], rhs=unb[:, jj, :], start=(jj == 0), stop=(jj == NT - 1))
                    return r
                s.op('pe', gn, reads=ukeys + ['alt'], writes=[('ps', bn)])
                s.op('dve', lambda e, bn=bn, h2=h2: e.tensor_tensor(out=ynq[0:1, :], in0=ps[bn][0:1, :], in1=KNY[0:1, h2 * 512:(h2 + 1) * 512], op=ALU.mult),
                     reads=[('ps', bn), ('KNY', h2)], writes=['ynq'])
                for a in range(NT):
                    o = a % 2
                    s.dma('sp', tabc[o][:], CD['tabC'][a], ds_tc[o], writes=[('tabc', o)])
                    s.dma('sp', tabs[o][:], CD['tabS'][a], ds_ts[o], writes=[('tabs', o)])
                    s.dma('sp', kri[o][:, 0, :], KRE[a * 128:(a + 1) * 128, h2 * 512:(h2 + 1) * 512], ds_kri[o], reads=[('KRE', a, h2)], writes=[('kri', o, 0)])
                    s.dma('sp', kri[o][:, 1, :], KIM[a * 128:(a + 1) * 128, h2 * 512:(h2 + 1) * 512], ds_kri[o], reads=[('KIM', a, h2)], writes=[('kri', o, 1)])
                    bks = []
                    for (tab, tkey) in ((tabc, 'tabc'), (tabs, 'tabs')):
                        bk = nbank()
                        bks.append(bk)

                        def g(e, bk=bk, tab=tab, o=o):
                            for jj in range(NT):
                                r = e.matmul(ps[bk][:, :], lhsT=tab[o][:, jj * 128:(jj + 1) * 128], rhs=unb[:, jj, :], start=(jj == 0), stop=(jj == NT - 1))
                            return r
                        s.op('pe', g, reads=ukeys + [(tkey, o)], writes=[('ps', bk)])
                    br, bi = bks
                    kk_ = [('kri', o, 0), ('kri', o, 1)]
                    s.op('dve', lambda e, br=br, o=o: e.tensor_tensor(out=tm[0][:], in0=ps[br][:, :], in1=kri[o][:, 0, :], op=ALU.mult), reads=[('ps', br)] + kk_, writes=[('tm', 0)])
                    s.op('dve', lambda e, bi=bi, o=o: e.tensor_tensor(out=tm[1][:], in0=ps[bi][:, :], in1=kri[o][:, 1, :], op=ALU.mult), reads=[('ps', bi)] + kk_, writes=[('tm', 1)])
                    s.op('pool', lambda e, a=a: e.tensor_tensor(out=ynb[:, a, :], in0=tm[0][:], in1=tm[1][:], op=ALU.subtract), reads=[('tm', 0), ('tm', 1)], writes=[('ynb', a)])
                    s.op('dve', lambda e, br=br, o=o: e.tensor_tensor(out=tm[2][:], in0=ps[br][:, :], in1=kri[o][:, 1, :], op=ALU.mult), reads=[('ps', br)] + kk_, writes=[('tm', 2)])
                    s.op('dve', lambda e, bi=bi, o=o: e.tensor_tensor(out=tm[3][:], in0=ps[bi][:, :], in1=kri[o][:, 0, :], op=ALU.mult), reads=[('ps', bi)] + kk_, writes=[('tm', 3)])
                    s.op('pool', lambda e, a=a: e.tensor_tensor(out=ynb[:, NT + a, :], in0=tm[2][:], in1=tm[3][:], op=ALU.add), reads=[('tm', 2), ('tm', 3)], writes=[('ynb', NT + a)])
                ykeys = [('ynb', a) for a in range(2 * NT)] + ['ynq']
                for bp in range(NT):
                    o = bp % 2
                    q4 = bp % 4
                    g4 = 0
                    s.dma('sp', tabc[o][:], CD['tabC'][bp], ds_tc[o], writes=[('tabc', o)])
                    s.dma('sp', tabs[o][:], CD['tabS'][bp], ds_ts[o], writes=[('tabs', o)])
                    if q4 == 0:
                        cols_ = slice(b * L + (bp // 4) * 512, b * L + (bp // 4) * 512 + 512)
                        s.dma('sp', gq[g4][:], GX0v[:, h2 * 4:(h2 + 1) * 4, cols_], ds_gq[g4],
                              reads=[('GX0', nb, cc, bp // 4) for cc in range(4)], writes=[('gq', g4)])
                    bk = nbank()

                    def g(e, bk=bk, o=o):
                        for jj in range(NT):
                            e.matmul(ps[bk][:, :], lhsT=tabc[o][:, jj * 128:(jj + 1) * 128], rhs=ynb[:, jj, :], start=(jj == 0), stop=False)
                        for jj in range(NT):
                            e.matmul(ps[bk][:, :], lhsT=tabs[o][:, jj * 128:(jj + 1) * 128], rhs=ynb[:, NT + jj, :], start=False, stop=False)
                        return e.matmul(ps[bk][:, :], lhsT=altr[0:1, :], rhs=ynq[0:1, :], start=False, stop=True)
                    s.op('pe', g, reads=ykeys + [('tabc', o), ('tabs', o), 'altr'], writes=[('ps', bk)])
                    s.op('act', lambda e, bk=bk, o=o: e.activation(out=ytok[o][:], in_=ps[bk][:, :], func=AF.Copy), reads=[('ps', bk)], writes=[('ytok', o)])
                    bt = nbank()

                    def g2(e, bt=bt, o=o):
                        for cc in range(4):
                            r = e.transpose(ps[bt][:, cc * 128:(cc + 1) * 128], ytok[o][:, cc * 128:(cc + 1) * 128], ident[:])
                        return r
                    s.op('pe', g2, reads=[('ytok', o), 'ident'], writes=[('ps', bt)])
                    s.op('dve', lambda e, bt=bt, g4=g4, q4=q4: e.tensor_tensor(out=ygst[g4][:, :, q4 * 128:(q4 + 1) * 128], in0=ps[bt][:, :].rearrange("p (c t) -> p c t", t=128),
                                                                                in1=gq[g4][:, :, q4 * 128:(q4 + 1) * 128], op=ALU.mult),
                         reads=[('ps', bt), ('gq', g4)], writes=[('ygst', g4, q4)])
                    if q4 == 3:
                        cols_ = slice(b * L + (bp // 4) * 512, b * L + (bp // 4) * 512 + 512)
                        s.dma('pool', YGv[:, h2 * 4:(h2 + 1) * 4, cols_], ygst[g4][:], ds_yg[g4], reads=[('ygst', g4, q) for q in range(4)],
                              writes=[('YG', b * TBS + bp // 4, h2)])
            lp.close()

        def mix_outproj(l, wname, widx, bias_name, bias_idx, ygkeys_fn, norm_after=None):
            lp = Pool_()
            wo = lp.t(uname('wo'), [128, 8, D], BF16)
            load_w(lambda kc, c0, n: wo[:, kc, c0:c0 + n], W[wname][widx], D, D, 'wo')
            wk = wkeys('wo', D, D)
            xin = [lp.t(uname('xin'), [128, 8, 512], F32) for _ in range(2)]
            ds_xin = [s.dsem(uname('xin')) for _ in range(2)]
            ygi = [lp.t(uname('ygi'), [128, 8, 512], BF16) for _ in range(2)]
            ds_ygi = [s.dsem(uname('ygi')) for _ in range(2)]
            ds_xo = [s.dsem(uname('xo')) for _ in range(2)]
            for tb in range(NTB):
                xs = tb % 2
                s.dma('sp', xin[xs][:], XTv[:, :, tb * 512:(tb + 1) * 512], ds_xin[xs], reads=[('XT', tb)], writes=[('xin', xs, n) for n in range(8)])
                s.dma('sp', ygi[xs][:], YGv[:, :, tb * 512:(tb + 1) * 512], ds_ygi[xs], reads=ygkeys_fn(tb), writes=[('ygi', xs)])
                for n in range(8):
                    bk = nbank()

                    def g(e, bk=bk, n=n, xs=xs):
                        for k in range(8):
                            r = e.matmul(ps[bk][:, :], lhsT=wo[:, k, n * 128:(n + 1) * 128], rhs=ygi[xs][:, k, :], start=(k == 0), stop=(k == 7))
                        return r
                    s.op('pe', g, reads=[('ygi', xs)] + wk, writes=[('ps', bk)])
                    if bias_name is not None:
                        s.op('dve', lambda e, bk=bk, n=n, xs=xs: e.scalar_tensor_tensor(out=xin[xs][:, n, :], in0=ps[bk][:, :], scalar=col(bias_name, bias_idx, n), in1=xin[xs][:, n, :],
                                                                                       op0=ALU.add, op1=ALU.add), reads=[('ps', bk), ('xin', xs, n), 'colsT'], writes=[('xin', xs, n)])
                    else:
                        s.op('dve', lambda e, bk=bk, n=n, xs=xs: e.tensor_tensor(out=xin[xs][:, n, :], in0=ps[bk][:, :], in1=xin[xs][:, n, :], op=ALU.add),
                             reads=[('ps', bk), ('xin', xs, n)], writes=[('xin', xs, n)])
                s.dma('pool', XTv[:, :, tb * 512:(tb + 1) * 512], xin[xs][:], ds_xo[xs], reads=[('xin', xs, n) for n in range(8)], writes=[('XT', tb)])
            lp.close()

        def stage_hyena(l, j):
            hy_filter(j)
            hy_spectrum(j)
            hy_inproj(l, j)
            hy_conv(l, j)
            mix_outproj(l, 'hy_w_out', j, 'hy_b_out', j, lambda tb: [('YG', tb, 0), ('YG', tb, 1)])

        P_R, P_K, P_V, P_A, P_SF, P_SB, P_SV = range(7)

        def rw_proj(l, j):
            vres = (j > 0)
            lp = Pool_()
            lpd = {'sq': lp.t(uname('sq'), [128, 8, 514], BF16), 'rs': lp.t(uname('rs'), [128, 514], F32)}
            wr = lp.t(uname('wr'), [128, 8, D], BF16)
            wk_ = lp.t(uname('wk'), [128, 8, D], BF16)
            wv = lp.t(uname('wv'), [128, 8, D], BF16)
            w1 = [lp.t(uname('w1'), [128, 8, 64], BF16) for _ in range(2)]
            a1 = lp.t(uname('a1'), [128, 8, 64], BF16)
            g1 = lp.t(uname('g1'), [128, 8, 128], BF16)
            w2 = [lp.t(uname('w2'), [64, D], BF16) for _ in range(2)]
            a2 = lp.t(uname('a2'), [64, D], BF16)
            g2 = lp.t(uname('g2'), [128, D], BF16)
            load_w(lambda kc, c0, n: wr[:, kc, c0:c0 + n], W['rw_w_r'][j], D, D, 'wr')
            load_w(lambda kc, c0, n: wk_[:, kc, c0:c0 + n], W['rw_w_k'][j], D, D, 'wk')
            load_w(lambda kc, c0, n: wv[:, kc, c0:c0 + n], W['rw_w_v'][j], D, D, 'wv')
            for d in range(2):
                load_w(lambda kc, c0, n, d=d: w1[d][:, kc, 0:n], W['rw_w1'][j, d], D, 64, ('w1', d))
                load_w(lambda kc, c0, n, d=d: w2[d][0:64, c0:c0 + n], W['rw_w2'][j, d], 64, D, ('w2', d))
            load_w(lambda kc, c0, n: a1[:, kc, 0:n], W['rw_a1'][j], D, 64, 'a1')
            load_w(lambda kc, c0, n: a2[0:64, c0:c0 + n], W['rw_a2'][j], 64, D, 'a2')
            load_w(lambda kc, c0, n: g1[:, kc, 0:n], W['rw_g1'][j], D, 128, 'g1')
            load_w(lambda kc, c0, n: g2[:, c0:c0 + n], W['rw_g2'][j], 128, D, 'g2')
            if vres:
                v1 = lp.t(uname('v1'), [128, 8, 32], BF16)
                v2 = lp.t(uname('v2'), [32, D], BF16)
                load_w(lambda kc, c0, n: v1[:, kc, 0:n], W['rw_v1'][j - 1], D, 32, 'v1')
                load_w(lambda kc, c0, n: v2[0:32, c0:c0 + n], W['rw_v2'][j - 1], 32, D, 'v2')
            xh = lp.t(uname('xh'), [128, 8, 514], F32)
            ds_xh = s.dsem(uname('xh'))
            xx = lp.t(uname('xx'), [128, 8, 512], F32)
            xj = [lp.t(uname('xj'), [128, 8, 512], BF16) for _ in range(2)]
            lo_t = {nm: lp.t(uname(nm), [128, 512], BF16) for nm in ('twf', 'twb', 'ta', 'tv', 'tg')}
            ost = [lp.t(uname('ost'), [128, 512], F32) for _ in range(4)]
            ds_ost = [s.dsem(uname('ost')) for _ in range(4)]
            ds_vf = [s.dsem(uname('vfs')) for _ in range(4)]
            gst = [lp.t(uname('gst'), [128, 512], BF16) for _ in range(2)]
            ds_gst = [s.dsem(uname('gst')) for _ in range(2)]
            oc = [0]
            gc = [0]
            xc = [0]

            def make_xj(m, hk):
                sl = xc[0] % 2
                xc[0] += 1
                for k in range(8):
                    s.op('dve', lambda e, k=k, sl=sl: e.scalar_tensor_tensor(out=xj[sl][:, k, :], in0=xx[:, k, :], scalar=col('rw_mu', j * 6 + m, k), in1=xh[:, k, 1:513],
                                                                            op0=ALU.mult, op1=ALU.add), reads=['xx', 'colsT'] + hk, writes=[('xj', sl, k)])
                return sl, [('xj', sl, k) for k in range(8)]

            def proj(wt_fn, wkeys_, sl, xk, M, bank_rows=None):
                bk = nbank()

                def g(e, bk=bk):
                    for k in range(8):
                        r = e.matmul(ps[bk][0:M, :], lhsT=wt_fn(k), rhs=xj[sl][:, k, :], start=(k == 0), stop=(k == 7))
                    return r
                s.op('pe', g, reads=xk + wkeys_, writes=[('ps', bk)])
                return bk

            def store_f32(bk, pidx, n, tb, func=None, bias=None):
                o = oc[0] % 4
                oc[0] += 1
                if func is None:
                    en = evac_eng()
                    s.op(en, copy_op(en, ost[o][:], ps[bk][:, :]), reads=[('ps', bk)], writes=[('ost', o)])
                else:
                    if bias is None:
                        s.op('act', lambda e, bk=bk, o=o: e.activation(out=ost[o][:], in_=ps[bk][:, :], func=func), reads=[('ps', bk)], writes=[('ost', o)])
                    else:
                        s.op('act', lambda e, bk=bk, o=o: e.activation(out=ost[o][:], in_=ps[bk][:, :], func=func, bias=bias), reads=[('ps', bk), 'colsT'], writes=[('ost', o)])
                s.dma('pool', PRJv[pidx][:, n, tb * 512:(tb + 1) * 512], ost[o][:], ds_ost[o], reads=[('ost', o)], writes=[('PRJ', pidx, tb, n)])
                return o

            for tb in range(NTB):
                b, t0 = tb // TBS, (tb % TBS) * 512
                lo = 1 if t0 == 0 else 0
                hi = 513 if t0 + 512 == L else 514
                nbr = [tb] + ([tb - 1] if lo == 0 else []) + ([tb + 1] if hi == 514 else [])
                if lo == 1:
                    s.op('pool', lambda e: e.memset(xh[:, :, 0:1], 0.0), writes=['xhl'])
                if hi == 513:
                    s.op('pool', lambda e: e.memset(xh[:, :, 513:514], 0.0), writes=['xhl'])
                c0 = tb * 512 - 1
                s.dma('sp', xh[:, :, lo:hi], XTv[:, :, c0 + lo:c0 + hi], ds_xh, reads=[('XT', t_) for t_ in nbr], writes=['xhl'] + [('xh', k) for k in range(8)])
                CUT = int(os.environ.get('CUT', '99'))
                if CUT <= 0:
                    continue
                hk = rmsnorm(lpd, xh, 'xhl', 514, 'norm_mix_g', l, xh, 'xh', 'rw')
                if CUT <= 1:
                    continue
                s.op('dve', lambda e: e.tensor_tensor(out=xx[:], in0=xh[:, :, 0:512], in1=xh[:, :, 2:514], op=ALU.add), reads=hk, writes=['xx'])
                s.op('dve', lambda e: e.scalar_tensor_tensor(out=xx[:], in0=xx[:], scalar=0.5, in1=xh[:, :, 1:513], op0=ALU.mult, op1=ALU.subtract), reads=hk + ['xx'], writes=['xx'])
                if CUT <= 2:
                    continue
                sl, xk = make_xj(1, hk)
                for d, nm in ((0, 'twf'), (1, 'twb')):
                    bk = proj(lambda k, d=d: w1[d][:, k, :], wkeys(('w1', d), D, 64), sl, xk, 64)
                    s.op('act', lambda e, bk=bk, nm=nm: e.activation(out=lo_t[nm][0:64, :], in_=ps[bk][0:64, :], func=AF.Tanh), reads=[('ps', bk)], writes=[nm])
                for d, nm, pidx in ((0, 'twf', P_SF), (1, 'twb', P_SB)):
                    for n in range(8):
                        bk = nbank()
                        s.op('pe', lambda e, bk=bk, d=d, n=n, nm=nm: e.matmul(ps[bk][:, :], lhsT=w2[d][0:64, n * 128:(n + 1) * 128], rhs=lo_t[nm][0:64, :], start=True, stop=True),
                             reads=[nm] + wkeys(('w2', d), 64, D), writes=[('ps', bk)])
                        store_f32(bk, pidx, n, tb, AF.Sigmoid, col('rw_w0', j * 2 + d, n))
                if CUT <= 3:
                    continue
                sl, xk = make_xj(4, hk)
                bk = proj(lambda k: a1[:, k, :], wkeys('a1', D, 64), sl, xk, 64)
                s.op('act', lambda e, bk=bk: e.activation(out=lo_t['ta'][0:64, :], in_=ps[bk][0:64, :], func=AF.Copy), reads=[('ps', bk)], writes=['ta'])
                for n in range(8):
                    bk = nbank()
                    s.op('pe', lambda e, bk=bk, n=n: e.matmul(ps[bk][:, :], lhsT=a2[0:64, n * 128:(n + 1) * 128], rhs=lo_t['ta'][0:64, :], start=True, stop=True),
                         reads=['ta'] + wkeys('a2', 64, D), writes=[('ps', bk)])
                    store_f32(bk, P_A, n, tb, AF.Sigmoid, col('rw_a0', j, n))
                if CUT <= 4:
                    continue
                sl, xk = make_xj(5, hk)
                bk = proj(lambda k: g1[:, k, :], wkeys('g1', D, 128), sl, xk, 128)
                s.op('act', lambda e, bk=bk: e.activation(out=lo_t['tg'][:], in_=ps[bk][:, :], func=AF.Sigmoid), reads=[('ps', bk)], writes=['tg'])
                for n in range(8):
                    bk = nbank()
                    s.op('pe', lambda e, bk=bk, n=n: e.matmul(ps[bk][:, :], lhsT=g2[:, n * 128:(n + 1) * 128], rhs=lo_t['tg'][:], start=True, stop=True),
                         reads=['tg'] + wkeys('g2', 128, D), writes=[('ps', bk)])
                    o = gc[0] % 2
                    gc[0] += 1
                    en = evac_eng()
                    s.op(en, copy_op(en, gst[o][:], ps[bk][:, :]), reads=[('ps', bk)], writes=[('gst', o)])
                    s.dma('pool', G2v[:, n, tb * 512:(tb + 1) * 512], gst[o][:], ds_gst[o], reads=[('gst', o)], writes=[('G2', tb, n)])
                if CUT <= 5:
                    continue
                for (m, wt, wkey, pidx) in ((0, wr, 'wr', P_R), (2, wk_, 'wk', P_K), (3, wv, 'wv', P_V))[:int(os.environ.get('CUTM', '3'))]:
                    sl, xk = make_xj(m, hk)
                    if m == 3 and vres:
                        bk = proj(lambda k: v1[:, k, :], wkeys('v1', D, 32), sl, xk, 32)
                        s.op('act', lambda e, bk=bk: e.activation(out=lo_t['tv'][0:32, :], in_=ps[bk][0:32, :], func=AF.Copy), reads=[('ps', bk)], writes=['tv'])
                        for n in range(8):
                            bk = nbank()
                            s.op('pe', lambda e, bk=bk, n=n: e.matmul(ps[bk][:, :], lhsT=v2[0:32, n * 128:(n + 1) * 128], rhs=lo_t['tv'][0:32, :], start=True, stop=True),
                                 reads=['tv'] + wkeys('v2', 32, D), writes=[('ps', bk)])
                            store_f32(bk, P_SV, n, tb, AF.Sigmoid, col('rw_v0', j - 1, n))
                    for n in range(8):
                        bk = proj(lambda k, wt=wt, n=n: wt[:, k, n * 128:(n + 1) * 128], wkeys(wkey, D, D), sl, xk, 128)
                        o = store_f32(bk, pidx, n, tb)
                        if m == 3 and not vres:
                            s.dma('act', VFv[:, n, tb * 512:(tb + 1) * 512], ost[o][:], ds_vf[o], reads=[('ost', o)], writes=[('VF', tb, n)])
            lp.close()

        def rw_chunk(l, j):
            vres = (j > 0)
            lp = Pool_()
            mA = lp.t(uname('mA'), [128, 2, 4, 64], F32)
            mB = lp.t(uname('mB'), [128, 2, 4, 128], F32)
            mC = lp.t(uname('mC'), [128, 2, 4, 128], F32)
            bones = lp.t(uname('bones'), [128, 128], BF16)
            segm = lp.t(uname('segm'), [128, 2, 512], F32)
            idr = lp.t(uname('idr'), [128, 8, 64], F32)
            ds = s.dsem(uname('rwc'))
            s.dma('sp', mA[:], CD['mA'].rearrange("p (d c k) -> p d c k", d=2, c=4), ds, writes=['mA'])
            s.dma('sp', mB[:], CD['mB'].rearrange("p (d c k) -> p d c k", d=2, c=4), ds, writes=['mB'])
            s.dma('sp', mC[:], CD['mC'].rearrange("p (d c k) -> p d c k", d=2, c=4), ds, writes=['mC'])
            s.dma('sp', bones[:], CD['bones'][:, :], ds, writes=['bones'])
            s.dma('sp', segm[:], CD['segm'].rearrange("p (d t) -> p d t", d=2), ds, writes=['segm'])
            s.dma('sp', idr[:], CD['identrep'].rearrange("p (c k) -> p c k", c=8), ds, writes=['idr'])
            NIN = 8 if vres else 6
            inp_ = [lp.t(uname('rin'), [128, NIN, 512], F32) for _ in range(2)]
            ds_in = [s.dsem(uname('rin')) for _ in range(2)]
            I_R, I_K, I_V, I_A, I_SF, I_SB, I_SV, I_VF = range(8)
            tf = {nm: [lp.t(uname(nm), [128, 512], F32) for _ in range(2)] for nm in ('kk', 'rn', 'kkn', 'fac', 'kmod', 'bb')}
            td = {nm: lp.t(uname(nm), [128, 512], F32) for nm in ('lw', 'cin', 'cex', 'e1', 'e2', 'e3', 'e4')}
            kk2 = lp.t(uname('kk2'), [128, 512], BF16)
            rkb = lp.t(uname('rkb'), [128, 512], BF16)
            wc = [lp.t(uname('wc'), [128, 2, 8], F32) for _ in range(2)]
            LR = [lp.t(uname('LR'), [128, 2, 8, 128], BF16) for _ in range(2)]
            BK = [lp.t(uname('BK'), [128, 2, 8, 128], BF16) for _ in range(2)]
            Bh = [lp.t(uname('Bh'), [128, 2, 512], BF16) for _ in range(2)]
            Kh = [lp.t(uname('Kh'), [128, 2, 512], BF16) for _ in range(2)]
            vb = [lp.t(uname('vb'), [128, 512], BF16) for _ in range(2)]
            Vt = lp.t(uname('Vt'), [128, 4, 128], BF16)
            Bht = lp.t(uname('Bht'), [128, 2, 4, 128], BF16)
            Kht = lp.t(uname('Kht'), [128, 2, 4, 128], BF16)
            Zt = lp.t(uname('Zt'), [128, 2, 2, 4, 128], BF16)
            XX = lp.t(uname('XX'), [128, 4, 2, 4, 64], BF16)
            MBt = lp.t(uname('MBt'), [128, 4, 4, 128], BF16)
            MCt = lp.t(uname('MCt'), [128, 4, 4, 128], BF16)
            qst = [lp.t(uname('qst'), [128, 2, 8, 64], F32) for _ in range(2)]
            gst_ = [lp.t(uname('gst'), [128, 2, 8, 64], F32) for _ in range(2)]
            hst = [lp.t(uname('hst'), [128, 2, 8, 64], F32) for _ in range(2)]
            yst = [lp.t(uname('yst'), [128, 8, 64], F32) for _ in range(2)]
            dwc = lp.t(uname('dwc'), [128, 8, 64], F32)
            bon = [lp.t(uname('bon'), [128, 512], BF16) for _ in range(2)]
            ds_out = [s.dsem(uname('rwo')) for _ in range(2)]
            it = 0
            pbc = [0]

            def v3(t_):
                return t_[:].rearrange("p (c k) -> p c k", k=64)

            def P1(tb, n, sl):
                if True:
                    X = inp_[sl]
                    cols_ = slice(tb * 512, (tb + 1) * 512)
                    ik = []
                    for q, pidx in enumerate((P_R, P_K, P_V, P_A, P_SF, P_SB) + ((P_SV,) if vres else ())):
                        s.dma('sp', X[:, q, :], PRJv[pidx][:, n, cols_], ds_in[sl], reads=[('PRJ', pidx, tb, n)], writes=[('rin', sl, q)])
                        yield
                        ik.append(('rin', sl, q))
                    if vres:
                        s.dma('sp', X[:, I_VF, :], VFv[:, n, cols_], ds_in[sl], reads=[('VF', tb, n)], writes=[('rin', sl, I_VF)])
                        yield
                        ik.append(('rin', sl, I_VF))
                    r_, k_, v_, a_ = X[:, I_R, :], X[:, I_K, :], X[:, I_V, :], X[:, I_A, :]
                    F = {nm: tf[nm][sl] for nm in tf}
                    fk = lambda nm: (nm, sl)
                    if vres:
                        s.op('pool', lambda e, X=X: e.tensor_tensor(out=X[:, I_VF, :], in0=X[:, I_VF, :], in1=X[:, I_V, :], op=ALU.subtract), reads=ik, writes=[('rin', sl, I_VF)])
                        yield
                        s.op('pool', lambda e, X=X: e.tensor_tensor(out=X[:, I_VF, :], in0=X[:, I_VF, :], in1=X[:, I_SV, :], op=ALU.mult), reads=ik, writes=[('rin', sl, I_VF)])
                        yield
                        s.op('pool', lambda e, X=X: e.tensor_tensor(out=X[:, I_V, :], in0=X[:, I_V, :], in1=X[:, I_VF, :], op=ALU.add), reads=ik, writes=[('rin', sl, I_V)])
                        yield
                    s.op('dve', lambda e, F=F, k_=k_: e.tensor_scalar(out=F['kk'][:], in0=k_, scalar1=col('rw_k_k', j, n), scalar2=None, op0=ALU.mult), reads=ik + ['colsT'], writes=[fk('kk')])
                    yield
                    s.op('pool', lambda e, F=F: e.tensor_tensor(out=kk2[:], in0=F['kk'][:], in1=F['kk'][:], op=ALU.mult), reads=[fk('kk')], writes=['kk2'])
                    yield
                    bk = nbank()
                    s.op('pe', lambda e, bk=bk: e.matmul(ps[bk][:, :], lhsT=bones[:], rhs=kk2[:], start=True, stop=True), reads=['kk2', 'bones'], writes=[('ps', bk)])
                    yield
                    s.op('dve', lambda e, bk=bk, F=F: e.tensor_scalar(out=F['rn'][:], in0=ps[bk][:, :], scalar1=1e-24, scalar2=None, op0=ALU.max), reads=[('ps', bk)], writes=[fk('rn')])
                    yield
                    s.op('act', lambda e, F=F: e.activation(out=F['rn'][:], in_=F['rn'][:], func=AF.Sqrt), reads=[fk('rn')], writes=[fk('rn')])
                    yield
                    s.op('dve', lambda e, F=F: e.reciprocal(out=F['rn'][:], in_=F['rn'][:]), reads=[fk('rn')], writes=[fk('rn')])
                    yield
                    s.op('pool', lambda e, F=F: e.tensor_tensor(out=F['kkn'][:], in0=F['kk'][:], in1=F['rn'][:], op=ALU.mult), reads=[fk('kk'), fk('rn')], writes=[fk('kkn')])
                    yield
                    s.op('dve', lambda e, F=F, a_=a_: e.tensor_scalar(out=F['fac'][:], in0=a_, scalar1=-1.0, scalar2=col('rw_k_a', j, n), op0=ALU.add, op1=ALU.mult),
                         reads=ik + ['colsT'], writes=[fk('fac')])
                    yield
                    s.op('dve', lambda e, F=F, k_=k_: e.scalar_tensor_tensor(out=F['kmod'][:], in0=F['fac'][:], scalar=1.0, in1=k_, op0=ALU.add, op1=ALU.mult),
                         reads=ik + [fk('fac')], writes=[fk('kmod')])
                    yield
                    s.op('pool', lambda e, F=F, a_=a_: e.tensor_tensor(out=F['bb'][:], in0=F['kkn'][:], in1=a_, op=ALU.mult), reads=ik + [fk('kkn')], writes=[fk('bb')])
                    yield
                    s.op('dve', lambda e, F=F, r_=r_: e.scalar_tensor_tensor(out=rkb[:], in0=r_, scalar=col('rw_r_k', j, n), in1=F['kmod'][:], op0=ALU.mult, op1=ALU.mult),
                         reads=ik + [fk('kmod'), 'colsT'], writes=['rkb'])
                    yield
                    bk = nbank()
                    s.op('pe', lambda e, bk=bk: e.matmul(ps[bk][:, :], lhsT=bones[:], rhs=rkb[:], start=True, stop=True), reads=['rkb', 'bones'], writes=[('ps', bk)])
                    yield
                    s.op('dve', lambda e, bk=bk, v_=v_, sl=sl: e.tensor_tensor(out=bon[sl][:], in0=ps[bk][:, :], in1=v_, op=ALU.mult), reads=[('ps', bk)] + ik, writes=[('bon', sl)])
                    yield
                    s.dma('act', BONv[:, n, cols_], bon[sl][:], ds_out[sl], reads=[('bon', sl)], writes=[('BON', tb, n)])
                    yield
                    s.op('act', lambda e, v_=v_, sl=sl: e.activation(out=vb[sl][:], in_=v_, func=AF.Copy), reads=ik, writes=[('vb', sl)])
                    yield
                    for d in range(2):
                        sg_ = X[:, I_SF + d, :]
                        s.op('dve', lambda e, sg_=sg_: e.tensor_scalar(out=td['lw'][:], in0=sg_, scalar1=-0.6065306597126334, scalar2=None, op0=ALU.mult), reads=ik, writes=['lw'])
                        yield
                        if d == 0:
                            s.op('dve', lambda e: e.tensor_tensor_scan(out=td['cin'][:], data0=segm[:, 0, :], data1=td['lw'][:], initial=0.0, op0=ALU.mult, op1=ALU.add),
                                 reads=['lw', 'segm'], writes=['cin'])
                            yield
                            totap = v3(td['cin'])[:, :, 63:64]
                        else:
                            s.op('dve', lambda e: e.tensor_tensor_scan(out=td['cin'][:, ::-1], data0=segm[:, 1, ::-1], data1=td['lw'][:, ::-1], initial=0.0, op0=ALU.mult, op1=ALU.add),
                                 reads=['lw', 'segm'], writes=['cin'])
                            yield
                            totap = v3(td['cin'])[:, :, 0:1]
                        s.op('pool', lambda e: e.tensor_tensor(out=td['cex'][:], in0=td['cin'][:], in1=td['lw'][:], op=ALU.subtract), reads=['cin', 'lw'], writes=['cex'])
                        yield
                        s.op('act', lambda e: e.activation(out=td['e1'][:], in_=td['cex'][:], func=AF.Exp), reads=['cex'], writes=['e1'])
                        yield
                        s.op('act', lambda e: e.activation(out=td['e2'][:], in_=td['cin'][:], func=AF.Exp, scale=-1.0), reads=['cin'], writes=['e2'])
                        yield
                        s.op('act', lambda e: e.activation(out=td['e3'][:], in_=td['cin'][:], func=AF.Exp), reads=['cin'], writes=['e3'])
                        yield
                        s.op('dve', lambda e, totap=totap: e.tensor_tensor(out=v3(td['e4']), in0=totap.to_broadcast([128, 8, 64]), in1=v3(td['cin']), op=ALU.subtract), reads=['cin'], writes=['e4'])
                        yield
                        s.op('act', lambda e: e.activation(out=td['e4'][:], in_=td['e4'][:], func=AF.Exp), reads=['e4'], writes=['e4'])
                        yield
                        s.op('act', lambda e, totap=totap, d=d, sl=sl: e.activation(out=wc[sl][:, d, :].rearrange("p (c o) -> p c o", o=1), in_=totap, func=AF.Exp), reads=['cin'], writes=[('wc', sl, d)])
                        yield
                        s.op('dve', lambda e, F=F, d=d, sl=sl: e.tensor_tensor(out=LR[sl][:, d, :, 0:64], in0=v3(F['kkn']), in1=v3(td['e1']), op=ALU.mult), reads=[fk('kkn'), 'e1'], writes=[('LR', sl, d, 0)])
                        yield
                        s.op('pool', lambda e, d=d, sl=sl, r_=r_: e.tensor_tensor(out=LR[sl][:, d, :, 64:128], in0=r_.rearrange("p (c k) -> p c k", k=64), in1=v3(td['e3']), op=ALU.mult),
                             reads=ik + ['e3'], writes=[('LR', sl, d, 1)])
                        yield
                        s.op('dve', lambda e, F=F, d=d, sl=sl: e.tensor_tensor(out=BK[sl][:, d, :, 0:64], in0=v3(F['bb']), in1=v3(td['e2']), op=ALU.mult), reads=[fk('bb'), 'e2'], writes=[('BK', sl, d, 0)])
                        yield
                        s.op('pool', lambda e, F=F, d=d, sl=sl: e.tensor_tensor(out=BK[sl][:, d, :, 64:128], in0=v3(F['kmod']), in1=v3(td['e2']), op=ALU.mult), reads=[fk('kmod'), 'e2'], writes=[('BK', sl, d, 1)])
                        yield
                        s.op('dve', lambda e, F=F, d=d, sl=sl: e.tensor_tensor(out=Bh[sl][:, d, :], in0=F['bb'][:], in1=td['e4'][:], op=ALU.mult), reads=[fk('bb'), 'e4'], writes=[('Bh', sl, d)])
                        yield
                        s.op('pool', lambda e, F=F, d=d, sl=sl: e.tensor_tensor(out=Kh[sl][:, d, :], in0=F['kmod'][:], in1=td['e4'][:], op=ALU.mult), reads=[fk('kmod'), 'e4'], writes=[('Kh', sl, d)])
                        yield
            def P2M(tb, n, sl, pump):
                if True:
                    X = inp_[sl]
                    cols_ = slice(tb * 512, (tb + 1) * 512)
                    ik = [('rin', sl, q) for q in range(NIN)]
                    r_, k_, v_, a_ = X[:, I_R, :], X[:, I_K, :], X[:, I_V, :], X[:, I_A, :]
                    F = {nm: tf[nm][sl] for nm in tf}
                    fk = lambda nm: (nm, sl)
                    lrk = [('LR', sl, d, q) for d in range(2) for q in range(2)]
                    bkk = [('BK', sl, d, q) for d in range(2) for q in range(2)]

                    def tr128(src_fn, skeys, dst, dkey):
                        hb_ = pbc[0] % 2
                        pbc[0] += 1

                        def g(e, hb_=hb_):
                            for cp in range(4):
                                r = e.transpose(psbs[hb_][:, cp * 128:(cp + 1) * 128], src_fn(cp), identb[:])
                            return r
                        s.op('pe', g, reads=skeys + ['identb'], writes=[('psb', hb_)])
                        en = evac_eng()
                        s.op(en, copy_op(en, dst, psbs[hb_][:, 0:512].rearrange("p (c k) -> p c k", k=128)), reads=[('psb', hb_)], writes=[dkey])
                    tr128(lambda cp: vb[sl][:, cp * 128:(cp + 1) * 128], [('vb', sl)], Vt[:], 'Vt')
                    for d in range(2):
                        tr128(lambda cp, d=d: Bh[sl][:, d, cp * 128:(cp + 1) * 128], [('Bh', sl, d)], Bht[:, d, :, :], ('Bht', d))
                        tr128(lambda cp, d=d: Kh[sl][:, d, cp * 128:(cp + 1) * 128], [('Kh', sl, d)], Kht[:, d, :, :], ('Kht', d))
                        hb_ = pbc[0] % 2
                        pbc[0] += 1

                        def g(e, hb_=hb_, d=d):
                            for c in range(8):
                                par, cp = c % 2, c // 2
                                r = e.transpose(psbs[hb_][64 * par:64 * par + 64, cp * 128:(cp + 1) * 128], LR[sl][:, d, c, 0:64], identb[:])
                            return r
                        s.op('pe', g, reads=lrk + ['identb'], writes=[('psb', hb_)])
                        s.op('dve', lambda e, hb_=hb_, d=d: e.tensor_copy(out=Zt[:, d, :, :, 0:64], in_=psbs[hb_][:, 0:512].rearrange("p (c h k) -> p h c k", c=4, h=2)),
                             reads=[('psb', hb_)], writes=[('Zt', d, 0, 0), ('Zt', d, 1, 0)])
                    for d in range(2):
                        for hh in range(2):
                            cb = d * 2 + hh
                            hp = 64 * hh
                            bA, bB, bC = nbank(), nbank(), nbank()

                            def g(e, d=d, hp=hp, bA=bA, bB=bB, bC=bC):
                                for c in range(8):
                                    par, cp = c % 2, c // 2
                                    po = 64 * par
                                    e.matmul(ps[bA][po:po + 64, cp * 64:(cp + 1) * 64], lhsT=LR[sl][hp:hp + 64, d, c, 0:64], rhs=BK[sl][hp:hp + 64, d, c, 0:64], start=True, stop=True)
                                    e.matmul(ps[bB][po:po + 64, cp * 128:(cp + 1) * 128], lhsT=BK[sl][hp:hp + 64, d, c, 0:64], rhs=LR[sl][hp:hp + 64, d, c, :], start=True, stop=True)
                                    r = e.matmul(ps[bC][po:po + 64, cp * 128:(cp + 1) * 128], lhsT=BK[sl][hp:hp + 64, d, c, 64:128], rhs=LR[sl][hp:hp + 64, d, c, :], start=True, stop=True)
                                return r
                            s.op('pe', g, reads=lrk + bkk, writes=[('ps', bA), ('ps', bB), ('ps', bC)])
                            s.op('dve', lambda e, bA=bA, cb=cb, d=d: e.tensor_tensor(out=XX[:, cb, 0, :, :], in0=ps[bA][:, 0:256].rearrange("p (c k) -> p c k", k=64), in1=mA[:, d, :, :], op=ALU.mult),
                                 reads=[('ps', bA), 'mA'], writes=[('XX', cb)])
                            s.op('dve', lambda e, bB=bB, cb=cb, d=d: e.tensor_tensor(out=MBt[:, cb, :, :], in0=ps[bB][:, :].rearrange("p (c k) -> p c k", k=128), in1=mB[:, d, :, :], op=ALU.mult),
                                 reads=[('ps', bB), 'mB'], writes=[('MBt', cb)])
                            s.op('dve', lambda e, bC=bC, cb=cb, d=d: e.tensor_tensor(out=MCt[:, cb, :, :], in0=ps[bC][:, :].rearrange("p (c k) -> p c k", k=128), in1=mC[:, d, :, :], op=ALU.mult),
                                 reads=[('ps', bC), 'mC'], writes=[('MCt', cb)])
                            bD = nbank()

                            def g(e, cb=cb, hh=hh, bD=bD):
                                for c in range(8):
                                    par, cp = c % 2, c // 2
                                    po = 64 * par
                                    r = e.matmul(ps[bD][po:po + 64, cp * 64:(cp + 1) * 64], lhsT=MCt[po:po + 64, cb, cp, 0:64], rhs=Vt[po:po + 64, cp, hh * 64:(hh + 1) * 64], start=True, stop=True)
                                return r
                            s.op('pe', g, reads=[('MCt', cb), 'Vt'], writes=[('ps', bD)])
                            s.op('act', lambda e, bD=bD, d=d, hh=hh: e.activation(out=Zt[:, d, hh, :, 64:128], in_=ps[bD][:, 0:256].rearrange("p (c k) -> p c k", k=64), func=AF.Copy, scale=-1.0),
                                 reads=[('ps', bD)], writes=[('Zt', d, hh, 1)])
                            pump()
                    for lv in range(6):
                        for d in range(2):
                            for hh in range(2):
                                cb = d * 2 + hh
                                zk = [('Zt', d, hh, 0), ('Zt', d, hh, 1)]
                                bX, bZ = nbank(), nbank()

                                def g(e, cb=cb, d=d, hh=hh, bX=bX, bZ=bZ, lv=lv):
                                    for c in range(8):
                                        par, cp = c % 2, c // 2
                                        po = 64 * par
                                        Xc = XX[po:po + 64, cb, 0, cp, :]
                                        XTc = MBt[po:po + 64, cb, cp, 0:64] if lv == 0 else XX[po:po + 64, cb, 1, cp, :]
                                        if lv < 5:
                                            e.matmul(ps[bX][po:po + 64, cp * 64:(cp + 1) * 64], lhsT=XTc, rhs=Xc, start=True, stop=True)
                                            e.matmul(ps[bX][po:po + 64, 256 + cp * 64:256 + (cp + 1) * 64], lhsT=Xc, rhs=XTc, start=True, stop=True)
                                        r = e.matmul(ps[bZ][po:po + 64, cp * 128:(cp + 1) * 128], lhsT=XTc, rhs=Zt[po:po + 64, d, hh, cp, :], start=True, stop=True)
                                    return r
                                s.op('pe', g, reads=[('XX', cb), ('MBt', cb)] + zk, writes=[('ps', bX), ('ps', bZ)])
                                if lv < 5:
                                    s.op('act', lambda e, bX=bX, cb=cb: e.activation(out=XX[:, cb, :, :, :], in_=ps[bX][:, :].rearrange("p (x c k) -> p x c k", x=2, c=4), func=AF.Copy),
                                         reads=[('ps', bX)], writes=[('XX', cb)])
                                s.op('dve', lambda e, bZ=bZ, d=d, hh=hh: e.tensor_tensor(out=Zt[:, d, hh, :, :], in0=ps[bZ][:, :].rearrange("p (c k) -> p c k", k=128), in1=Zt[:, d, hh, :, :], op=ALU.add),
                                     reads=[('ps', bZ)] + zk, writes=zk)
                                pump()
                    allz = [('Zt', d, hh, q) for d in range(2) for hh in range(2) for q in range(2)]
                    allm = [('MBt', cb) for cb in range(4)] + [('MCt', cb) for cb in range(4)]
                    def pv(bank):
                        return ps[bank][:, 0:256].rearrange("p (c k) -> p c k", k=64)
                    for d in range(2):
                        bQ, bG, bH = (nbank(), nbank()), (nbank(), nbank()), (nbank(), nbank())

                        def g(e, d=d, bQ=bQ, bG=bG, bH=bH):
                            for par in range(2):
                                po = 64 * par
                                for hh in range(2):
                                    cb = d * 2 + hh
                                    hp = 64 * hh
                                    for cp in range(4):
                                        KKh = Zt[po:po + 64, d, hh, cp, 0:64]
                                        NU = Zt[po:po + 64, d, hh, cp, 64:128]
                                        e.matmul(ps[bQ[par]][hp:hp + 64, cp * 64:(cp + 1) * 64], lhsT=KKh, rhs=MBt[po:po + 64, cb, cp, 64:128], start=True, stop=True)
                                        e.matmul(ps[bG[par]][hp:hp + 64, cp * 64:(cp + 1) * 64], lhsT=KKh, rhs=Bht[po:po + 64, d, cp, hh * 64:(hh + 1) * 64], start=True, stop=True)
                                        e.matmul(ps[bH[par]][hp:hp + 64, cp * 64:(cp + 1) * 64], lhsT=Kht[po:po + 64, d, cp, hh * 64:(hh + 1) * 64], rhs=Vt[po:po + 64, cp, hh * 64:(hh + 1) * 64], start=True, stop=False)
                                        r = e.matmul(ps[bH[par]][hp:hp + 64, cp * 64:(cp + 1) * 64], lhsT=Bht[po:po + 64, d, cp, hh * 64:(hh + 1) * 64], rhs=NU, start=False, stop=True)
                            return r
                        s.op('pe', g, reads=allz + allm + ['Vt', ('Bht', d), ('Kht', d)], writes=[('ps', x) for x in bQ + bG + bH])
                        s.op('dve', lambda e, d=d: e.tensor_tensor(out=dwc[:], in0=idr[:], in1=wc[sl][:, d, :].rearrange("p (c o) -> p c o", o=1).to_broadcast([128, 8, 64]), op=ALU.mult),
                             reads=['idr', ('wc', sl, d)], writes=['dwc'])
                        for par in range(2):
                            s.op('dve', lambda e, bQ=bQ, d=d, par=par: e.scalar_tensor_tensor(out=qst[sl][:, d, par::2, :], in0=pv(bQ[par]), scalar=-1.0, in1=LR[sl][:, d, par::2, 64:128], op0=ALU.mult, op1=ALU.add),
                                 reads=[('ps', bQ[par])] + lrk, writes=[('qst', sl, d, par)])
                            s.op('dve', lambda e, bG=bG, d=d, par=par: e.scalar_tensor_tensor(out=gst_[sl][:, d, par::2, :], in0=pv(bG[par]), scalar=-1.0, in1=dwc[:, par::2, :], op0=ALU.mult, op1=ALU.add),
                                 reads=[('ps', bG[par]), 'dwc'], writes=[('gstq', sl, d, par)])
                            s.op('act', lambda e, bH=bH, d=d, par=par: e.activation(out=hst[sl][:, d, par::2, :], in_=pv(bH[par]), func=AF.Copy),
                                 reads=[('ps', bH[par])], writes=[('hst', sl, d, par)])
                        s.dma('act', QTv[d][:, n, cols_], qst[sl][:, d, :, :].rearrange("p c k -> p (c k)"), ds_out[sl], reads=[('qst', sl, d, 0), ('qst', sl, d, 1)], writes=[('QT', d, tb, n)])
                        s.dma('act', GSv[d][:, n, cols_], gst_[sl][:, d, :, :].rearrange("p c k -> p (c k)"), ds_out[sl], reads=[('gstq', sl, d, 0), ('gstq', sl, d, 1)], writes=[('GS', d, tb, n)])
                        s.dma('act', HSv[d][:, n, cols_], hst[sl][:, d, :, :].rearrange("p c k -> p (c k)"), ds_out[sl], reads=[('hst', sl, d, 0), ('hst', sl, d, 1)], writes=[('HS', d, tb, n)])
                    bY = (nbank(), nbank())

                    def g(e, bY=bY):
                        for par in range(2):
                            po = 64 * par
                            for hh in range(2):
                                hp = 64 * hh
                                for cp in range(4):
                                    for d in range(2):
                                        cb = d * 2 + hh
                                        e.matmul(ps[bY[par]][hp:hp + 64, cp * 64:(cp + 1) * 64], lhsT=Vt[po:po + 64, cp, hh * 64:(hh + 1) * 64], rhs=MCt[po:po + 64, cb, cp, 64:128], start=(d == 0), stop=False)
                                        r = e.matmul(ps[bY[par]][hp:hp + 64, cp * 64:(cp + 1) * 64], lhsT=Zt[po:po + 64, d, hh, cp, 64:128], rhs=MBt[po:po + 64, cb, cp, 64:128], start=False, stop=(d == 1))
                        return r
                    s.op('pe', g, reads=allz + allm + ['Vt'], writes=[('ps', bY[0]), ('ps', bY[1])])
                    for par in range(2):
                        s.op('act', lambda e, bY=bY, par=par: e.activation(out=yst[sl][:, par::2, :], in_=pv(bY[par]), func=AF.Copy), reads=[('ps', bY[par])], writes=[('yst', sl, par)])
                    s.dma('act', YLv[:, n, cols_], yst[sl][:].rearrange("p c k -> p (c k)"), ds_out[sl], reads=[('yst', sl, 0), ('yst', sl, 1)], writes=[('YL', tb, n)])
            items = [(tb_, n_) for tb_ in range(NTB) for n_ in range(8)]
            SENT = object()

            def drain(g_):
                for _ in g_:
                    pass
            drain(P1(items[0][0], items[0][1], 0))
            for i_, (tb_, n_) in enumerate(items):
                sl_ = i_ % 2
                nxt = P1(items[i_ + 1][0], items[i_ + 1][1], 1 - sl_) if i_ + 1 < len(items) else iter(())

                def pump(k_=3, nxt=nxt):
                    for _ in range(k_):
                        if next(nxt, SENT) is SENT:
                            break
                P2M(tb_, n_, sl_, pump)
                drain(nxt)
            lp.close()

        def rw_scan(l, j):
            lp = Pool_()
            NS = 2
            qt = [lp.t(uname('qt'), [128, 8, 512], F32) for _ in range(NS)]
            gs = [lp.t(uname('gs'), [128, 8, 512], F32) for _ in range(NS)]
            hs = [lp.t(uname('hs'), [128, 8, 512], F32) for _ in range(NS)]
            ds_q = [s.dsem(uname('qt')) for _ in range(NS)]
            ys = [lp.t(uname('ys'), [128, 8, 512], F32) for _ in range(2)]
            ds_ys = [s.dsem(uname('ys')) for _ in range(2)]
            ST = {(b, d): [lp.t(uname('ST'), [128, 8, 64], F32) for _ in range(2)] for b in range(NB) for d in range(2)}
            stc = {}
            for b in range(NB):
                for d in range(2):
                    s.op('pool', lambda e, b=b, d=d: e.memset(ST[(b, d)][0][:], 0.0), writes=[('ST', b, d, 0)])
                    stc[(b, d)] = 0
            lc = 0
            yc = 0
            for it in range(TBS):
                for b in range(NB):
                    for d in range(2):
                        blk = it if d == 0 else TBS - 1 - it
                        tb = b * TBS + blk
                        cols_ = slice(tb * 512, (tb + 1) * 512)
                        sl = lc % NS
                        lc += 1
                        s.dma('sp', qt[sl][:], QTv[d][:, :, cols_], ds_q[sl], reads=[('QT', d, tb, n) for n in range(8)], writes=[('qt', sl)])
                        s.dma('sp', gs[sl][:], GSv[d][:, :, cols_], ds_q[sl], reads=[('GS', d, tb, n) for n in range(8)], writes=[('gs', sl)])
                        s.dma('sp', hs[sl][:], HSv[d][:, :, cols_], ds_q[sl], reads=[('HS', d, tb, n) for n in range(8)], writes=[('hs', sl)])
                        yo = yc % 2
                        yc += 1
                        order = range(8) if d == 0 else range(7, -1, -1)
                        for c in order:
                            cur = stc[(b, d)]
                            nxt = 1 - cur
                            Sc, Sn = ST[(b, d)][cur], ST[(b, d)][nxt]
                            bY, bS = nbank(), nbank()

                            def g(e, bY=bY, Sc=Sc, c=c, sl=sl):
                                for n in range(8):
                                    for hh in range(2):
                                        hp = 64 * hh
                                        r = e.matmul(ps[bY][hp:hp + 64, n * 64:(n + 1) * 64], lhsT=Sc[hp:hp + 64, n, :], rhs=qt[sl][hp:hp + 64, n, c * 64:(c + 1) * 64], start=True, stop=True)
                                return r
                            s.op('pe', g, reads=[('ST', b, d, cur), ('qt', sl)], writes=[('ps', bY)])
                            s.op('act', lambda e, bY=bY, yo=yo, c=c: e.activation(out=ys[yo][:, :, c * 64:(c + 1) * 64], in_=ps[bY][:, :].rearrange("p (n k) -> p n k", k=64), func=AF.Copy),
                                 reads=[('ps', bY)], writes=[('ys', yo, c)])

                            def g2(e, bS=bS, Sc=Sc, c=c, sl=sl):
                                for n in range(8):
                                    for hh in range(2):
                                        hp = 64 * hh
                                        r = e.matmul(ps[bS][hp:hp + 64, n * 64:(n + 1) * 64], lhsT=gs[sl][hp:hp + 64, n, c * 64:(c + 1) * 64], rhs=Sc[hp:hp + 64, n, :], start=True, stop=True)
                                return r
                            s.op('pe', g2, reads=[('ST', b, d, cur), ('gs', sl)], writes=[('ps', bS)])
                            s.op('dve', lambda e, bS=bS, Sn=Sn, c=c, sl=sl: e.tensor_tensor(out=Sn[:], in0=ps[bS][:, :].rearrange("p (n k) -> p n k", k=64), in1=hs[sl][:, :, c * 64:(c + 1) * 64], op=ALU.add),
                                 reads=[('ps', bS), ('hs', sl)], writes=[('ST', b, d, nxt)])
                            stc[(b, d)] = nxt
                        s.dma('pool', YSv[d][:, :, cols_], ys[yo][:], ds_ys[yo], reads=[('ys', yo, c) for c in range(8)], writes=[('YS', d, tb)])
            lp.close()

        def rw_post(l, j):
            lp = Pool_()
            bo64 = lp.t(uname('bo64'), [128, 128], F32)
            ds = s.dsem(uname('rwp'))
            s.dma('sp', bo64[:], CD['bo64'][:, :], ds, writes=['bo64'])
            yin = [lp.t(uname('yin'), [128, 3, 512], F32) for _ in range(4)]
            bg = [lp.t(uname('bg'), [128, 2, 512], BF16) for _ in range(4)]
            ds_in = [s.dsem(uname('yin')) for _ in range(4)]
            tt_ = {nm: [lp.t(uname(nm), [128, 512], F32) for _ in range(4)] for nm in ('y', 'sq', 'mean', 'var', 'dd')}
            ygo = [lp.t(uname('ygo'), [128, 512], BF16) for _ in range(4)]
            ds_o = [s.dsem(uname('ygo')) for _ in range(4)]
            def body(tb, n, sl):
                cols_ = slice(tb * 512, (tb + 1) * 512)
                if True:
                    Y = yin[sl]
                    s.dma('sp', Y[:, 0, :], YLv[:, n, cols_], ds_in[sl], reads=[('YL', tb, n)], writes=[('yin', sl, 0)])
                    yield
                    s.dma('sp', Y[:, 1, :], YSv[0][:, n, cols_], ds_in[sl], reads=[('YS', 0, tb)], writes=[('yin', sl, 1)])
                    yield
                    s.dma('sp', Y[:, 2, :], YSv[1][:, n, cols_], ds_in[sl], reads=[('YS', 1, tb)], writes=[('yin', sl, 2)])
                    yield
                    s.dma('sp', bg[sl][:, 0, :], BONv[:, n, cols_], ds_in[sl], reads=[('BON', tb, n)], writes=[('bg', sl, 0)])
                    yield
                    s.dma('sp', bg[sl][:, 1, :], G2v[:, n, cols_], ds_in[sl], reads=[('G2', tb, n)], writes=[('bg', sl, 1)])
                    yield
                    yk = [('yin', sl, q) for q in range(3)]
                    Tt = {nm: tt_[nm][sl] for nm in tt_}
                    tk = lambda nm: (nm, sl)
                    s.op('dve', lambda e, Y=Y, Tt=Tt: e.tensor_tensor(out=Tt['y'][:], in0=Y[:, 0, :], in1=Y[:, 1, :], op=ALU.add), reads=yk, writes=[tk('y')])
                    yield
                    s.op('dve', lambda e, Y=Y, Tt=Tt: e.tensor_tensor(out=Tt['y'][:], in0=Tt['y'][:], in1=Y[:, 2, :], op=ALU.add), reads=yk + [tk('y')], writes=[tk('y')])
                    yield
                    s.op('act', lambda e, Tt=Tt: e.activation(out=Tt['sq'][:], in_=Tt['y'][:], func=AF.Square), reads=[tk('y')], writes=[tk('sq')])
                    yield
                    bM, bE = 2 * sl, 2 * sl + 1
                    s.op('pe', lambda e, bM=bM, Tt=Tt: e.matmul(ps[bM][:, :], lhsT=bo64[:], rhs=Tt['y'][:], start=True, stop=True), reads=[tk('y'), 'bo64'], writes=[('ps', bM)])
                    yield
                    s.op('pe', lambda e, bE=bE, Tt=Tt: e.matmul(ps[bE][:, :], lhsT=bo64[:], rhs=Tt['sq'][:], start=True, stop=True), reads=[tk('sq'), 'bo64'], writes=[('ps', bE)])
                    yield
                    s.op('act', lambda e, bM=bM, Tt=Tt: e.activation(out=Tt['mean'][:], in_=ps[bM][:, :], func=AF.Copy), reads=[('ps', bM)], writes=[tk('mean')])
                    yield
                    s.op('dve', lambda e, Tt=Tt: e.tensor_tensor(out=Tt['var'][:], in0=Tt['mean'][:], in1=Tt['mean'][:], op=ALU.mult), reads=[tk('mean')], writes=[tk('var')])
                    yield
                    s.op('dve', lambda e, bE=bE, Tt=Tt: e.tensor_tensor(out=Tt['var'][:], in0=ps[bE][:, :], in1=Tt['var'][:], op=ALU.subtract), reads=[('ps', bE), tk('var')], writes=[tk('var')])
                    yield
                    s.op('dve', lambda e, Tt=Tt: e.tensor_scalar(out=Tt['var'][:], in0=Tt['var'][:], scalar1=0.0, scalar2=64e-5, op0=ALU.max, op1=ALU.add), reads=[tk('var')], writes=[tk('var')])
                    yield
                    s.op('act', lambda e, Tt=Tt: e.activation(out=Tt['var'][:], in_=Tt['var'][:], func=AF.Sqrt), reads=[tk('var')], writes=[tk('var')])
                    yield
                    s.op('dve', lambda e, Tt=Tt: e.reciprocal(out=Tt['var'][:], in_=Tt['var'][:]), reads=[tk('var')], writes=[tk('var')])
                    yield
                    s.op('dve', lambda e, Tt=Tt: e.tensor_tensor(out=Tt['dd'][:], in0=Tt['y'][:], in1=Tt['mean'][:], op=ALU.subtract), reads=[tk('y'), tk('mean')], writes=[tk('dd')])
                    yield
                    s.op('dve', lambda e, Tt=Tt: e.tensor_tensor(out=Tt['dd'][:], in0=Tt['dd'][:], in1=Tt['var'][:], op=ALU.mult), reads=[tk('dd'), tk('var')], writes=[tk('dd')])
                    yield
                    s.op('act', lambda e, Tt=Tt: e.activation(out=Tt['dd'][:], in_=Tt['dd'][:], func=AF.Identity, bias=col('rw_ln_b', j, n), scale=col('rw_ln_w', j, n)),
                         reads=[tk('dd'), 'colsT'], writes=[tk('dd')])
                    yield
                    s.op('pool', lambda e, Tt=Tt, sl=sl: e.tensor_tensor(out=Tt['dd'][:], in0=Tt['dd'][:], in1=bg[sl][:, 0, :], op=ALU.add), reads=[tk('dd'), ('bg', sl, 0)], writes=[tk('dd')])
                    yield
                    s.op('pool', lambda e, Tt=Tt, sl=sl: e.tensor_tensor(out=ygo[sl][:], in0=Tt['dd'][:], in1=bg[sl][:, 1, :], op=ALU.mult), reads=[tk('dd'), ('bg', sl, 1)], writes=[('ygo', sl)])
                    yield
                    s.dma('act', YGv[:, n, cols_], ygo[sl][:], ds_o[sl], reads=[('ygo', sl)], writes=[('YGr', tb, n)])
                    yield
            KI = 3
            items = [(tb_, n_) for tb_ in range(NTB) for n_ in range(8)]
            for g0_ in range(0, len(items), KI):
                gens = [body(items[g0_ + q_][0], items[g0_ + q_][1], q_) for q_ in range(min(KI, len(items) - g0_))]
                alive = list(gens)
                while alive:
                    nxt_alive = []
                    for g_ in alive:
                        try:
                            next(g_)
                            nxt_alive.append(g_)
                        except StopIteration:
                            pass
                    alive = nxt_alive
            lp.close()

        def stage_rwkv(l, j):
            rw_proj(l, j)
            if cfg.stop_after == 'proj':
                return
            rw_chunk(l, j)
            if cfg.stop_after == 'chunk':
                return
            rw_scan(l, j)
            if cfg.stop_after == 'scan':
                return
            rw_post(l, j)
            if cfg.stop_after == 'post':
                return
            mix_outproj(l, 'rw_w_o', j, None, None, lambda tb: [('YGr', tb, n) for n in range(8)])

        try:
            stage_in()
            for l, kind in enumerate(cfg.layers):
                if kind == 'H':
                    stage_hyena(l, l // 2)
                elif kind == 'R':
                    stage_rwkv(l, l // 2)
                if cfg.ffn:
                    stage_ffn(l)
            stage_out()
            s.finish()
        except Exception:
            import traceback
            print(traceback.format_exc()[-1500:], flush=True)
            raise
    return nc


_CACHE = {}


def make_in_maps(cfg, inputs, ncores):
    pv, _, _ = pack_vecs(inputs)
    consts = host_consts(cfg.L)
    x = np.asarray(inputs['x'], np.float32)
    maps = []
    for c in range(ncores):
        m = {'x': np.ascontiguousarray(x[c * cfg.NB:(c + 1) * cfg.NB]), 'pvec': pv}
        for n in BIG_W:
            m[n] = np.ascontiguousarray(np.asarray(inputs[n], np.float32))
        for n, v in consts.items():
            m['c_' + n] = v
        maps.append(m)
    return maps


def kernel(**inputs):
    cfg = Cfg()
    shapes = {k: tuple(np.asarray(v).shape) for k, v in inputs.items()}
    nc = build(cfg, shapes)
    maps = make_in_maps(cfg, inputs, 8)
    res = run_bass_kernel_spmd(nc, maps, core_ids=list(range(8)))
    return np.concatenate([np.asarray(r['out'], np.float32) for r in res.results], axis=0)
```
